# Optimizing a Trainium2 kernel written in Bass

```python
import jax, jax.numpy as jnp
from jax import lax
import numpy as np

D_MODEL = 1024
BATCH = 8
SEQ = 4096
DEPTH = 4
DEC_BATCH = 8
DEC_SEQ = 64
PAST_LEN = 2048

CHUNK = 64
D_MIX = D_MODEL
D_CONV = D_MIX // 2
D_POOL = D_MIX - D_CONV
CONV_WIDTH = 31
POOL_WINDOWS = (2, 4, 8, 16)
N_POOL_GROUPS = len(POOL_WINDOWS)
POOL_GROUP = D_POOL // N_POOL_GROUPS
MAX_POOL = max(POOL_WINDOWS)
D_FF = 4 * D_MODEL
PLE_DIM = 256
EPS = 1e-6

kernel_name = "hybrid_conv_pool_stream_step"


def _rmsnorm(x, g):
    xf = x.astype(jnp.float32)
    r = xf * lax.rsqrt(jnp.mean(xf * xf, axis=-1, keepdims=True) + EPS)
    return (r * g.astype(jnp.float32)).astype(x.dtype)


def _layernorm(x, g, b):
    xf = x.astype(jnp.float32)
    mu = jnp.mean(xf, axis=-1, keepdims=True)
    var = jnp.mean(jnp.square(xf - mu), axis=-1, keepdims=True)
    y = (xf - mu) * lax.rsqrt(var + EPS) * g.astype(jnp.float32) + b.astype(jnp.float32)
    return y.astype(x.dtype)


def _layer(x, p, conv_buf, pool_buf, pos0, w_in, conv_w, conv_b, ln_g, ln_b, pool_w, pool_scale,
           w_out, g_mix, g_ffn, g_ple, w_ff1, w_ff2, w_ple, w_gate):
    B, L, _ = x.shape
    h = _rmsnorm(x, g_mix)
    z = h @ w_in
    a = z[..., :D_CONV]
    gt = z[..., D_CONV:2 * D_CONV]
    u = z[..., 2 * D_CONV:]

    c = a * jax.nn.sigmoid(gt)
    c_pad = jnp.concatenate([conv_buf.astype(c.dtype), c], axis=1)
    cv = lax.conv_general_dilated(
        c_pad, conv_w[:, None, :].astype(c.dtype), (1,), 'VALID',
        dimension_numbers=('NWC', 'WIO', 'NWC'), feature_group_count=D_CONV)
    cv = cv + conv_b
    cv = _layernorm(cv, ln_g, ln_b)
    cv = cv * jax.nn.sigmoid(cv)
    new_conv = c_pad[:, -(CONV_WIDTH - 1):]

    u_pad = jnp.concatenate([pool_buf.astype(u.dtype), u], axis=1)
    new_pool = u_pad[:, -(MAX_POOL - 1):]
    uf = u_pad.astype(jnp.float32)
    csum = jnp.concatenate([jnp.zeros((B, 1, D_POOL), jnp.float32), jnp.cumsum(uf, axis=1)], axis=1)
    csum = csum.reshape(B, L + MAX_POOL, N_POOL_GROUPS, POOL_GROUP)
    pos = pos0 + jnp.arange(L)
    means = []
    for g, w in enumerate(POOL_WINDOWS):
        s = csum[:, MAX_POOL:, g] - csum[:, MAX_POOL - w:MAX_POOL - w + L, g]
        cnt = jnp.minimum(w, pos + 1).astype(jnp.float32)
        means.append(s / cnt[None, :, None])
    pooled = jnp.stack(means, axis=2)
    d = pooled - uf[:, MAX_POOL - 1:].reshape(B, L, N_POOL_GROUPS, POOL_GROUP)
    pm = jnp.einsum('blgc,gcd->blgd', d.astype(x.dtype), pool_w).reshape(B, L, D_POOL) * pool_scale

    x = x + jnp.concatenate([cv.astype(x.dtype), pm.astype(x.dtype)], axis=-1) @ w_out

    h2 = _rmsnorm(x, g_ffn)
    x = x + jnp.square(jax.nn.relu(h2 @ w_ff1)) @ w_ff2

    gate = jax.nn.sigmoid(_rmsnorm(x, g_ple) @ w_gate)
    x = x + gate * (p @ w_ple)
    return x, new_conv, new_pool


def _trunk(x, p, conv_bufs, pool_bufs, pos0, w_in, conv_w, conv_b, ln_g, ln_b, pool_w, pool_scale,
           w_out, g_mix, g_ffn, g_ple, w_ff1, w_ff2, w_ple, w_gate, g_final):
    new_convs, new_pools = [], []
    for i in range(DEPTH):
        x, nc, npl = _layer(x, p[i], conv_bufs[i], pool_bufs[i], pos0, w_in[i], conv_w[i], conv_b[i],
                            ln_g[i], ln_b[i], pool_w[i], pool_scale[i], w_out[i], g_mix[i], g_ffn[i],
                            g_ple[i], w_ff1[i], w_ff2[i], w_ple[i], w_gate[i])
        new_convs.append(nc)
        new_pools.append(npl)
    return _rmsnorm(x, g_final), jnp.stack(new_convs, 0), jnp.stack(new_pools, 0)


def setup_inputs(seed: int = 0) -> dict:
    key = jax.random.key(seed)
    ks = jax.random.split(key, 24)
    f32 = jnp.float32
    nrm = lambda k, s, sc: (jax.random.normal(k, s, f32) * sc).astype(f32)
    return {
        "x_prompt": nrm(ks[0], (BATCH, SEQ, D_MODEL), 1.0),
        "x_sample": nrm(ks[1], (DEC_BATCH, DEC_SEQ, D_MODEL), 1.0),
        "p_prompt": nrm(ks[2], (DEPTH, BATCH, SEQ, PLE_DIM), 1.0),
        "p_sample": nrm(ks[3], (DEPTH, DEC_BATCH, DEC_SEQ, PLE_DIM), 1.0),
        "cache_conv": nrm(ks[4], (DEPTH, DEC_BATCH, CONV_WIDTH - 1, D_CONV), 1.0),
        "cache_pool": nrm(ks[5], (DEPTH, DEC_BATCH, MAX_POOL - 1, D_POOL), 1.0),
        "w_in": nrm(ks[6], (DEPTH, D_MODEL, 2 * D_CONV + D_POOL), D_MODEL ** -0.5),
        "conv_w": nrm(ks[7], (DEPTH, CONV_WIDTH, D_CONV), CONV_WIDTH ** -0.5),
        "conv_b": nrm(ks[8], (DEPTH, D_CONV), 0.02),
        "ln_g": 1.0 + nrm(ks[9], (DEPTH, D_CONV), 0.05),
        "ln_b": nrm(ks[10], (DEPTH, D_CONV), 0.02),
        "pool_w": nrm(ks[11], (DEPTH, N_POOL_GROUPS, POOL_GROUP, POOL_GROUP), POOL_GROUP ** -0.5),
        "pool_scale": 1.0 + nrm(ks[12], (DEPTH, D_POOL), 0.05),
        "w_out": nrm(ks[13], (DEPTH, D_MIX, D_MODEL), D_MIX ** -0.5),
        "g_mix": 1.0 + nrm(ks[14], (DEPTH, D_MODEL), 0.05),
        "g_ffn": 1.0 + nrm(ks[15], (DEPTH, D_MODEL), 0.05),
        "g_ple": 1.0 + nrm(ks[16], (DEPTH, D_MODEL), 0.05),
        "w_ff1": nrm(ks[17], (DEPTH, D_MODEL, D_FF), D_MODEL ** -0.5),
        "w_ff2": nrm(ks[18], (DEPTH, D_FF, D_MODEL), D_FF ** -0.5),
        "w_ple": nrm(ks[19], (DEPTH, PLE_DIM, D_MODEL), PLE_DIM ** -0.5),
        "w_gate": nrm(ks[20], (DEPTH, D_MODEL, D_MODEL), D_MODEL ** -0.5),
        "g_final": 1.0 + nrm(ks[21], (D_MODEL,), 0.05),
    }


def reference(x_prompt, x_sample, p_prompt, p_sample, cache_conv, cache_pool, w_in, conv_w, conv_b,
              ln_g, ln_b, pool_w, pool_scale, w_out, g_mix, g_ffn, g_ple, w_ff1, w_ff2, w_ple, w_gate,
              g_final):
    B = x_prompt.shape[0]
    zero_conv = jnp.zeros((DEPTH, B, CONV_WIDTH - 1, D_CONV), x_prompt.dtype)
    zero_pool = jnp.zeros((DEPTH, B, MAX_POOL - 1, D_POOL), x_prompt.dtype)
    y_prompt, new_conv_prompt, new_pool_prompt = _trunk(
        x_prompt, p_prompt, zero_conv, zero_pool, 0, w_in, conv_w, conv_b, ln_g, ln_b, pool_w,
        pool_scale, w_out, g_mix, g_ffn, g_ple, w_ff1, w_ff2, w_ple, w_gate, g_final)
    y_sample, new_conv_sample, new_pool_sample = _trunk(
        x_sample, p_sample, cache_conv, cache_pool, PAST_LEN, w_in, conv_w, conv_b, ln_g, ln_b, pool_w,
        pool_scale, w_out, g_mix, g_ffn, g_ple, w_ff1, w_ff2, w_ple, w_gate, g_final)
    return (y_prompt, y_sample, new_conv_prompt, new_pool_prompt, new_conv_sample, new_pool_sample)
```

```python
import contextlib
import numpy as np
import concourse.bass as bass
import concourse.mybir as mybir
from concourse.bass_utils import run_bass_kernel_spmd

F32 = mybir.dt.float32
BF16 = mybir.dt.bfloat16
ALU = mybir.AluOpType
AF = mybir.ActivationFunctionType

NCORES = 8
D = 1024
KD = D // 128
SEQ = 4096
DEC = 64
DEPTH = 4
DC = 512
CW = 31
HC = CW - 1
HP = 15
PLE = 256
DFF = 4096
EPS = 1e-6
TT = 512
NQ = 4
SLOT = 8 * 1024
NSLOT = 3
POOL_W = (2, 4, 8, 16)

CFG = {"n_pass": 4, "do_sample": True, "depth": DEPTH}


class SemObj:
    def __init__(self, nc, stack, name):
        self.sem = stack.enter_context(nc.semaphore(name))
        self.n = 0


class Eng(SemObj):
    def __init__(self, nc, stack, eng, name):
        super().__init__(nc, stack, "s_" + name)
        self.eng = eng
        self.seen = {}

    def wait(self, deps):
        best = {}
        for (S, c) in deps:
            if c > best.get(S, 0):
                best[S] = c
        for S, c in best.items():
            if c > self.seen.get(S, 0):
                self.eng.wait_ge(S.sem, c)
                self.seen[S] = c


class Buf:
    __slots__ = ("w", "r", "ro")

    def __init__(self):
        self.w = None
        self.r = {}
        self.ro = False


def _deps(reads, writes):
    deps = []
    for b in reads:
        if b.w is not None:
            deps.append(b.w)
    for b in writes:
        if b.w is not None:
            deps.append(b.w)
        deps.extend(b.r.items())
    return deps


def _commit(tok, reads, writes):
    S, c = tok
    for b in reads:
        if not b.ro:
            b.r[S] = c
    for b in writes:
        b.w = tok
        b.r = {}


def grp(E, fns, reads=(), writes=()):
    E.wait(_deps(reads, writes))
    inst = None
    for fn in fns:
        inst = fn(E.eng)
    E.n += 1
    inst.then_inc(E.sem, 1)
    _commit((E, E.n), reads, writes)


def op(E, fn, reads=(), writes=()):
    grp(E, [fn], reads, writes)


def dma(Q, Dm, out, in_, reads=(), writes=()):
    Q.wait(_deps(reads, writes))
    Q.eng.dma_start(out=out, in_=in_).then_inc(Dm.sem, 16)
    Dm.n += 16
    _commit((Dm, Dm.n), reads, writes)


class TileCtx:
    pass


def build_program():
    nc = bass.Bass("TRN2", target_bir_lowering=False)
    depth = CFG["depth"]

    def din(name, shape):
        return nc.dram_tensor(name, list(shape), F32, kind="ExternalInput").ap()

    def dout(name, shape):
        return nc.dram_tensor(name, list(shape), F32, kind="ExternalOutput").ap()

    xp = din("xp", [SEQ, D]); xs = din("xs", [DEC, D])
    pp = din("pp", [DEPTH, SEQ, PLE]); psm = din("psm", [DEPTH, DEC, PLE])
    cc = din("cc", [DEPTH, HC, DC]); cpl = din("cpl", [DEPTH, HP, DC])
    w_in = din("w_in", [DEPTH, D, 3 * DC]); conv_w = din("conv_w", [DEPTH, CW, DC])
    conv_b = din("conv_b", [DEPTH, DC]); ln_g = din("ln_g", [DEPTH, DC]); ln_b = din("ln_b", [DEPTH, DC])
    pool_w = din("pool_w", [DEPTH, 4, 128, 128]); pool_scale = din("pool_scale", [DEPTH, DC])
    w_out = din("w_out", [DEPTH, D, D]); g_mix = din("g_mix", [DEPTH, D]); g_ffn = din("g_ffn", [DEPTH, D])
    g_ple = din("g_ple", [DEPTH, D]); w_ff1 = din("w_ff1", [DEPTH, D, DFF]); w_ff2 = din("w_ff2", [DEPTH, DFF, D])
    w_ple = din("w_ple", [DEPTH, PLE, D]); w_gate = din("w_gate", [DEPTH, D, D]); g_final = din("g_final", [D])
    yp = dout("yp", [SEQ, D]); ys = dout("ys", [DEC, D])
    ncp = dout("ncp", [DEPTH, HC, DC]); npp = dout("npp", [DEPTH, HP, DC])
    ncs = dout("ncs", [DEPTH, HC, DC]); nps = dout("nps", [DEPTH, HP, DC])

    with contextlib.ExitStack() as st:
        def sb(name, shape, dt):
            return st.enter_context(nc.sbuf_tensor(name, list(shape), dt))

        PE = Eng(nc, st, nc.tensor, "pe"); ACT = Eng(nc, st, nc.scalar, "act")
        DVE = Eng(nc, st, nc.vector, "dve"); POOL = Eng(nc, st, nc.gpsimd, "pool")
        SP = Eng(nc, st, nc.sync, "sp")

        ident = sb("ident", [128, 128], F32); identB = Buf()
        ones_d = sb("ones_d", [128, 128], BF16)
        ones_c = sb("ones_c", [128, 128], BF16)
        epsb = sb("epsb", [128, 1], F32)
        constB = Buf()
        invcnt = sb("invcnt", [128, 4, 16], F32)
        NV = 3 * KD + 4 * 4
        C_GMIX, C_GFFN, C_GPLE, C_CB, C_LG, C_LB, C_PS = 0, 8, 16, 24, 28, 32, 36
        C_GFIN = DEPTH * NV
        C_CW = C_GFIN + KD
        NCOL = C_CW + DEPTH * CW * 4
        tab = sb("tab", [128, NCOL], F32); tabB = Buf()

        def tcol(l, base, k):
            c = l * NV + base + k
            return tab[:, c:c + 1]

        def cwcol(l, j, m):
            c = C_CW + l * CW * 4 + j * 4 + m
            return tab[:, c:c + 1]

        banks = [st.enter_context(nc.psum_tensor("bank%d" % i, [128, 512], F32)) for i in range(8)]
        bankB = [Buf() for _ in range(8)]
        bank_i = [0]

        def next_bank():
            i = bank_i[0]
            bank_i[0] = (i + 1) % 8
            return banks[i], bankB[i]

        lnsc = sb("lnsc", [128, 8, TT], BF16); lnscB = Buf()
        stat = [sb("stat%d" % i, [128, TT], F32) for i in range(4)]
        statB = [Buf() for _ in range(4)]
        sigs = [sb("sig%d" % i, [128, TT], F32) for i in range(2)]; sigB = [Buf(), Buf()]; sig_i = [0]
        relu = [sb("relu%d" % i, [128, TT], BF16) for i in range(2)]; reluB = [Buf(), Buf()]; relu_i = [0]
        gtmp = [sb("gtmp%d" % i, [128, TT], F32) for i in range(2)]; gtmpB = [Buf(), Buf()]; gtmp_i = [0]
        psA = sb("psA", [128, HP + TT], F32); psB_ = sb("psB", [128, HP + TT], F32)
        psAB = Buf(); psBB = Buf()
        xst = [sb("xst%d" % i, [128, D], F32) for i in range(2)]
        xstB = [Buf(), Buf()]; xstD = [SemObj(nc, st, "xstd%d" % i) for i in range(2)]; xst_i = [0]
        pst = sb("pst", [128, 4, PLE], F32); pstB = Buf(); pstD = SemObj(nc, st, "pstd")
        hst = sb("hst", [32, DC], F32); hstB = Buf(); hstD = SemObj(nc, st, "hstd")
        diag = [sb("diag%d" % i, [128, CW, 128], BF16) for i in range(2)]; diagB = [Buf(), Buf()]; diag_i = [0]
        hist_c = sb("hist_c", [128, DEPTH, 4, HC], BF16); hist_cB = [Buf() for _ in range(DEPTH)]
        hist_u = sb("hist_u", [128, DEPTH, 4, HP], F32); hist_uB = [Buf() for _ in range(DEPTH)]
        tail32 = sb("tail32", [128, 4, HC], F32); tail32B = Buf()

        ring = [sb("ring%d" % i, [128, SLOT], BF16) for i in range(NSLOT)]
        ringB = [Buf() for _ in range(NSLOT)]
        ringD = [SemObj(nc, st, "ringd%d" % i) for i in range(NSLOT)]

        def make_tile(name):
            t = TileCtx()
            t.x = sb(name + "_x", [128, KD, TT], F32); t.xB = [Buf() for _ in range(KD)]
            t.h = sb(name + "_h", [128, KD, TT], BF16); t.hB = [Buf() for _ in range(KD)]
            t.cbf = sb(name + "_c", [128, 4, HC + TT], BF16); t.cB = [Buf() for _ in range(4)]
            t.u = sb(name + "_u", [128, 4, HP + TT], F32); t.uB = [Buf() for _ in range(4)]
            t.cv = sb(name + "_cv", [128, 4, TT], F32)
            t.hid = t.cv.bitcast(BF16).reshape([128, KD, TT])
            t.hidB = [Buf() for _ in range(KD)]
            t.pT = sb(name + "_pT", [128, 2, TT], BF16); t.pTB = Buf()
            return t

        tA = make_tile("tA"); tB = make_tile("tB")

        setupD = SemObj(nc, st, "setupd")
        op(POOL, lambda e: e.memset(ident[:], 0.0), writes=[identB])
        op(POOL, lambda e: e.affine_select(out=ident[:], in_=ident[:], compare_op=ALU.not_equal, fill=1.0,
                                           base=0, pattern=[[-1, 128]], channel_multiplier=1),
           reads=[identB], writes=[identB])
        grp(POOL, [lambda e: e.memset(ones_d[:], 1.0 / D), lambda e: e.memset(ones_c[:], 1.0 / DC),
                   lambda e: e.memset(epsb[:], EPS)], writes=[constB])
        fns = []
        for g, w in enumerate(POOL_W):
            for pos in range(16):
                v = 1.0 / min(w, pos + 1)
                fns.append(lambda e, g=g, pos=pos, v=v: e.memset(invcnt[:, g, pos:pos + 1], v))
        grp(POOL, fns, writes=[constB])

        stg = tA.x
        stgB = Buf()
        vecs = [(g_mix, C_GMIX, KD), (g_ffn, C_GFFN, KD), (g_ple, C_GPLE, KD), (conv_b, C_CB, 4),
                (ln_g, C_LG, 4), (ln_b, C_LB, 4), (pool_scale, C_PS, 4)]
        for l in range(DEPTH):
            ti, r0 = l // 2, (l % 2) * NV
            for (v, base, nk) in vecs:
                src = v[l].rearrange("(k p) -> k p", p=128)
                dma(SP, setupD, stg[r0 + base:r0 + base + nk, ti, 0:128], src, writes=[stgB])
        dma(SP, setupD, stg[2 * NV:2 * NV + KD, 1, 0:128], g_final.rearrange("(k p) -> k p", p=128), writes=[stgB])
        for l in range(DEPTH):
            src = conv_w[l].rearrange("j (m p) -> j m p", p=128)
            dma(SP, setupD, stg[0:CW * 4, 2 + l, 0:128], conv_w[l].rearrange("j (m p) -> (j m) p", p=128), writes=[stgB])
        stgB.w = (setupD, setupD.n)
        tr_plan = [(0, 2 * NV, 0), (1, 2 * NV + KD, 2 * NV)] + [(2 + l, CW * 4, C_CW + l * CW * 4) for l in range(DEPTH)]
        for (ti, rows, col0) in tr_plan:
            bk, bkB = next_bank()
            grp(PE, [lambda e, ti=ti, rows=rows, bk=bk: e.transpose(bk[:, 0:rows], stg[0:rows, ti, 0:128], ident[0:rows, 0:rows])],
                reads=[stgB, identB], writes=[bkB])
            op(ACT, lambda e, rows=rows, col0=col0, bk=bk: e.activation(out=tab[:, col0:col0 + rows], in_=bk[:, 0:rows], func=AF.Copy),
               reads=[bkB], writes=[tabB])
        for E in (PE, ACT, DVE, POOL):
            E.wait([tabB.w, constB.w, identB.w])
        for b in (tabB, constB, identB):
            b.ro = True
        for k in range(KD):
            tA.xB[k].r = dict(stgB.r)
            tA.xB[k].w = stgB.w

        def unit_srcs(l, kind, q=0):
            if kind == "in_ag":
                return [(0, w_in[l].rearrange("(k p) n -> p k n", p=128)[:, :, 0:2 * DC], KD, 2 * DC)]
            if kind == "in_u":
                return [(0, w_in[l].rearrange("(k p) n -> p k n", p=128)[:, :, 2 * DC:3 * DC], KD, DC),
                        (KD * DC, pool_w[l].rearrange("g c d -> c g d"), 4, 128)]
            if kind == "out":
                return [(0, w_out[l].rearrange("(k p) n -> p k n", p=128), KD, D)]
            if kind == "ff1":
                return [(0, w_ff1[l].rearrange("(k p) n -> p k n", p=128)[:, :, q * D:(q + 1) * D], KD, D)]
            if kind == "ff2":
                return [(0, w_ff2[l][q * D:(q + 1) * D, :].rearrange("(k p) n -> p k n", p=128), KD, D)]
            if kind == "gate":
                return [(0, w_gate[l].rearrange("(k p) n -> p k n", p=128), KD, D)]
            if kind == "ple":
                return [(0, w_ple[l].rearrange("(k p) n -> p k n", p=128), 2, D)]
            raise ValueError(kind)

        layer_units = [("in_ag", 0), ("in_u", 0), ("out", 0)]
        for q in range(NQ):
            layer_units += [("ff1", q), ("ff2", q)]
        layer_units += [("gate", 0), ("ple", 0)]
        NU = len(layer_units)

        n_pass = CFG["n_pass"]
        passes = [("prompt", p) for p in range(n_pass)]
        if CFG["do_sample"]:
            passes.append(("sample", 0))
        stream = [(pi, l, ui) for pi in range(len(passes)) for l in range(depth) for ui in range(NU)]
        stream_pos = {key: i for i, key in enumerate(stream)}
        issued = [0]

        def issue_loads(upto):
            upto = min(upto, len(stream) - 1)
            while issued[0] <= upto:
                i = issued[0]
                (pi, l, ui) = stream[i]
                kind, q = layer_units[ui]
                s = i % NSLOT
                for (off, src, nk, nn) in unit_srcs(l, kind, q):
                    dst = ring[s][:, off:off + nk * nn].rearrange("p (k n) -> p k n", k=nk)
                    half = max(1, nk // 2)
                    for k0 in range(0, nk, half):
                        dma(POOL, ringD[s], dst[:, k0:k0 + half, :], src[:, k0:k0 + half, :], writes=[ringB[s]])
                issued[0] += 1

        def get_unit(pi, l, kind, q=0, look=NSLOT - 1):
            ui = layer_units.index((kind, q))
            i = stream_pos[(pi, l, ui)]
            issue_loads(i + look)
            s = i % NSLOT
            return ring[s], ringB[s]

        def wview(slot, nk, nn, off=0):
            return slot[:, off:off + nk * nn].rearrange("p (k n) -> p k n", k=nk)

        def mm_group(bk, bkB, T, pairs, reads):
            n = len(pairs)
            fns = []
            for i, (lhsT, rhs) in enumerate(pairs):
                fns.append(lambda e, lhsT=lhsT, rhs=rhs, i=i: e.matmul(bk[:, 0:T], lhsT=lhsT, rhs=rhs,
                                                                      start=(i == 0), stop=(i == n - 1)))
            grp(PE, fns, reads=reads, writes=[bkB])

        def rstd_from_bank(bk, bkB, T, si):
            op(ACT, lambda e: e.activation(out=stat[si][:, 0:T], in_=bk[:, 0:T], func=AF.Sqrt, bias=epsb[:, 0:1]),
               reads=[bkB], writes=[statB[si]])
            op(DVE, lambda e: e.reciprocal(out=stat[si][:, 0:T], in_=stat[si][:, 0:T]),
               reads=[statB[si]], writes=[statB[si]])

        def rmsnorm(t, T, gbase, l, out_f32=False):
            op(ACT, lambda e: e.activation(out=t.h[:, :, 0:T], in_=t.x[:, :, 0:T], func=AF.Square),
               reads=t.xB, writes=t.hB)
            bk, bkB = next_bank()
            mm_group(bk, bkB, T, [(ones_d[:], t.h[:, k, 0:T]) for k in range(KD)], reads=t.hB)
            rstd_from_bank(bk, bkB, T, 0)
            fns = []
            for k in range(KD):
                gcol = tab[:, C_GFIN + k:C_GFIN + k + 1] if l is None else tcol(l, gbase, k)
                dst = t.x[:, k, 0:T] if out_f32 else t.h[:, k, 0:T]
                fns.append(lambda e, k=k, gcol=gcol, dst=dst: e.scalar_tensor_tensor(
                    out=dst, in0=t.x[:, k, 0:T], scalar=gcol, in1=stat[0][:, 0:T], op0=ALU.mult, op1=ALU.mult))
            if out_f32:
                grp(DVE, fns, reads=[statB[0]] + t.xB, writes=t.xB)
            else:
                grp(DVE, fns, reads=[statB[0]] + t.xB, writes=t.hB)

        def load_x(t, T, src, tok0):
            nch = max(1, T // 128)
            for tc in range(nch):
                rows = min(128, T)
                b = xst_i[0]; xst_i[0] ^= 1
                dma(SP, xstD[b], xst[b][0:rows, :], src[tok0 + tc * 128:tok0 + tc * 128 + rows, :], writes=[xstB[b]])
                for half in range(2):
                    bk, bkB = next_bank()
                    fns = []
                    for kk in range(4):
                        k = half * 4 + kk
                        fns.append(lambda e, k=k, kk=kk, b=b, bk=bk, rows=rows: e.transpose(
                            bk[:, kk * 128:kk * 128 + rows], xst[b][0:rows, k * 128:(k + 1) * 128], ident[0:rows, 0:rows]))
                    grp(PE, fns, reads=[xstB[b]], writes=[bkB])
                    src_v = bk[:, :].rearrange("p (a b) -> p a b", a=4)[:, :, 0:rows]
                    op(ACT, lambda e, half=half, tc=tc, src_v=src_v, rows=rows: e.activation(
                        out=t.x[:, half * 4:half * 4 + 4, tc * 128:tc * 128 + rows], in_=src_v, func=AF.Copy),
                       reads=[bkB], writes=t.xB[half * 4:half * 4 + 4])

        def store_y(t, T, dst, tok0):
            nch = max(1, T // 128)
            for tc in range(nch):
                rows = min(128, T)
                b = xst_i[0]; xst_i[0] ^= 1
                for half in range(2):
                    bk, bkB = next_bank()
                    fns = []
                    for kk in range(4):
                        k = half * 4 + kk
                        fns.append(lambda e, k=k, kk=kk, bk=bk, rows=rows, tc=tc: e.transpose(
                            bk[0:rows, kk * 128:(kk + 1) * 128], t.x[:, k, tc * 128:tc * 128 + rows], ident[:, :]))
                    grp(PE, fns, reads=t.xB[half * 4:half * 4 + 4], writes=[bkB])
                    op(ACT, lambda e, half=half, b=b, bk=bk, rows=rows: e.activation(
                        out=xst[b][0:rows, half * 512:(half + 1) * 512], in_=bk[0:rows, :], func=AF.Copy),
                       reads=[bkB], writes=[xstB[b]])
                dma(SP, xstD[b], dst[tok0 + tc * 128:tok0 + tc * 128 + rows, :], xst[b][0:rows, :], reads=[xstB[b]])

        def load_p(t, T, src_l, tok0):
            nch = max(1, T // 128)
            rows = min(128, T)
            if nch > 1:
                dma(SP, pstD, pst[:, 0:nch, :], src_l[tok0:tok0 + T, :].rearrange("(c p) f -> p c f", p=128), writes=[pstB])
            else:
                dma(SP, pstD, pst[0:rows, 0, :], src_l[tok0:tok0 + rows, :], writes=[pstB])
            for fc in range(2):
                bk, bkB = next_bank()
                fns = []
                for tc in range(nch):
                    fns.append(lambda e, tc=tc, fc=fc, bk=bk: e.transpose(
                        bk[:, tc * 128:tc * 128 + rows], pst[0:rows, tc, fc * 128:(fc + 1) * 128], ident[0:rows, 0:rows]))
                grp(PE, fns, reads=[pstB], writes=[bkB])
                op(ACT, lambda e, fc=fc, bk=bk: e.activation(out=t.pT[:, fc, 0:T], in_=bk[:, 0:T], func=AF.Copy),
                   reads=[bkB], writes=[t.pTB])

        def build_diag(l, m):
            b = diag_i[0]; diag_i[0] ^= 1
            fns = []
            for j in range(CW):
                fns.append(lambda e, j=j, b=b: e.tensor_scalar_mul(out=diag[b][:, j, :], in0=ident[:, :], scalar1=cwcol(l, j, m)))
            grp(DVE, fns, writes=[diagB[b]])
            return b

        def out_hist(src_ap3, srcB, n, dst):
            bk, bkB = next_bank()
            fns = []
            for m in range(4):
                fns.append(lambda e, m=m, bk=bk: e.transpose(bk[0:n, m * 128:(m + 1) * 128], src_ap3[:, m, :], ident[:, :]))
            grp(PE, fns, reads=srcB, writes=[bkB])
            op(ACT, lambda e, bk=bk: e.activation(out=hst[0:n, :], in_=bk[0:n, :], func=AF.Copy), reads=[bkB], writes=[hstB])
            dma(SP, hstD, dst, hst[0:n, :], reads=[hstB])

        def load_hist(src, n, l, is_conv):
            dma(SP, hstD, hst[0:n, :], src, writes=[hstB])
            bk, bkB = next_bank()
            fns = []
            for m in range(4):
                fns.append(lambda e, m=m, bk=bk: e.transpose(bk[:, m * 32:m * 32 + n], hst[0:n, m * 128:(m + 1) * 128], ident[0:n, 0:n]))
            grp(PE, fns, reads=[hstB], writes=[bkB])
            src_v = bk[:, 0:128].rearrange("p (a b) -> p a b", a=4)[:, :, 0:n]
            if is_conv:
                op(ACT, lambda e: e.activation(out=hist_c[:, l, :, :], in_=src_v, func=AF.Copy), reads=[bkB], writes=[hist_cB[l]])
            else:
                op(ACT, lambda e: e.activation(out=hist_u[:, l, :, :], in_=src_v, func=AF.Copy), reads=[bkB], writes=[hist_uB[l]])

        def stage_in(tiles, pi, l):
            wag, wagB = get_unit(pi, l, "in_ag")
            Wag = wview(wag, KD, 2 * DC)
            for (t, T, info) in tiles:
                op(ACT, lambda e, t=t: e.activation(out=t.cbf[:, :, 0:HC], in_=hist_c[:, l, :, :], func=AF.Copy),
                   reads=[hist_cB[l]], writes=t.cB)
                for m in range(4):
                    bka, bkaB = next_bank()
                    mm_group(bka, bkaB, T, [(Wag[:, k, m * 128:(m + 1) * 128], t.h[:, k, 0:T]) for k in range(KD)],
                             reads=[wagB] + t.hB)
                    bkg, bkgB = next_bank()
                    mm_group(bkg, bkgB, T, [(Wag[:, k, DC + m * 128:DC + (m + 1) * 128], t.h[:, k, 0:T]) for k in range(KD)],
                             reads=[wagB] + t.hB)
                    si = sig_i[0]; sig_i[0] ^= 1
                    op(ACT, lambda e, si=si, bkg=bkg, T=T: e.activation(out=sigs[si][:, 0:T], in_=bkg[:, 0:T], func=AF.Sigmoid),
                       reads=[bkgB], writes=[sigB[si]])
                    fns = [lambda e, t=t, m=m, bka=bka, si=si, T=T: e.tensor_tensor(
                        out=t.cbf[:, m, HC:HC + T], in0=bka[:, 0:T], in1=sigs[si][:, 0:T], op=ALU.mult)]
                    wr = [t.cB[m]]
                    if info["last"]:
                        fns.append(lambda e, m=m, bka=bka, si=si, T=T: e.tensor_tensor(
                            out=tail32[:, m, :], in0=bka[:, T - HC:T], in1=sigs[si][:, T - HC:T], op=ALU.mult))
                        wr = wr + [tail32B]
                    grp(DVE, fns, reads=[bkaB, sigB[si]], writes=wr)
                if info["last"]:
                    out_hist(tail32, [tail32B], HC, info["nc_out"][l])
                else:
                    op(ACT, lambda e, t=t, T=T: e.activation(out=hist_c[:, l, :, :], in_=t.cbf[:, :, T:T + HC], func=AF.Copy),
                       reads=t.cB, writes=[hist_cB[l]])
            wu, wuB = get_unit(pi, l, "in_u")
            Wu = wview(wu, KD, DC)
            for (t, T, info) in tiles:
                op(ACT, lambda e, t=t: e.activation(out=t.u[:, :, 0:HP], in_=hist_u[:, l, :, :], func=AF.Copy),
                   reads=[hist_uB[l]], writes=t.uB)
                for m in range(4):
                    bk, bkB = next_bank()
                    mm_group(bk, bkB, T, [(Wu[:, k, m * 128:(m + 1) * 128], t.h[:, k, 0:T]) for k in range(KD)],
                             reads=[wuB] + t.hB)
                    op(ACT, lambda e, t=t, m=m, bk=bk, T=T: e.activation(out=t.u[:, m, HP:HP + T], in_=bk[:, 0:T], func=AF.Copy),
                       reads=[bkB], writes=[t.uB[m]])
                op(ACT, lambda e, t=t, T=T: e.activation(out=hist_u[:, l, :, :], in_=t.u[:, :, T:T + HP], func=AF.Copy),
                   reads=t.uB, writes=[hist_uB[l]])
                if info["last"]:
                    out_hist(hist_u[:, l, :, :], [hist_uB[l]], HP, info["np_out"][l])
            return Wu, wu, wuB

        def stage_pool(tiles, l, wu, wuB):
            Wp = wview(wu, 4, 128, off=KD * DC)
            for (t, T, info) in tiles:
                L = HP + T
                for g in range(4):
                    ug = t.u[:, g, :]
                    steps = [(psA, psAB, 1)]
                    if g >= 1:
                        steps.append((psB_, psBB, 2))
                    if g >= 2:
                        steps.append((psA, psAB, 4))
                    if g >= 3:
                        steps.append((psB_, psBB, 8))
                    src, srcB, lo = ug, t.uB[g], 0
                    for (dst, dstB, sh) in steps:
                        nlo = lo + sh
                        op(DVE, lambda e, dst=dst, src=src, nlo=nlo, sh=sh, L=L: e.tensor_tensor(
                            out=dst[:, nlo:L], in0=src[:, nlo:L], in1=src[:, nlo - sh:L - sh], op=ALU.add),
                           reads=[srcB], writes=[dstB])
                        src, srcB, lo = dst, dstB, nlo
                    w = POOL_W[g]
                    op(DVE, lambda e, src=src, ug=ug, w=w, t=t, g=g, T=T: e.scalar_tensor_tensor(
                        out=t.h[:, g, 0:T], in0=src[:, HP:HP + T], scalar=1.0 / w, in1=ug[:, HP:HP + T],
                        op0=ALU.mult, op1=ALU.subtract), reads=[srcB, t.uB[g]], writes=[t.hB[g]])
                    if info["first"]:
                        nfix = w - 1
                        op(DVE, lambda e, src=src, g=g, nfix=nfix: e.tensor_tensor(
                            out=stat[3][:, 0:nfix], in0=src[:, HP:HP + nfix], in1=invcnt[:, g, 0:nfix], op=ALU.mult),
                           reads=[srcB], writes=[statB[3]])
                        op(DVE, lambda e, ug=ug, t=t, g=g, nfix=nfix: e.tensor_tensor(
                            out=t.h[:, g, 0:nfix], in0=stat[3][:, 0:nfix], in1=ug[:, HP:HP + nfix], op=ALU.subtract),
                           reads=[statB[3], t.uB[g]], writes=[t.hB[g]])
                    bk, bkB = next_bank()
                    mm_group(bk, bkB, T, [(Wp[:, g, :], t.h[:, g, 0:T])], reads=[wuB, t.hB[g]])
                    op(ACT, lambda e, t=t, g=g, bk=bk, T=T: e.activation(out=t.h[:, 4 + g, 0:T], in_=bk[:, 0:T], func=AF.Copy,
                                                                        scale=tcol(l, C_PS, g)),
                       reads=[bkB], writes=[t.hB[4 + g]])

        def stage_conv(tiles, l):
            for m in range(4):
                db = build_diag(l, m)
                for ti, (t, T, info) in enumerate(tiles):
                    bk, bkB = next_bank()
                    mm_group(bk, bkB, T, [(diag[db][:, j, :], t.cbf[:, m, j:j + T]) for j in range(CW)],
                             reads=[diagB[db], t.cB[m]])
                    op(ACT, lambda e, t=t, m=m, bk=bk, T=T: e.activation(
                        out=t.cv[:, m, 0:T], in_=bk[:, 0:T], func=AF.Identity, bias=tcol(l, C_CB, m)),
                       reads=[bkB], writes=[t.hidB[2 * m], t.hidB[2 * m + 1]])
            for (t, T, info) in tiles:
                op(ACT, lambda e, t=t, T=T: e.activation(out=lnsc[:, 0:4, 0:T], in_=t.cv[:, :, 0:T], func=AF.Copy),
                   reads=t.hidB, writes=[lnscB])
                op(ACT, lambda e, t=t, T=T: e.activation(out=lnsc[:, 4:8, 0:T], in_=t.cv[:, :, 0:T], func=AF.Square),
                   reads=t.hidB, writes=[lnscB])
                bkm, bkmB = next_bank()
                mm_group(bkm, bkmB, T, [(ones_c[:], lnsc[:, m, 0:T]) for m in range(4)], reads=[lnscB])
                bkq, bkqB = next_bank()
                mm_group(bkq, bkqB, T, [(ones_c[:], lnsc[:, 4 + m, 0:T]) for m in range(4)], reads=[lnscB])
                op(ACT, lambda e, bkm=bkm, T=T: e.activation(out=stat[1][:, 0:T], in_=bkm[:, 0:T], func=AF.Copy),
                   reads=[bkmB], writes=[statB[1]])
                grp(DVE, [lambda e, T=T: e.tensor_tensor(out=stat[2][:, 0:T], in0=stat[1][:, 0:T], in1=stat[1][:, 0:T], op=ALU.mult)],
                    reads=[statB[1]], writes=[statB[2]])
                op(DVE, lambda e, bkq=bkq, T=T: e.tensor_tensor(out=stat[2][:, 0:T], in0=bkq[:, 0:T], in1=stat[2][:, 0:T], op=ALU.subtract),
                   reads=[bkqB, statB[2]], writes=[statB[2]])
                op(DVE, lambda e, T=T: e.tensor_scalar(out=stat[2][:, 0:T], in0=stat[2][:, 0:T], scalar1=0.0, scalar2=EPS,
                                                      op0=ALU.max, op1=ALU.add),
                   reads=[statB[2]], writes=[statB[2]])
                op(ACT, lambda e, T=T: e.activation(out=stat[2][:, 0:T], in_=stat[2][:, 0:T], func=AF.Sqrt),
                   reads=[statB[2]], writes=[statB[2]])
                op(DVE, lambda e, T=T: e.reciprocal(out=stat[2][:, 0:T], in_=stat[2][:, 0:T]), reads=[statB[2]], writes=[statB[2]])
                op(DVE, lambda e, T=T: e.scalar_tensor_tensor(out=stat[1][:, 0:T], in0=stat[1][:, 0:T], scalar=-1.0, in1=stat[2][:, 0:T],
                                                             op0=ALU.mult, op1=ALU.mult),
                   reads=[statB[1], statB[2]], writes=[statB[1]])
                for m in range(4):
                    hb = [t.hidB[2 * m], t.hidB[2 * m + 1]]
                    op(DVE, lambda e, t=t, m=m, T=T: e.tensor_tensor(out=t.cv[:, m, 0:T], in0=t.cv[:, m, 0:T], in1=stat[2][:, 0:T], op=ALU.mult),
                       reads=hb + [statB[2]], writes=hb)
                    op(DVE, lambda e, t=t, m=m, T=T: e.tensor_tensor(out=t.cv[:, m, 0:T], in0=t.cv[:, m, 0:T], in1=stat[1][:, 0:T], op=ALU.add),
                       reads=hb + [statB[1]], writes=hb)
                    op(ACT, lambda e, t=t, m=m, T=T: e.activation(out=t.h[:, m, 0:T], in_=t.cv[:, m, 0:T], func=AF.Silu,
                                                                 bias=tcol(l, C_LB, m), scale=tcol(l, C_LG, m)),
                       reads=hb, writes=[t.hB[m]])

        def stage_proj_add(tiles, pi, l, kind):
            w, wB = get_unit(pi, l, kind)
            W = wview(w, KD, D)
            for (t, T, info) in tiles:
                for m in range(KD):
                    bk, bkB = next_bank()
                    mm_group(bk, bkB, T, [(W[:, k, m * 128:(m + 1) * 128], t.h[:, k, 0:T]) for k in range(KD)],
                             reads=[wB] + t.hB)
                    op(DVE, lambda e, t=t, m=m, bk=bk, T=T: e.tensor_tensor(out=t.x[:, m, 0:T], in0=bk[:, 0:T], in1=t.x[:, m, 0:T], op=ALU.add),
                       reads=[bkB, t.xB[m]], writes=[t.xB[m]])

        def stage_ffn(tiles, pi, l):
            for q in range(NQ):
                w1, w1B = get_unit(pi, l, "ff1", q)
                W1 = wview(w1, KD, D)
                for (t, T, info) in tiles:
                    for m in range(KD):
                        bk, bkB = next_bank()
                        mm_group(bk, bkB, T, [(W1[:, k, m * 128:(m + 1) * 128], t.h[:, k, 0:T]) for k in range(KD)],
                                 reads=[w1B] + t.hB)
                        ri = relu_i[0]; relu_i[0] ^= 1
                        op(ACT, lambda e, ri=ri, bk=bk, T=T: e.activation(out=relu[ri][:, 0:T], in_=bk[:, 0:T], func=AF.Relu),
                           reads=[bkB], writes=[reluB[ri]])
                        op(DVE, lambda e, t=t, m=m, ri=ri, T=T: e.tensor_tensor(out=t.hid[:, m, 0:T], in0=relu[ri][:, 0:T],
                                                                               in1=relu[ri][:, 0:T], op=ALU.mult),
                           reads=[reluB[ri]], writes=[t.hidB[m]])
                w2, w2B = get_unit(pi, l, "ff2", q)
                W2 = wview(w2, KD, D)
                for (t, T, info) in tiles:
                    for m in range(KD):
                        bk, bkB = next_bank()
                        mm_group(bk, bkB, T, [(W2[:, k, m * 128:(m + 1) * 128], t.hid[:, k, 0:T]) for k in range(KD)],
                                 reads=[w2B] + t.hidB)
                        op(DVE, lambda e, t=t, m=m, bk=bk, T=T: e.tensor_tensor(out=t.x[:, m, 0:T], in0=bk[:, 0:T], in1=t.x[:, m, 0:T], op=ALU.add),
                           reads=[bkB, t.xB[m]], writes=[t.xB[m]])

        def stage_gate(tiles, pi, l):
            wg, wgB = get_unit(pi, l, "gate")
            Wg = wview(wg, KD, D)
            wp, wpB = get_unit(pi, l, "ple", look=NSLOT - 2)
            Wpl = wview(wp, 2, D)
            for (t, T, info) in tiles:
                for m in range(KD):
                    bkg, bkgB = next_bank()
                    mm_group(bkg, bkgB, T, [(Wg[:, k, m * 128:(m + 1) * 128], t.h[:, k, 0:T]) for k in range(KD)],
                             reads=[wgB] + t.hB)
                    bkp, bkpB = next_bank()
                    mm_group(bkp, bkpB, T, [(Wpl[:, kc, m * 128:(m + 1) * 128], t.pT[:, kc, 0:T]) for kc in range(2)],
                             reads=[wpB, t.pTB])
                    si = sig_i[0]; sig_i[0] ^= 1
                    op(ACT, lambda e, si=si, bkg=bkg, T=T: e.activation(out=sigs[si][:, 0:T], in_=bkg[:, 0:T], func=AF.Sigmoid),
                       reads=[bkgB], writes=[sigB[si]])
                    gi = gtmp_i[0]; gtmp_i[0] ^= 1
                    op(DVE, lambda e, gi=gi, si=si, bkp=bkp, T=T: e.tensor_tensor(out=gtmp[gi][:, 0:T], in0=bkp[:, 0:T], in1=sigs[si][:, 0:T], op=ALU.mult),
                       reads=[bkpB, sigB[si]], writes=[gtmpB[gi]])
                    op(DVE, lambda e, t=t, m=m, gi=gi, T=T: e.tensor_tensor(out=t.x[:, m, 0:T], in0=gtmp[gi][:, 0:T], in1=t.x[:, m, 0:T], op=ALU.add),
                       reads=[gtmpB[gi], t.xB[m]], writes=[t.xB[m]])

        for pi, (kind, p) in enumerate(passes):
            if kind == "prompt":
                tiles = []
                for ti, t in enumerate((tA, tB)):
                    tok0 = p * 2 * TT + ti * TT
                    info = {"first": tok0 == 0, "last": tok0 + TT == SEQ, "nc_out": ncp, "np_out": npp,
                            "tok0": tok0, "xsrc": xp, "psrc": pp, "ydst": yp}
                    tiles.append((t, TT, info))
                if p == 0:
                    for l in range(depth):
                        op(DVE, lambda e, l=l: e.memset(hist_c[:, l, :, :], 0.0), writes=[hist_cB[l]])
                        op(DVE, lambda e, l=l: e.memset(hist_u[:, l, :, :], 0.0), writes=[hist_uB[l]])
            else:
                info = {"first": False, "last": True, "nc_out": ncs, "np_out": nps,
                        "tok0": 0, "xsrc": xs, "psrc": psm, "ydst": ys}
                tiles = [(tA, DEC, info)]
                for l in range(depth):
                    load_hist(cc[l], HC, l, True)
                    load_hist(cpl[l], HP, l, False)
            for (t, T, info) in tiles:
                load_x(t, T, info["xsrc"], info["tok0"])
            for l in range(depth):
                for (t, T, info) in tiles:
                    rmsnorm(t, T, C_GMIX, l)
                _, wu, wuB = stage_in(tiles, pi, l)
                stage_pool(tiles, l, wu, wuB)
                stage_conv(tiles, l)
                stage_proj_add(tiles, pi, l, "out")
                for (t, T, info) in tiles:
                    rmsnorm(t, T, C_GFFN, l)
                stage_ffn(tiles, pi, l)
                for (t, T, info) in tiles:
                    rmsnorm(t, T, C_GPLE, l)
                    load_p(t, T, info["psrc"][l], info["tok0"])
                stage_gate(tiles, pi, l)
            for (t, T, info) in tiles:
                rmsnorm(t, T, None, None, out_f32=True)
                store_y(t, T, info["ydst"], info["tok0"])

        SP.wait([(xstD[0], xstD[0].n), (xstD[1], xstD[1].n), (hstD, hstD.n)])
    return nc


_PROGRAM = [None]


def kernel(x_prompt, x_sample, p_prompt, p_sample, cache_conv, cache_pool, w_in, conv_w, conv_b,
           ln_g, ln_b, pool_w, pool_scale, w_out, g_mix, g_ffn, g_ple, w_ff1, w_ff2, w_ple, w_gate, g_final):
    f = lambda a: np.ascontiguousarray(np.asarray(a, dtype=np.float32))
    shared = {"w_in": f(w_in), "conv_w": f(conv_w), "conv_b": f(conv_b), "ln_g": f(ln_g), "ln_b": f(ln_b),
              "pool_w": f(pool_w), "pool_scale": f(pool_scale), "w_out": f(w_out), "g_mix": f(g_mix),
              "g_ffn": f(g_ffn), "g_ple": f(g_ple), "w_ff1": f(w_ff1), "w_ff2": f(w_ff2), "w_ple": f(w_ple),
              "w_gate": f(w_gate), "g_final": f(g_final)}
    in_maps = []
    for i in range(NCORES):
        m = dict(shared)
        m["xp"] = f(x_prompt[i]); m["xs"] = f(x_sample[i])
        m["pp"] = f(p_prompt[:, i]); m["psm"] = f(p_sample[:, i])
        m["cc"] = f(cache_conv[:, i]); m["cpl"] = f(cache_pool[:, i])
        in_maps.append(m)
    nc = build_program()
    res = run_bass_kernel_spmd(nc, in_maps, core_ids=list(range(NCORES)))
    r = res.results
    y_prompt = np.stack([r[i]["yp"] for i in range(NCORES)], 0)
    y_sample = np.stack([r[i]["ys"] for i in range(NCORES)], 0)
    ncp = np.stack([r[i]["ncp"] for i in range(NCORES)], 1)
    npp = np.stack([r[i]["npp"] for i in range(NCORES)], 1)
    ncs = np.stack([r[i]["ncs"] for i in range(NCORES)], 1)
    nps = np.stack([r[i]["nps"] for i in range(NCORES)], 1)
    return (y_prompt.astype(np.float32), y_sample.astype(np.float32), ncp.astype(np.float32),
            npp.astype(np.float32), ncs.astype(np.float32), nps.astype(np.float32))
```

```python
import contextlib
import numpy as np
import concourse.bass as bass
import concourse.mybir as mybir
from concourse.bass_utils import run_bass_kernel_spmd

F32 = mybir.dt.float32
BF16 = mybir.dt.bfloat16
ALU = mybir.AluOpType
AF = mybir.ActivationFunctionType

NCORES = 8
D = 1024
KD = D // 128
SEQ = 4096
DEC = 64
DEPTH = 4
DC = 512
CW = 31
HC = CW - 1
HP = 15
PLE = 256
DFF = 4096
EPS = 1e-6
TT = 512
NQ = 4
SLOT = 5 * 1024
NSLOT = 4
POOL_W = (2, 4, 8, 16)

CFG = {"n_pass": 4, "do_sample": True, "depth": DEPTH}


class SemObj:
    def __init__(self, nc, stack, name):
        self.sem = stack.enter_context(nc.semaphore(name))
        self.n = 0


class Eng(SemObj):
    def __init__(self, nc, stack, eng, name):
        super().__init__(nc, stack, "s_" + name)
        self.eng = eng
        self.seen = {}

    def wait(self, deps):
        best = {}
        for (S, c) in deps:
            if c > best.get(S, 0):
                best[S] = c
        for S, c in best.items():
            if c > self.seen.get(S, 0):
                self.eng.wait_ge(S.sem, c)
                self.seen[S] = c


class Buf:
    __slots__ = ("w", "r", "ro")

    def __init__(self):
        self.w = None
        self.r = {}
        self.ro = False


def _deps(reads, writes):
    deps = []
    for b in reads:
        if b.w is not None:
            deps.append(b.w)
    for b in writes:
        if b.w is not None:
            deps.append(b.w)
        deps.extend(b.r.items())
    return deps


def _commit(tok, reads, writes):
    S, c = tok
    for b in reads:
        if not b.ro:
            b.r[S] = c
    for b in writes:
        b.w = tok
        b.r = {}


def grp(E, fns, reads=(), writes=(), nosame=False):
    deps = _deps(reads, writes)
    if nosame:
        deps = [d for d in deps if d[0] is not E]
    E.wait(deps)
    inst = None
    for fn in fns:
        inst = fn(E.eng)
    E.n += 1
    inst.then_inc(E.sem, 1)
    _commit((E, E.n), reads, writes)


def op(E, fn, reads=(), writes=()):
    grp(E, [fn], reads, writes)


def dma(Q, Dm, out, in_, reads=(), writes=()):
    Q.wait(_deps(reads, writes))
    Q.eng.dma_start(out=out, in_=in_).then_inc(Dm.sem, 16)
    Dm.n += 16
    _commit((Dm, Dm.n), reads, writes)


class TileCtx:
    pass


class Stream:
    pass


def build_program():
    nc = bass.Bass("TRN2", target_bir_lowering=False)
    depth = CFG["depth"]
    n_pass = CFG["n_pass"]

    def din(name, shape):
        return nc.dram_tensor(name, list(shape), F32, kind="ExternalInput").ap()

    def dout(name, shape):
        return nc.dram_tensor(name, list(shape), F32, kind="ExternalOutput").ap()

    xp = din("xp", [SEQ, D]); xs = din("xs", [DEC, D])
    pp = din("pp", [DEPTH, SEQ, PLE]); psm = din("psm", [DEPTH, DEC, PLE])
    cc = din("cc", [DEPTH, HC, DC]); cpl = din("cpl", [DEPTH, HP, DC])
    w_in = din("w_in", [DEPTH, D, 3 * DC]); conv_w = din("conv_w", [DEPTH, CW, DC])
    conv_b = din("conv_b", [DEPTH, DC]); ln_g = din("ln_g", [DEPTH, DC]); ln_b = din("ln_b", [DEPTH, DC])
    pool_w = din("pool_w", [DEPTH, 4, 128, 128]); pool_scale = din("pool_scale", [DEPTH, DC])
    w_out = din("w_out", [DEPTH, D, D]); g_mix = din("g_mix", [DEPTH, D]); g_ffn = din("g_ffn", [DEPTH, D])
    g_ple = din("g_ple", [DEPTH, D]); w_ff1 = din("w_ff1", [DEPTH, D, DFF]); w_ff2 = din("w_ff2", [DEPTH, DFF, D])
    w_ple = din("w_ple", [DEPTH, PLE, D]); w_gate = din("w_gate", [DEPTH, D, D]); g_final = din("g_final", [D])
    yp = dout("yp", [SEQ, D]); ys = dout("ys", [DEC, D])
    ncp = dout("ncp", [DEPTH, HC, DC]); npp = dout("npp", [DEPTH, HP, DC])
    ncs = dout("ncs", [DEPTH, HC, DC]); nps = dout("nps", [DEPTH, HP, DC])

    with contextlib.ExitStack() as st:
        def sb(name, shape, dt):
            return st.enter_context(nc.sbuf_tensor(name, list(shape), dt))

        PE = Eng(nc, st, nc.tensor, "pe"); ACT = Eng(nc, st, nc.scalar, "act")
        DVE = Eng(nc, st, nc.vector, "dve"); POOL = Eng(nc, st, nc.gpsimd, "pool")
        SP = Eng(nc, st, nc.sync, "sp")

        ident = sb("ident", [128, 128], F32); identB = Buf()
        ones_d = sb("ones_d", [128, 128], BF16)
        ones_c = sb("ones_c", [128, 128], BF16)
        epsb = sb("epsb", [128, 1], F32)
        constB = Buf()
        invcnt = sb("invcnt", [128, 4, 16], F32)
        NV = 3 * KD + 4 * 4
        C_GMIX, C_GFFN, C_GPLE, C_CB, C_LG, C_LB, C_PS = 0, 8, 16, 24, 28, 32, 36
        C_GFIN = DEPTH * NV
        C_CW = C_GFIN + KD
        NCOL = C_CW + DEPTH * CW * 4
        tab = sb("tab", [128, NCOL], F32); tabB = Buf()

        def tcol(l, base, k):
            c = l * NV + base + k
            return tab[:, c:c + 1]

        def cwcol(l, j, m):
            c = C_CW + l * CW * 4 + j * 4 + m
            return tab[:, c:c + 1]

        banks = [st.enter_context(nc.psum_tensor("bank%d" % i, [128, 512], F32)) for i in range(8)]
        bankB = [Buf() for _ in range(8)]
        bank_i = [0]

        def next_bank():
            i = bank_i[0]
            bank_i[0] = (i + 1) % 8
            return banks[i], bankB[i]

        def pool_of(name, n, shape, dt, with_sem=False):
            aps = [sb("%s%d" % (name, i), shape, dt) for i in range(n)]
            bufs = [Buf() for _ in range(n)]
            sems = [SemObj(nc, st, "%sd%d" % (name, i)) for i in range(n)] if with_sem else None
            idx = [0]

            def nxt():
                i = idx[0]
                idx[0] = (i + 1) % n
                if with_sem:
                    return aps[i], bufs[i], sems[i]
                return aps[i], bufs[i]
            nxt.sems = sems
            return nxt

        lnsc = sb("lnsc", [128, 4, TT], BF16); lnscB = Buf()
        diag = [sb("diag%d" % i, [128, CW, 128], BF16) for i in range(2)]; diagB = [Buf(), Buf()]; diag_i = [0]
        new_stat3 = pool_of("stat", 6, [128, TT], F32, with_sem=True)

        def new_stat():
            a, b, _ = new_stat3()
            return a, b
        new_sig = pool_of("sig", 3, [128, TT], F32)
        new_relu = pool_of("relu", 3, [128, TT], BF16)
        new_gtmp = pool_of("gtmp", 2, [128, TT], F32)
        psA = sb("psA", [128, HP + TT], F32); psB_ = sb("psB", [128, HP + TT], F32)
        psAB = Buf(); psBB = Buf()
        tail32 = sb("tail32", [128, 4, HC], F32); tail32B = Buf()
        hist_c = sb("hist_c", [128, 2, DEPTH, 4, HC], BF16); hist_cB = [[Buf() for _ in range(DEPTH)] for _ in range(2)]
        hist_u = sb("hist_u", [128, 2, DEPTH, 4, HP], F32); hist_uB = [[Buf() for _ in range(DEPTH)] for _ in range(2)]

        def make_tile(name, T):
            t = TileCtx()
            t.T = T
            t.x = sb(name + "_x", [128, KD, T], F32); t.xB = [Buf() for _ in range(KD)]
            t.h = sb(name + "_h", [128, KD, T], BF16); t.hB = [Buf() for _ in range(KD)]
            t.cbf = sb(name + "_c", [128, 4, HC + T], BF16); t.cB = [Buf() for _ in range(4)]
            t.u = sb(name + "_u", [128, 4, HP + T], F32); t.uB = [Buf() for _ in range(4)]
            t.cv = sb(name + "_cv", [128, 4, T], F32)
            t.hid = t.cv.bitcast(BF16).reshape([128, KD, T])
            t.hidB = [Buf() for _ in range(KD)]
            t.xstD = [SemObj(nc, st, name + "_xd%d" % i) for i in range(2)]
            t.pstD = [SemObj(nc, st, name + "_pd%d" % i) for i in range(2)]
            if T == TT:
                t.xst = [t.cv[:, 2 * b:2 * b + 2, :].rearrange("p a t -> p (a t)") for b in range(2)]
                t.xstB = [t.hidB[0:4], t.hidB[4:8]]
                t.pst = [t.u[:, a, 0:512].rearrange("p (b f) -> p b f", f=PLE) for a in range(2)]
                t.pstB = t.uB[0:2]
            else:
                t.xst_t = sb(name + "_xst", [128, D], F32)
                t.xst = [t.xst_t[:, :], t.xst_t[:, :]]
                b_ = Buf(); t.xstB = [[b_], [b_]]
                t.xstD = [t.xstD[0], t.xstD[0]]
                t.pst_t = sb(name + "_pst", [128, 1, PLE], F32)
                t.pst = [t.pst_t[:, :, :]]
                t.pstB = [Buf()]
            return t

        tA = make_tile("tA", TT); tB = make_tile("tB", TT)
        tS = make_tile("tS", DEC) if CFG["do_sample"] else None

        streams = []
        for sid in range(1):
            S = Stream()
            S.sid = sid
            S.ring = [sb("ring%d_%d" % (sid, i), [128, SLOT], BF16) for i in range(NSLOT)]
            S.ringB = [Buf() for _ in range(NSLOT)]
            S.ringD = [SemObj(nc, st, "ringd%d_%d" % (sid, i)) for i in range(NSLOT)]
            S.issued = 0
            streams.append(S)

        setupD = SemObj(nc, st, "setupd")
        op(POOL, lambda e: e.memset(ident[:], 0.0), writes=[identB])
        op(POOL, lambda e: e.affine_select(out=ident[:], in_=ident[:], compare_op=ALU.not_equal, fill=1.0,
                                           base=0, pattern=[[-1, 128]], channel_multiplier=1),
           reads=[identB], writes=[identB])
        grp(POOL, [lambda e: e.memset(ones_d[:], 1.0 / D), lambda e: e.memset(ones_c[:], 1.0 / DC),
                   lambda e: e.memset(epsb[:], EPS)], writes=[constB])
        fns = []
        for g, w in enumerate(POOL_W):
            for pos in range(16):
                v = 1.0 / min(w, pos + 1)
                fns.append(lambda e, g=g, pos=pos, v=v: e.memset(invcnt[:, g, pos:pos + 1], v))
        grp(POOL, fns, writes=[constB])

        stg = tA.x
        stgB = Buf()
        vecs = [(g_mix, C_GMIX, KD), (g_ffn, C_GFFN, KD), (g_ple, C_GPLE, KD), (conv_b, C_CB, 4),
                (ln_g, C_LG, 4), (ln_b, C_LB, 4), (pool_scale, C_PS, 4)]
        for l in range(DEPTH):
            ti, r0 = l // 2, (l % 2) * NV
            for (v, base, nk) in vecs:
                src = v[l].rearrange("(k p) -> k p", p=128)
                dma(SP, setupD, stg[r0 + base:r0 + base + nk, ti, 0:128], src, writes=[stgB])
        dma(SP, setupD, stg[2 * NV:2 * NV + KD, 1, 0:128], g_final.rearrange("(k p) -> k p", p=128), writes=[stgB])
        for l in range(DEPTH):
            dma(SP, setupD, stg[0:CW * 4, 2 + l, 0:128], conv_w[l].rearrange("j (m p) -> (j m) p", p=128), writes=[stgB])
        stgB.w = (setupD, setupD.n)
        tr_plan = [(0, 2 * NV, 0), (1, 2 * NV + KD, 2 * NV)] + [(2 + l, CW * 4, C_CW + l * CW * 4) for l in range(DEPTH)]
        for (ti, rows, col0) in tr_plan:
            bk, bkB = next_bank()
            grp(PE, [lambda e, ti=ti, rows=rows, bk=bk: e.transpose(bk[:, 0:rows], stg[0:rows, ti, 0:128], ident[0:rows, 0:rows])],
                reads=[stgB, identB], writes=[bkB])
            op(ACT, lambda e, rows=rows, col0=col0, bk=bk: e.activation(out=tab[:, col0:col0 + rows], in_=bk[:, 0:rows], func=AF.Copy),
               reads=[bkB], writes=[tabB])
        for E in (PE, ACT, DVE, POOL):
            E.wait([tabB.w, constB.w, identB.w])
        for b in (tabB, constB, identB):
            b.ro = True
        for k in range(KD):
            tA.xB[k].r = dict(stgB.r)
            tA.xB[k].w = stgB.w

        def wv(w2d):
            return w2d.rearrange("(k p) n -> p k n", p=128)

        def unit_srcs(l, kind, q):
            if kind == "in_ag":
                W = wv(w_in[l])
                return [(0, W[:, :, q * 256:(q + 1) * 256], KD, 256),
                        (KD * 256, W[:, :, DC + q * 256:DC + (q + 1) * 256], KD, 256)]
            if kind == "in_u":
                return [(0, wv(w_in[l])[:, :, 2 * DC:3 * DC], KD, DC),
                        (KD * DC, pool_w[l].rearrange("g c d -> c g d"), 4, 128)]
            if kind == "out":
                return [(0, wv(w_out[l])[:, :, q * 512:(q + 1) * 512], KD, 512)]
            if kind == "ff1":
                return [(0, wv(w_ff1[l])[:, :, q * 512:(q + 1) * 512], KD, 512)]
            if kind == "ff2":
                qq, hf = q // 2, q % 2
                return [(0, wv(w_ff2[l][qq * D:(qq + 1) * D, :])[:, :, hf * 512:(hf + 1) * 512], KD, 512)]
            if kind == "gp":
                return [(0, wv(w_gate[l])[:, :, q * 512:(q + 1) * 512], KD, 512),
                        (KD * 512, wv(w_ple[l])[:, :, q * 512:(q + 1) * 512], 2, 512)]
            raise ValueError(kind)

        layer_units = [("in_ag", 0), ("in_ag", 1), ("in_u", 0), ("out", 0), ("out", 1)]
        for qq in range(NQ):
            layer_units += [("ff1", 2 * qq), ("ff1", 2 * qq + 1), ("ff2", 2 * qq), ("ff2", 2 * qq + 1)]
        layer_units += [("gp", 0), ("gp", 1)]
        NU = len(layer_units)
        unit_index = {ku: i for i, ku in enumerate(layer_units)}
        n_units_total = n_pass * depth * NU

        def issue_loads(S, upto):
            upto = min(upto, n_units_total - 1)
            while S.issued <= upto:
                i = S.issued
                l = (i // NU) % depth
                kind, q = layer_units[i % NU]
                s = i % NSLOT
                for (off, src, nk, nn) in unit_srcs(l, kind, q):
                    dst = S.ring[s][:, off:off + nk * nn].rearrange("p (k n) -> p k n", k=nk)
                    half = max(1, nk // 2) if nk * nn >= 2048 else nk
                    for k0 in range(0, nk, half):
                        dma(POOL, S.ringD[s], dst[:, k0:k0 + half, :], src[:, k0:k0 + half, :], writes=[S.ringB[s]])
                S.issued += 1

        def get_unit(S, p, l, kind, q):
            i = (p * depth + l) * NU + unit_index[(kind, q)]
            issue_loads(S, i + NSLOT - 1)
            s = i % NSLOT
            return S.ring[s], S.ringB[s]

        def wview(slot, nk, nn, off=0):
            return slot[:, off:off + nk * nn].rearrange("p (k n) -> p k n", k=nk)

        def mmcost(n, T):
            return n * 0.25 if T == TT else ("s", n * 0.11)

        def mm_group(bk, bkB, T, pairs, reads):
            n = len(pairs)
            fns = []
            for i, (lhsT, rhs) in enumerate(pairs):
                fns.append(lambda e, lhsT=lhsT, rhs=rhs, i=i: e.matmul(bk[:, 0:T], lhsT=lhsT, rhs=rhs,
                                                                      start=(i == 0), stop=(i == n - 1)))
            grp(PE, fns, reads=reads, writes=[bkB])

        def rstd_from(src_ap, srcB, T, bias):
            s, sB = new_stat()
            if bias:
                op(ACT, lambda e: e.activation(out=s[:, 0:T], in_=src_ap, func=AF.Sqrt, bias=epsb[:, 0:1]),
                   reads=srcB, writes=[sB])
            else:
                op(ACT, lambda e: e.activation(out=s[:, 0:T], in_=src_ap, func=AF.Sqrt), reads=srcB, writes=[sB])
            r, rB = new_stat()
            op(DVE, lambda e: e.reciprocal(out=r[:, 0:T], in_=s[:, 0:T]), reads=[sB], writes=[rB])
            return r, rB

        def rmsnorm(t, T, gbase, l, out_f32=False):
            op(ACT, lambda e: e.activation(out=t.h[:, :, 0:T], in_=t.x[:, :, 0:T], func=AF.Square),
               reads=t.xB, writes=t.hB)
            bk, bkB = next_bank()
            mm_group(bk, bkB, T, [(ones_d[:], t.h[:, k, 0:T]) for k in range(KD)], reads=t.hB)
            r, rB = rstd_from(bk[:, 0:T], [bkB], T, True)
            fns = []
            for k in range(KD):
                gcol = tab[:, C_GFIN + k:C_GFIN + k + 1] if l is None else tcol(l, gbase, k)
                dst = t.x[:, k, 0:T] if out_f32 else t.h[:, k, 0:T]
                fns.append(lambda e, k=k, gcol=gcol, dst=dst: e.scalar_tensor_tensor(
                    out=dst, in0=t.x[:, k, 0:T], scalar=gcol, in1=r[:, 0:T], op0=ALU.mult, op1=ALU.mult))
            grp(DVE, fns, reads=[rB] + t.xB, writes=(t.xB if out_f32 else t.hB))
            return 12.0 if T == TT else ("s", 5.0)

        def load_x(t, T, src, tok0):
            nch = max(1, T // 128)
            rows = min(128, T)
            for tc in range(nch):
                b = tc % 2
                dma(SP, t.xstD[b], t.xst[b][0:rows, :], src[tok0 + tc * 128:tok0 + tc * 128 + rows, :], writes=t.xstB[b])
                for half in range(2):
                    bk, bkB = next_bank()
                    fns = []
                    for kk in range(4):
                        k = half * 4 + kk
                        fns.append(lambda e, k=k, kk=kk, b=b, bk=bk: e.transpose(
                            bk[:, kk * 128:kk * 128 + rows], t.xst[b][0:rows, k * 128:(k + 1) * 128], ident[0:rows, 0:rows]))
                    grp(PE, fns, reads=t.xstB[b], writes=[bkB])
                    src_v = bk[:, :].rearrange("p (a b) -> p a b", a=4)[:, :, 0:rows]
                    op(ACT, lambda e, half=half, tc=tc, src_v=src_v: e.activation(
                        out=t.x[:, half * 4:half * 4 + 4, tc * 128:tc * 128 + rows], in_=src_v, func=AF.Copy),
                       reads=[bkB], writes=t.xB[half * 4:half * 4 + 4])
                yield 2.5 if T == TT else ("s", 2.5)

        def store_y(t, T, dst, tok0):
            nch = max(1, T // 128)
            rows = min(128, T)
            for tc in range(nch):
                b = tc % 2
                for half in range(2):
                    bk, bkB = next_bank()
                    fns = []
                    for kk in range(4):
                        k = half * 4 + kk
                        fns.append(lambda e, k=k, kk=kk, bk=bk, tc=tc: e.transpose(
                            bk[0:rows, kk * 128:(kk + 1) * 128], t.x[:, k, tc * 128:tc * 128 + rows], ident[:, :]))
                    grp(PE, fns, reads=t.xB[half * 4:half * 4 + 4], writes=[bkB])
                    op(ACT, lambda e, half=half, b=b, bk=bk: e.activation(
                        out=t.xst[b][0:rows, half * 512:(half + 1) * 512], in_=bk[0:rows, :], func=AF.Copy),
                       reads=[bkB], writes=t.xstB[b])
                dma(SP, t.xstD[b], dst[tok0 + tc * 128:tok0 + tc * 128 + rows, :], t.xst[b][0:rows, :], reads=t.xstB[b])
                yield 2.5 if T == TT else ("s", 2.5)

        def load_p(t, T, src_l, tok0, pT, pTB):
            nch = max(1, T // 128)
            rows = min(128, T)
            if nch > 1:
                for a in range(2):
                    dma(SP, t.pstD[a], t.pst[a], src_l[tok0 + a * 256:tok0 + (a + 1) * 256, :].rearrange("(c p) f -> p c f", p=128),
                        writes=[t.pstB[a]])
                pch = [(t.pst[tc // 2][:, tc % 2, :], t.pstB[tc // 2]) for tc in range(nch)]
            else:
                dma(SP, t.pstD[0], t.pst[0][0:rows, 0, :], src_l[tok0:tok0 + rows, :], writes=[t.pstB[0]])
                pch = [(t.pst[0][:, 0, :], t.pstB[0])]
            for fc in range(2):
                bk, bkB = next_bank()
                fns = []
                for tc in range(nch):
                    fns.append(lambda e, tc=tc, fc=fc, bk=bk: e.transpose(
                        bk[:, tc * 128:tc * 128 + rows], pch[tc][0][0:rows, fc * 128:(fc + 1) * 128], ident[0:rows, 0:rows]))
                grp(PE, fns, reads=[b for (_, b) in pch], writes=[bkB])
                op(ACT, lambda e, fc=fc, bk=bk: e.activation(out=pT[:, fc, 0:T], in_=bk[:, 0:T], func=AF.Copy),
                   reads=[bkB], writes=pTB)

        def out_hist(src_ap3, srcB, n, dst):
            if CFG.get("dbg_no_out_hist"):
                return
            bk, bkB = next_bank()
            fns = []
            for m in range(4):
                fns.append(lambda e, m=m, bk=bk: e.transpose(bk[0:n, m * 128:(m + 1) * 128], src_ap3[:, m, :], ident[:, :]))
            grp(PE, fns, reads=srcB, writes=[bkB])
            hs, hsB, hsD = new_stat3()
            op(ACT, lambda e, bk=bk: e.activation(out=hs[0:n, :], in_=bk[0:n, :], func=AF.Copy), reads=[bkB], writes=[hsB])
            dma(SP, hsD, dst, hs[0:n, :], reads=[hsB])

        def load_hist(src, n, dst_ap, dstB):
            hs, hsB, hsD = new_stat3()
            dma(SP, hsD, hs[0:n, :], src, writes=[hsB])
            bk, bkB = next_bank()
            fns = []
            for m in range(4):
                fns.append(lambda e, m=m, bk=bk: e.transpose(bk[:, m * 32:m * 32 + n], hs[0:n, m * 128:(m + 1) * 128], ident[0:n, 0:n]))
            grp(PE, fns, reads=[hsB], writes=[bkB])
            src_v = bk[:, 0:128].rearrange("p (a b) -> p a b", a=4)[:, :, 0:n]
            op(ACT, lambda e: e.activation(out=dst_ap, in_=src_v, func=AF.Copy), reads=[bkB], writes=[dstB])

        hist_flag = set()

        def mixer_half(S, p, l, tiles):
            for (t, T, info) in tiles:
                yield rmsnorm(t, T, C_GMIX, l)
            hs = [None] * len(tiles)
            for q in range(2):
                w, wB = get_unit(S, p, l, "in_ag", q)
                Wa = wview(w, KD, 256); Wg = wview(w, KD, 256, off=KD * 256)
                for (t, T, info) in tiles:
                    hc, hcB = info["hc"][:, l, :, :], info["hcB"][l]
                    for mm in range(2):
                        m = 2 * q + mm
                        bka, bkaB = next_bank()
                        mm_group(bka, bkaB, T, [(Wa[:, k, mm * 128:(mm + 1) * 128], t.h[:, k, 0:T]) for k in range(KD)],
                                 reads=[wB] + t.hB)
                        bkg, bkgB = next_bank()
                        mm_group(bkg, bkgB, T, [(Wg[:, k, mm * 128:(mm + 1) * 128], t.h[:, k, 0:T]) for k in range(KD)],
                                 reads=[wB] + t.hB)
                        sg, sgB = new_sig()
                        op(ACT, lambda e, sg=sg, bkg=bkg, T=T: e.activation(out=sg[:, 0:T], in_=bkg[:, 0:T], func=AF.Sigmoid),
                           reads=[bkgB], writes=[sgB])
                        fns = [lambda e, t=t, m=m, bka=bka, sg=sg, T=T: e.tensor_tensor(
                            out=t.cbf[:, m, HC:HC + T], in0=bka[:, 0:T], in1=sg[:, 0:T], op=ALU.mult)]
                        wr = [t.cB[m]]
                        if info["last"]:
                            fns.append(lambda e, m=m, bka=bka, sg=sg, T=T: e.tensor_tensor(
                                out=tail32[:, m, :], in0=bka[:, T - HC:T], in1=sg[:, T - HC:T], op=ALU.mult))
                            wr = wr + [tail32B]
                        grp(DVE, fns, reads=[bkaB, sgB], writes=wr)
                        yield mmcost(16, T)
                    if q == 1:
                        op(ACT, lambda e, t=t, hc=hc: e.activation(out=t.cbf[:, :, 0:HC], in_=hc, func=AF.Copy),
                           reads=[hcB], writes=t.cB)
                        if info["last"]:
                            out_hist(tail32, [tail32B], HC, info["nc_out"][l])
                        else:
                            op(ACT, lambda e, t=t, T=T, hc=hc: e.activation(out=hc, in_=t.cbf[:, :, T:T + HC], func=AF.Copy),
                               reads=t.cB, writes=[hcB])
            w, wB = get_unit(S, p, l, "in_u", 0)
            Wu = wview(w, KD, DC); Wp = wview(w, 4, 128, off=KD * DC)
            for (t, T, info) in tiles:
                hu, huB = info["hu"][:, l, :, :], info["huB"][l]
                op(ACT, lambda e, t=t, hu=hu: e.activation(out=t.u[:, :, 0:HP], in_=hu, func=AF.Copy),
                   reads=[huB], writes=t.uB)
                for m in range(4):
                    bk, bkB = next_bank()
                    mm_group(bk, bkB, T, [(Wu[:, k, m * 128:(m + 1) * 128], t.h[:, k, 0:T]) for k in range(KD)],
                             reads=[wB] + t.hB)
                    op(ACT, lambda e, t=t, m=m, bk=bk, T=T: e.activation(out=t.u[:, m, HP:HP + T], in_=bk[:, 0:T], func=AF.Copy),
                       reads=[bkB], writes=[t.uB[m]])
                    yield mmcost(8, T)
                op(ACT, lambda e, t=t, T=T, hu=hu: e.activation(out=hu, in_=t.u[:, :, T:T + HP], func=AF.Copy),
                   reads=t.uB, writes=[huB])
                if info["last"]:
                    out_hist(hu, [huB], HP, info["np_out"][l])
            for (t, T, info) in tiles:
                L = HP + T
                for g in range(4):
                    ug = t.u[:, g, :]
                    steps = [(psA, psAB, 1)]
                    if g >= 1:
                        steps.append((psB_, psBB, 2))
                    if g >= 2:
                        steps.append((psA, psAB, 4))
                    if g >= 3:
                        steps.append((psB_, psBB, 8))
                    src, srcB, lo = ug, t.uB[g], 0
                    for (dst, dstB, sh) in steps:
                        nlo = lo + sh
                        op(DVE, lambda e, dst=dst, src=src, nlo=nlo, sh=sh, L=L: e.tensor_tensor(
                            out=dst[:, nlo:L], in0=src[:, nlo:L], in1=src[:, nlo - sh:L - sh], op=ALU.add),
                           reads=[srcB], writes=[dstB])
                        src, srcB, lo = dst, dstB, nlo
                    w_ = POOL_W[g]
                    op(DVE, lambda e, src=src, ug=ug, w_=w_, t=t, g=g, T=T: e.scalar_tensor_tensor(
                        out=t.h[:, g, 0:T], in0=src[:, HP:HP + T], scalar=1.0 / w_, in1=ug[:, HP:HP + T],
                        op0=ALU.mult, op1=ALU.subtract), reads=[srcB, t.uB[g]], writes=[t.hB[g]])
                    if info["first"]:
                        nfix = w_ - 1
                        fx, fxB = new_stat()
                        op(DVE, lambda e, src=src, g=g, nfix=nfix, fx=fx: e.tensor_tensor(
                            out=fx[:, 0:nfix], in0=src[:, HP:HP + nfix], in1=invcnt[:, g, 0:nfix], op=ALU.mult),
                           reads=[srcB], writes=[fxB])
                        op(DVE, lambda e, ug=ug, t=t, g=g, nfix=nfix, fx=fx: e.tensor_tensor(
                            out=t.h[:, g, 0:nfix], in0=fx[:, 0:nfix], in1=ug[:, HP:HP + nfix], op=ALU.subtract),
                           reads=[fxB, t.uB[g]], writes=[t.hB[g]])
                    bk, bkB = next_bank()
                    mm_group(bk, bkB, T, [(Wp[:, g, :], t.h[:, g, 0:T])], reads=[wB, t.hB[g]])
                    op(ACT, lambda e, t=t, g=g, bk=bk, T=T: e.activation(out=t.h[:, 4 + g, 0:T], in_=bk[:, 0:T], func=AF.Copy,
                                                                        scale=tcol(l, C_PS, g)),
                       reads=[bkB], writes=[t.hB[4 + g]])
                    yield (0.65 * (len(steps) + 1)) if T == TT else ("s", 0.23 * (len(steps) + 1))
            def build_diag(m):
                b = diag_i[0]; diag_i[0] ^= 1
                fns = []
                for j in range(CW):
                    fns.append(lambda e, j=j, b=b: e.tensor_scalar_mul(out=diag[b][:, j, :], in0=ident[:, :], scalar1=cwcol(l, j, m)))
                grp(DVE, fns, writes=[diagB[b]])
                return b

            def conv_chunk(t, T, m):
                db = build_diag(m)
                bk, bkB = next_bank()
                mm_group(bk, bkB, T, [(diag[db][:, j, :], t.cbf[:, m, j:j + T]) for j in range(CW)],
                         reads=[diagB[db], t.cB[m]])
                op(ACT, lambda e: e.activation(out=t.cv[:, m, 0:T], in_=bk[:, 0:T], func=AF.Identity, bias=tcol(l, C_CB, m)),
                   reads=[bkB], writes=[t.hidB[2 * m], t.hidB[2 * m + 1]])

            def ln_pre(t, T):
                op(ACT, lambda e: e.activation(out=lnsc[:, :, 0:T], in_=t.cv[:, :, 0:T], func=AF.Copy),
                   reads=t.hidB, writes=[lnscB])
                op(ACT, lambda e: e.activation(out=t.h[:, 0:4, 0:T], in_=t.cv[:, :, 0:T], func=AF.Square),
                   reads=t.hidB, writes=t.hB[0:4])

            def ln_post(t, T):
                bkm, bkmB = next_bank()
                mm_group(bkm, bkmB, T, [(ones_c[:], lnsc[:, m, 0:T]) for m in range(4)], reads=[lnscB])
                bkq, bkqB = next_bank()
                mm_group(bkq, bkqB, T, [(ones_c[:], t.h[:, m, 0:T]) for m in range(4)], reads=t.hB[0:4])
                mean, meanB = new_stat()
                var, varB = new_stat()
                op(ACT, lambda e: e.activation(out=mean[:, 0:T], in_=bkm[:, 0:T], func=AF.Copy), reads=[bkmB], writes=[meanB])
                op(DVE, lambda e: e.tensor_tensor(out=var[:, 0:T], in0=mean[:, 0:T], in1=mean[:, 0:T], op=ALU.mult),
                   reads=[meanB], writes=[varB])
                op(DVE, lambda e: e.tensor_tensor(out=var[:, 0:T], in0=bkq[:, 0:T], in1=var[:, 0:T], op=ALU.subtract),
                   reads=[bkqB, varB], writes=[varB])
                op(DVE, lambda e: e.tensor_scalar(out=var[:, 0:T], in0=var[:, 0:T], scalar1=0.0, scalar2=EPS, op0=ALU.max, op1=ALU.add),
                   reads=[varB], writes=[varB])
                r, rB = rstd_from(var[:, 0:T], [varB], T, False)
                op(DVE, lambda e: e.scalar_tensor_tensor(out=mean[:, 0:T], in0=mean[:, 0:T], scalar=-1.0, in1=r[:, 0:T],
                                                        op0=ALU.mult, op1=ALU.mult), reads=[meanB, rB], writes=[meanB])
                for m in range(4):
                    hb = [t.hidB[2 * m], t.hidB[2 * m + 1]]
                    op(DVE, lambda e, m=m: e.tensor_tensor(out=t.cv[:, m, 0:T], in0=t.cv[:, m, 0:T], in1=r[:, 0:T], op=ALU.mult),
                       reads=hb + [rB], writes=hb)
                    op(DVE, lambda e, m=m: e.tensor_tensor(out=t.cv[:, m, 0:T], in0=t.cv[:, m, 0:T], in1=mean[:, 0:T], op=ALU.add),
                       reads=hb + [meanB], writes=hb)
                    op(ACT, lambda e, m=m: e.activation(out=t.h[:, m, 0:T], in_=t.cv[:, m, 0:T], func=AF.Silu,
                                                        bias=tcol(l, C_LB, m), scale=tcol(l, C_LG, m)), reads=hb, writes=[t.hB[m]])

            prev = None
            for (t, T, info) in tiles:
                for m in range(4):
                    conv_chunk(t, T, m)
                    if m == 0 and prev is not None:
                        ln_post(*prev)
                    yield 8.0
                if prev is not None:
                    pass
                ln_pre(t, T)
                prev = (t, T)
            ln_post(*prev)
            yield 12.0
            for q in range(2):
                w, wB = get_unit(S, p, l, "out", q)
                W = wview(w, KD, 512)
                for (t, T, info) in tiles:
                    for mm in range(4):
                        m = 4 * q + mm
                        bk, bkB = next_bank()
                        mm_group(bk, bkB, T, [(W[:, k, mm * 128:(mm + 1) * 128], t.h[:, k, 0:T]) for k in range(KD)],
                                 reads=[wB] + t.hB)
                        op(DVE, lambda e, t=t, m=m, bk=bk, T=T: e.tensor_tensor(out=t.x[:, m, 0:T], in0=bk[:, 0:T], in1=t.x[:, m, 0:T], op=ALU.add),
                           reads=[bkB, t.xB[m]], writes=[t.xB[m]])
                        yield mmcost(8, T)

        def ffn_half(S, p, l, tiles):
            for (t, T, info) in tiles:
                yield rmsnorm(t, T, C_GFFN, l)
            for qq in range(NQ):
                for hf in range(2):
                    w, wB = get_unit(S, p, l, "ff1", 2 * qq + hf)
                    W1 = wview(w, KD, 512)
                    for (t, T, info) in tiles:
                        for mm in range(4):
                            m = 4 * hf + mm
                            bk, bkB = next_bank()
                            mm_group(bk, bkB, T, [(W1[:, k, mm * 128:(mm + 1) * 128], t.h[:, k, 0:T]) for k in range(KD)],
                                     reads=[wB] + t.hB)
                            rl, rlB = new_relu()
                            op(ACT, lambda e, rl=rl, bk=bk, T=T: e.activation(out=rl[:, 0:T], in_=bk[:, 0:T], func=AF.Relu),
                               reads=[bkB], writes=[rlB])
                            op(ACT, lambda e, t=t, m=m, rl=rl, T=T: e.activation(out=t.hid[:, m, 0:T], in_=rl[:, 0:T], func=AF.Square),
                               reads=[rlB], writes=[t.hidB[m]])
                            yield mmcost(8, T)
                for hf in range(2):
                    w, wB = get_unit(S, p, l, "ff2", 2 * qq + hf)
                    W2 = wview(w, KD, 512)
                    for (t, T, info) in tiles:
                        for mm in range(4):
                            m = 4 * hf + mm
                            bk, bkB = next_bank()
                            mm_group(bk, bkB, T, [(W2[:, k, mm * 128:(mm + 1) * 128], t.hid[:, k, 0:T]) for k in range(KD)],
                                     reads=[wB] + t.hidB)
                            op(DVE, lambda e, t=t, m=m, bk=bk, T=T: e.tensor_tensor(out=t.x[:, m, 0:T], in0=bk[:, 0:T], in1=t.x[:, m, 0:T], op=ALU.add),
                               reads=[bkB, t.xB[m]], writes=[t.xB[m]])
                            yield mmcost(8, T)
            for (t, T, info) in tiles:
                c = rmsnorm(t, T, C_GPLE, l)
                load_p(t, T, info["psrc"][l], info["tok0"], t.cbf, t.cB[0:2])
                yield (c + 2.0) if T == TT else ("s", 7.0)
            for q in range(2):
                w, wB = get_unit(S, p, l, "gp", q)
                Wg = wview(w, KD, 512); Wpl = wview(w, 2, 512, off=KD * 512)
                for (t, T, info) in tiles:
                    for mm in range(4):
                        m = 4 * q + mm
                        bkg, bkgB = next_bank()
                        mm_group(bkg, bkgB, T, [(Wg[:, k, mm * 128:(mm + 1) * 128], t.h[:, k, 0:T]) for k in range(KD)],
                                 reads=[wB] + t.hB)
                        bkp, bkpB = next_bank()
                        mm_group(bkp, bkpB, T, [(Wpl[:, kc, mm * 128:(mm + 1) * 128], t.cbf[:, kc, 0:T]) for kc in range(2)],
                                 reads=[wB] + t.cB[0:2])
                        sg, sgB = new_sig()
                        op(ACT, lambda e, sg=sg, bkg=bkg, T=T: e.activation(out=sg[:, 0:T], in_=bkg[:, 0:T], func=AF.Sigmoid),
                           reads=[bkgB], writes=[sgB])
                        gt, gtB = new_gtmp()
                        op(DVE, lambda e, gt=gt, sg=sg, bkp=bkp, T=T: e.tensor_tensor(out=gt[:, 0:T], in0=bkp[:, 0:T], in1=sg[:, 0:T], op=ALU.mult),
                           reads=[bkpB, sgB], writes=[gtB])
                        op(DVE, lambda e, t=t, m=m, gt=gt, T=T: e.tensor_tensor(out=t.x[:, m, 0:T], in0=gt[:, 0:T], in1=t.x[:, m, 0:T], op=ALU.add),
                           reads=[gtB, t.xB[m]], writes=[t.xB[m]])
                        yield mmcost(10, T)

        def stream_gen(S):
            for p in range(n_pass):
                tiles = []
                for ti, tile in enumerate((tA, tB)):
                    tok0 = p * 2 * TT + ti * TT
                    info = {"first": tok0 == 0, "last": tok0 + TT == SEQ, "nc_out": ncp, "np_out": npp,
                            "tok0": tok0, "xsrc": xp, "psrc": pp, "ydst": yp,
                            "hc": hist_c[:, 0], "hcB": hist_cB[0], "hu": hist_u[:, 0], "huB": hist_uB[0]}
                    tiles.append((tile, TT, info))
                if p == 0:
                    for l in range(depth):
                        op(DVE, lambda e, l=l: e.memset(hist_c[:, 0, l, :, :], 0.0), writes=[hist_cB[0][l]])
                        op(DVE, lambda e, l=l: e.memset(hist_u[:, 0, l, :, :], 0.0), writes=[hist_uB[0][l]])
                    if tS is not None:
                        infoS = {"first": False, "last": True, "nc_out": ncs, "np_out": nps,
                                 "tok0": 0, "xsrc": xs, "psrc": psm, "ydst": ys,
                                 "hc": hist_c[:, 1], "hcB": hist_cB[1], "hu": hist_u[:, 1], "huB": hist_uB[1]}
                        tiles.append((tS, DEC, infoS))
                        for l in range(depth):
                            load_hist(cc[l], HC, hist_c[:, 1, l, :, :], hist_cB[1][l])
                            load_hist(cpl[l], HP, hist_u[:, 1, l, :, :], hist_uB[1][l])
                for (t, T, inf) in tiles:
                    yield from load_x(t, T, inf["xsrc"], inf["tok0"])
                for l in range(depth):
                    yield from mixer_half(S, p, l, tiles)
                    yield from ffn_half(S, p, l, tiles)
                for (t, T, inf) in tiles:
                    rmsnorm(t, T, None, None, out_f32=True)
                for (t, T, inf) in tiles:
                    yield from store_y(t, T, inf["ydst"], inf["tok0"])

        for _ in stream_gen(streams[0]):
            pass

        fin = [(d_, d_.n) for d_ in new_stat3.sems]
        for t_ in (tA, tB, tS):
            if t_ is not None:
                fin += [(d_, d_.n) for d_ in t_.xstD + t_.pstD]
        SP.wait(fin)
    return nc


_PROGRAM = [None]


def kernel(x_prompt, x_sample, p_prompt, p_sample, cache_conv, cache_pool, w_in, conv_w, conv_b,
           ln_g, ln_b, pool_w, pool_scale, w_out, g_mix, g_ffn, g_ple, w_ff1, w_ff2, w_ple, w_gate, g_final):
    f = lambda a: np.ascontiguousarray(np.asarray(a, dtype=np.float32))
    shared = {"w_in": f(w_in), "conv_w": f(conv_w), "conv_b": f(conv_b), "ln_g": f(ln_g), "ln_b": f(ln_b),
              "pool_w": f(pool_w), "pool_scale": f(pool_scale), "w_out": f(w_out), "g_mix": f(g_mix),
              "g_ffn": f(g_ffn), "g_ple": f(g_ple), "w_ff1": f(w_ff1), "w_ff2": f(w_ff2), "w_ple": f(w_ple),
              "w_gate": f(w_gate), "g_final": f(g_final)}
    in_maps = []
    for i in range(NCORES):
        m = dict(shared)
        m["xp"] = f(x_prompt[i]); m["xs"] = f(x_sample[i])
        m["pp"] = f(p_prompt[:, i]); m["psm"] = f(p_sample[:, i])
        m["cc"] = f(cache_conv[:, i]); m["cpl"] = f(cache_pool[:, i])
        in_maps.append(m)
    nc = build_program()
    res = run_bass_kernel_spmd(nc, in_maps, core_ids=list(range(NCORES)))
    r = res.results
    y_prompt = np.stack([r[i]["yp"] for i in range(NCORES)], 0)
    y_sample = np.stack([r[i]["ys"] for i in range(NCORES)], 0)
    ncp = np.stack([r[i]["ncp"] for i in range(NCORES)], 1)
    npp = np.stack([r[i]["npp"] for i in range(NCORES)], 1)
    ncs = np.stack([r[i]["ncs"] for i in range(NCORES)], 1)
    nps = np.stack([r[i]["nps"] for i in range(NCORES)], 1)
    return (y_prompt.astype(np.float32), y_sample.astype(np.float32), ncp.astype(np.float32),
            npp.astype(np.float32), ncs.astype(np.float32), nps.astype(np.float32))
```

```python
import contextlib
import numpy as np
import concourse.bass as bass
import concourse.mybir as mybir
from concourse.bass_utils import run_bass_kernel_spmd

F32 = mybir.dt.float32
BF16 = mybir.dt.bfloat16
ALU = mybir.AluOpType
AF = mybir.ActivationFunctionType

NCORES = 8
D = 1024
KD = D // 128
SEQ = 4096
DEC = 64
DEPTH = 4
DC = 512
CW = 31
HC = CW - 1
HP = 15
PLE = 256
DFF = 4096
EPS = 1e-6
TT = 512
NQ = 4
SLOT = 5 * 1024
NSLOT = 3
POOL_W = (2, 4, 8, 16)

CFG = {"n_pass": 4, "do_sample": True, "depth": DEPTH}


class SemObj:
    def __init__(self, nc, stack, name):
        self.sem = stack.enter_context(nc.semaphore(name))
        self.n = 0


class Eng(SemObj):
    def __init__(self, nc, stack, eng, name):
        super().__init__(nc, stack, "s_" + name)
        self.eng = eng
        self.seen = {}

    def wait(self, deps):
        best = {}
        for (S, c) in deps:
            if c > best.get(S, 0):
                best[S] = c
        for S, c in best.items():
            if c > self.seen.get(S, 0):
                self.eng.wait_ge(S.sem, c)
                self.seen[S] = c


class Buf:
    __slots__ = ("w", "r", "ro")

    def __init__(self):
        self.w = None
        self.r = {}
        self.ro = False


def _deps(reads, writes):
    deps = []
    for b in reads:
        if b.w is not None:
            deps.append(b.w)
    for b in writes:
        if b.w is not None:
            deps.append(b.w)
        deps.extend(b.r.items())
    return deps


def _commit(tok, reads, writes):
    S, c = tok
    for b in reads:
        if not b.ro:
            b.r[S] = c
    for b in writes:
        b.w = tok
        b.r = {}


def grp(E, fns, reads=(), writes=(), nosame=False):
    deps = _deps(reads, writes)
    if nosame:
        deps = [d for d in deps if d[0] is not E]
    E.wait(deps)
    inst = None
    for fn in fns:
        inst = fn(E.eng)
    E.n += 1
    inst.then_inc(E.sem, 1)
    _commit((E, E.n), reads, writes)


def op(E, fn, reads=(), writes=()):
    grp(E, [fn], reads, writes)


def dma(Q, Dm, out, in_, reads=(), writes=()):
    Q.wait(_deps(reads, writes))
    Q.eng.dma_start(out=out, in_=in_).then_inc(Dm.sem, 16)
    Dm.n += 16
    _commit((Dm, Dm.n), reads, writes)


class TileCtx:
    pass


class Stream:
    pass


def build_program():
    nc = bass.Bass("TRN2", target_bir_lowering=False)
    depth = CFG["depth"]
    n_pass = CFG["n_pass"]

    def din(name, shape):
        return nc.dram_tensor(name, list(shape), F32, kind="ExternalInput").ap()

    def dout(name, shape):
        return nc.dram_tensor(name, list(shape), F32, kind="ExternalOutput").ap()

    xp = din("xp", [SEQ, D]); xs = din("xs", [DEC, D])
    pp = din("pp", [DEPTH, SEQ, PLE]); psm = din("psm", [DEPTH, DEC, PLE])
    cc = din("cc", [DEPTH, HC, DC]); cpl = din("cpl", [DEPTH, HP, DC])
    w_in = din("w_in", [DEPTH, D, 3 * DC]); conv_w = din("conv_w", [DEPTH, CW, DC])
    conv_b = din("conv_b", [DEPTH, DC]); ln_g = din("ln_g", [DEPTH, DC]); ln_b = din("ln_b", [DEPTH, DC])
    pool_w = din("pool_w", [DEPTH, 4, 128, 128]); pool_scale = din("pool_scale", [DEPTH, DC])
    w_out = din("w_out", [DEPTH, D, D]); g_mix = din("g_mix", [DEPTH, D]); g_ffn = din("g_ffn", [DEPTH, D])
    g_ple = din("g_ple", [DEPTH, D]); w_ff1 = din("w_ff1", [DEPTH, D, DFF]); w_ff2 = din("w_ff2", [DEPTH, DFF, D])
    w_ple = din("w_ple", [DEPTH, PLE, D]); w_gate = din("w_gate", [DEPTH, D, D]); g_final = din("g_final", [D])
    yp = dout("yp", [SEQ, D]); ys = dout("ys", [DEC, D])
    ncp = dout("ncp", [DEPTH, HC, DC]); npp = dout("npp", [DEPTH, HP, DC])
    ncs = dout("ncs", [DEPTH, HC, DC]); nps = dout("nps", [DEPTH, HP, DC])

    with contextlib.ExitStack() as st:
        def sb(name, shape, dt):
            return st.enter_context(nc.sbuf_tensor(name, list(shape), dt))

        PE = Eng(nc, st, nc.tensor, "pe"); ACT = Eng(nc, st, nc.scalar, "act")
        DVE = Eng(nc, st, nc.vector, "dve"); POOL = Eng(nc, st, nc.gpsimd, "pool")
        SP = Eng(nc, st, nc.sync, "sp")

        ident = sb("ident", [128, 128], F32); identB = Buf()
        ones_d = sb("ones_d", [128, 128], BF16)
        ones_c = sb("ones_c", [128, 128], BF16)
        epsb = sb("epsb", [128, 1], F32)
        constB = Buf()
        invcnt = sb("invcnt", [128, 4, 16], F32)
        NV = 3 * KD + 4 * 4
        C_GMIX, C_GFFN, C_GPLE, C_CB, C_LG, C_LB, C_PS = 0, 8, 16, 24, 28, 32, 36
        C_GFIN = DEPTH * NV
        C_CW = C_GFIN + KD
        NCOL = C_CW + DEPTH * CW * 4
        tab = sb("tab", [128, NCOL], F32); tabB = Buf()

        def tcol(l, base, k):
            c = l * NV + base + k
            return tab[:, c:c + 1]

        def cwcol(l, j, m):
            c = C_CW + l * CW * 4 + j * 4 + m
            return tab[:, c:c + 1]

        banks = [st.enter_context(nc.psum_tensor("bank%d" % i, [128, 512], F32)) for i in range(8)]
        bankB = [Buf() for _ in range(8)]
        bank_i = [0]

        def next_bank():
            i = bank_i[0]
            bank_i[0] = (i + 1) % 8
            return banks[i], bankB[i]

        def pool_of(name, n, shape, dt, with_sem=False):
            aps = [sb("%s%d" % (name, i), shape, dt) for i in range(n)]
            bufs = [Buf() for _ in range(n)]
            sems = [SemObj(nc, st, "%sd%d" % (name, i)) for i in range(n)] if with_sem else None
            idx = [0]

            def nxt():
                i = idx[0]
                idx[0] = (i + 1) % n
                if with_sem:
                    return aps[i], bufs[i], sems[i]
                return aps[i], bufs[i]
            nxt.sems = sems
            return nxt

        lnsc = sb("lnsc", [128, 4, TT], BF16); lnscB = Buf()
        NDIAG = 3
        diag = [sb("diag%d" % i, [128, CW, 128], BF16) for i in range(NDIAG)]; diagB = [Buf() for _ in range(NDIAG)]; diag_i = [0]
        new_stat3 = pool_of("stat", 6, [128, TT], F32, with_sem=True)

        def new_stat():
            a, b, _ = new_stat3()
            return a, b
        new_sig = pool_of("sig", 3, [128, TT], F32)
        new_relu = pool_of("relu", 3, [128, TT], BF16)
        new_gtmp = pool_of("gtmp", 2, [128, TT], F32)
        psA = sb("psA", [128, HP + TT], F32); psB_ = sb("psB", [128, HP + TT], F32)
        psAB = Buf(); psBB = Buf()
        tail32 = sb("tail32", [128, 4, HC], F32); tail32B = Buf()
        hist_c = sb("hist_c", [128, 2, DEPTH, 4, HC], BF16); hist_cB = [[Buf() for _ in range(DEPTH)] for _ in range(2)]
        hist_u = sb("hist_u", [128, 2, DEPTH, 4, HP], F32); hist_uB = [[Buf() for _ in range(DEPTH)] for _ in range(2)]

        def make_tile(name, T):
            t = TileCtx()
            t.T = T
            t.x = sb(name + "_x", [128, KD, T], F32); t.xB = [Buf() for _ in range(KD)]
            t.h = sb(name + "_h", [128, KD, T], BF16); t.hB = [Buf() for _ in range(KD)]
            t.cbf = sb(name + "_c", [128, 4, HC + T], BF16); t.cB = [Buf() for _ in range(4)]
            t.u = sb(name + "_u", [128, 4, HP + T], F32); t.uB = [Buf() for _ in range(4)]
            t.cv = sb(name + "_cv", [128, 4, T], F32)
            t.hid = t.cv.bitcast(BF16).reshape([128, KD, T])
            t.hidB = [Buf() for _ in range(KD)]
            t.xstD = [SemObj(nc, st, name + "_xd%d" % i) for i in range(2)]
            t.pstD = [SemObj(nc, st, name + "_pd%d" % i) for i in range(2)]
            if T == TT:
                t.xst = [t.cv[:, 2 * b:2 * b + 2, :].rearrange("p a t -> p (a t)") for b in range(2)]
                t.xstB = [t.hidB[0:4], t.hidB[4:8]]
                t.pst = [t.u[:, a, 0:512].rearrange("p (b f) -> p b f", f=PLE) for a in range(2)]
                t.pstB = t.uB[0:2]
            else:
                t.xst_t = sb(name + "_xst", [128, D], F32)
                t.xst = [t.xst_t[:, :], t.xst_t[:, :]]
                b_ = Buf(); t.xstB = [[b_], [b_]]
                t.xstD = [t.xstD[0], t.xstD[0]]
                t.pst_t = sb(name + "_pst", [128, 1, PLE], F32)
                t.pst = [t.pst_t[:, :, :]]
                t.pstB = [Buf()]
            return t

        tA = make_tile("tA", TT); tB = make_tile("tB", TT)
        tS = make_tile("tS", DEC) if CFG["do_sample"] else None

        streams = []
        for sid in range(1):
            S = Stream()
            S.sid = sid
            S.ring = [sb("ring%d_%d" % (sid, i), [128, SLOT], BF16) for i in range(NSLOT)]
            S.ringB = [Buf() for _ in range(NSLOT)]
            S.ringD = [SemObj(nc, st, "ringd%d_%d" % (sid, i)) for i in range(NSLOT)]
            S.issued = 0
            streams.append(S)

        setupD = SemObj(nc, st, "setupd")
        op(POOL, lambda e: e.memset(ident[:], 0.0), writes=[identB])
        op(POOL, lambda e: e.affine_select(out=ident[:], in_=ident[:], compare_op=ALU.not_equal, fill=1.0,
                                           base=0, pattern=[[-1, 128]], channel_multiplier=1),
           reads=[identB], writes=[identB])
        grp(POOL, [lambda e: e.memset(ones_d[:], 1.0 / D), lambda e: e.memset(ones_c[:], 1.0 / DC),
                   lambda e: e.memset(epsb[:], EPS)], writes=[constB])
        fns = []
        for g, w in enumerate(POOL_W):
            for pos in range(16):
                v = 1.0 / min(w, pos + 1)
                fns.append(lambda e, g=g, pos=pos, v=v: e.memset(invcnt[:, g, pos:pos + 1], v))
        grp(POOL, fns, writes=[constB])

        stg = tA.x
        stgB = Buf()
        vecs = [(g_mix, C_GMIX, KD), (g_ffn, C_GFFN, KD), (g_ple, C_GPLE, KD), (conv_b, C_CB, 4),
                (ln_g, C_LG, 4), (ln_b, C_LB, 4), (pool_scale, C_PS, 4)]
        for l in range(DEPTH):
            ti, r0 = l // 2, (l % 2) * NV
            for (v, base, nk) in vecs:
                src = v[l].rearrange("(k p) -> k p", p=128)
                dma(SP, setupD, stg[r0 + base:r0 + base + nk, ti, 0:128], src, writes=[stgB])
        dma(SP, setupD, stg[2 * NV:2 * NV + KD, 1, 0:128], g_final.rearrange("(k p) -> k p", p=128), writes=[stgB])
        for l in range(DEPTH):
            dma(SP, setupD, stg[0:CW * 4, 2 + l, 0:128], conv_w[l].rearrange("j (m p) -> (j m) p", p=128), writes=[stgB])
        stgB.w = (setupD, setupD.n)
        tr_plan = [(0, 2 * NV, 0), (1, 2 * NV + KD, 2 * NV)] + [(2 + l, CW * 4, C_CW + l * CW * 4) for l in range(DEPTH)]
        for (ti, rows, col0) in tr_plan:
            bk, bkB = next_bank()
            grp(PE, [lambda e, ti=ti, rows=rows, bk=bk: e.transpose(bk[:, 0:rows], stg[0:rows, ti, 0:128], ident[0:rows, 0:rows])],
                reads=[stgB, identB], writes=[bkB])
            op(ACT, lambda e, rows=rows, col0=col0, bk=bk: e.activation(out=tab[:, col0:col0 + rows], in_=bk[:, 0:rows], func=AF.Copy),
               reads=[bkB], writes=[tabB])
        for E in (PE, ACT, DVE, POOL):
            E.wait([tabB.w, constB.w, identB.w])
        for b in (tabB, constB, identB):
            b.ro = True
        for k in range(KD):
            tA.xB[k].r = dict(stgB.r)
            tA.xB[k].w = stgB.w

        def wv(w2d):
            return w2d.rearrange("(k p) n -> p k n", p=128)

        def unit_srcs(l, kind, q):
            if kind == "in_ag":
                W = wv(w_in[l])
                return [(0, W[:, :, q * 256:(q + 1) * 256], KD, 256),
                        (KD * 256, W[:, :, DC + q * 256:DC + (q + 1) * 256], KD, 256)]
            if kind == "in_u":
                return [(0, wv(w_in[l])[:, :, 2 * DC:3 * DC], KD, DC),
                        (KD * DC, pool_w[l].rearrange("g c d -> c g d"), 4, 128)]
            if kind == "out":
                return [(0, wv(w_out[l])[:, :, q * 512:(q + 1) * 512], KD, 512)]
            if kind == "ff1":
                return [(0, wv(w_ff1[l])[:, :, q * 512:(q + 1) * 512], KD, 512)]
            if kind == "ff2":
                qq, hf = q // 2, q % 2
                return [(0, wv(w_ff2[l][qq * D:(qq + 1) * D, :])[:, :, hf * 512:(hf + 1) * 512], KD, 512)]
            if kind == "gp":
                return [(0, wv(w_gate[l])[:, :, q * 512:(q + 1) * 512], KD, 512),
                        (KD * 512, wv(w_ple[l])[:, :, q * 512:(q + 1) * 512], 2, 512)]
            raise ValueError(kind)

        layer_units = [("in_ag", 0), ("in_ag", 1), ("in_u", 0), ("out", 0), ("out", 1)]
        for qq in range(NQ):
            layer_units += [("ff1", 2 * qq), ("ff1", 2 * qq + 1), ("ff2", 2 * qq), ("ff2", 2 * qq + 1)]
        layer_units += [("gp", 0), ("gp", 1)]
        NU = len(layer_units)
        unit_index = {ku: i for i, ku in enumerate(layer_units)}
        n_units_total = n_pass * depth * NU

        def issue_loads(S, upto):
            upto = min(upto, n_units_total - 1)
            while S.issued <= upto:
                i = S.issued
                l = (i // NU) % depth
                kind, q = layer_units[i % NU]
                s = i % NSLOT
                for (off, src, nk, nn) in unit_srcs(l, kind, q):
                    dst = S.ring[s][:, off:off + nk * nn].rearrange("p (k n) -> p k n", k=nk)
                    half = max(1, nk // 2) if nk * nn >= 2048 else nk
                    for k0 in range(0, nk, half):
                        dma(POOL, S.ringD[s], dst[:, k0:k0 + half, :], src[:, k0:k0 + half, :], writes=[S.ringB[s]])
                S.issued += 1

        def get_unit(S, p, l, kind, q):
            i = (p * depth + l) * NU + unit_index[(kind, q)]
            issue_loads(S, i + NSLOT - 1)
            s = i % NSLOT
            return S.ring[s], S.ringB[s]

        def wview(slot, nk, nn, off=0):
            return slot[:, off:off + nk * nn].rearrange("p (k n) -> p k n", k=nk)

        def mmcost(n, T):
            return n * 0.25 if T == TT else ("s", n * 0.11)

        def mm_group(bk, bkB, T, pairs, reads, per=None):
            n = len(pairs)
            PE.wait(_deps(reads, [bkB]))
            inst = None
            for i, (lhsT, rhs) in enumerate(pairs):
                if per is not None:
                    PE.wait(_deps([per[i]], []))
                inst = PE.eng.matmul(bk[:, 0:T], lhsT=lhsT, rhs=rhs, start=(i == 0), stop=(i == n - 1))
            PE.n += 1
            inst.then_inc(PE.sem, 1)
            _commit((PE, PE.n), list(reads) + (list(per) if per is not None else []), [bkB])

        def rstd_from(src_ap, srcB, T, bias):
            s, sB = new_stat()
            if bias:
                op(ACT, lambda e: e.activation(out=s[:, 0:T], in_=src_ap, func=AF.Sqrt, bias=epsb[:, 0:1]),
                   reads=srcB, writes=[sB])
            else:
                op(ACT, lambda e: e.activation(out=s[:, 0:T], in_=src_ap, func=AF.Sqrt), reads=srcB, writes=[sB])
            r, rB = new_stat()
            op(DVE, lambda e: e.reciprocal(out=r[:, 0:T], in_=s[:, 0:T]), reads=[sB], writes=[rB])
            return r, rB

        def rmsnorm(t, T, gbase, l, out_f32=False):
            op(ACT, lambda e: e.activation(out=t.h[:, :, 0:T], in_=t.x[:, :, 0:T], func=AF.Square),
               reads=t.xB, writes=t.hB)
            bk, bkB = next_bank()
            mm_group(bk, bkB, T, [(ones_d[:], t.h[:, k, 0:T]) for k in range(KD)], reads=t.hB)
            r, rB = rstd_from(bk[:, 0:T], [bkB], T, True)
            fns = []
            for k in range(KD):
                gcol = tab[:, C_GFIN + k:C_GFIN + k + 1] if l is None else tcol(l, gbase, k)
                dst = t.x[:, k, 0:T] if out_f32 else t.h[:, k, 0:T]
                fns.append(lambda e, k=k, gcol=gcol, dst=dst: e.scalar_tensor_tensor(
                    out=dst, in0=t.x[:, k, 0:T], scalar=gcol, in1=r[:, 0:T], op0=ALU.mult, op1=ALU.mult))
            grp(DVE, fns, reads=[rB] + t.xB, writes=(t.xB if out_f32 else t.hB))
            return 12.0 if T == TT else ("s", 5.0)

        def load_x(t, T, src, tok0):
            nch = max(1, T // 128)
            rows = min(128, T)
            for tc in range(nch):
                b = tc % 2
                dma(SP, t.xstD[b], t.xst[b][0:rows, :], src[tok0 + tc * 128:tok0 + tc * 128 + rows, :], writes=t.xstB[b])
                for half in range(2):
                    bk, bkB = next_bank()
                    fns = []
                    for kk in range(4):
                        k = half * 4 + kk
                        fns.append(lambda e, k=k, kk=kk, b=b, bk=bk: e.transpose(
                            bk[:, kk * 128:kk * 128 + rows], t.xst[b][0:rows, k * 128:(k + 1) * 128], ident[0:rows, 0:rows]))
                    grp(PE, fns, reads=t.xstB[b], writes=[bkB])
                    src_v = bk[:, :].rearrange("p (a b) -> p a b", a=4)[:, :, 0:rows]
                    op(ACT, lambda e, half=half, tc=tc, src_v=src_v: e.activation(
                        out=t.x[:, half * 4:half * 4 + 4, tc * 128:tc * 128 + rows], in_=src_v, func=AF.Copy),
                       reads=[bkB], writes=t.xB[half * 4:half * 4 + 4])
                yield 2.5 if T == TT else ("s", 2.5)

        def store_y(t, T, dst, tok0):
            nch = max(1, T // 128)
            rows = min(128, T)
            for tc in range(nch):
                b = tc % 2
                for half in range(2):
                    bk, bkB = next_bank()
                    fns = []
                    for kk in range(4):
                        k = half * 4 + kk
                        fns.append(lambda e, k=k, kk=kk, bk=bk, tc=tc: e.transpose(
                            bk[0:rows, kk * 128:(kk + 1) * 128], t.x[:, k, tc * 128:tc * 128 + rows], ident[:, :]))
                    grp(PE, fns, reads=t.xB[half * 4:half * 4 + 4], writes=[bkB])
                    op(ACT, lambda e, half=half, b=b, bk=bk: e.activation(
                        out=t.xst[b][0:rows, half * 512:(half + 1) * 512], in_=bk[0:rows, :], func=AF.Copy),
                       reads=[bkB], writes=t.xstB[b])
                dma(SP, t.xstD[b], dst[tok0 + tc * 128:tok0 + tc * 128 + rows, :], t.xst[b][0:rows, :], reads=t.xstB[b])
                yield 2.5 if T == TT else ("s", 2.5)

        def load_p(t, T, src_l, tok0, pT, pTB):
            nch = max(1, T // 128)
            rows = min(128, T)
            if nch > 1:
                for a in range(2):
                    dma(SP, t.pstD[a], t.pst[a], src_l[tok0 + a * 256:tok0 + (a + 1) * 256, :].rearrange("(c p) f -> p c f", p=128),
                        writes=[t.pstB[a]])
                pch = [(t.pst[tc // 2][:, tc % 2, :], t.pstB[tc // 2]) for tc in range(nch)]
            else:
                dma(SP, t.pstD[0], t.pst[0][0:rows, 0, :], src_l[tok0:tok0 + rows, :], writes=[t.pstB[0]])
                pch = [(t.pst[0][:, 0, :], t.pstB[0])]
            for fc in range(2):
                bk, bkB = next_bank()
                fns = []
                for tc in range(nch):
                    fns.append(lambda e, tc=tc, fc=fc, bk=bk: e.transpose(
                        bk[:, tc * 128:tc * 128 + rows], pch[tc][0][0:rows, fc * 128:(fc + 1) * 128], ident[0:rows, 0:rows]))
                grp(PE, fns, reads=[b for (_, b) in pch], writes=[bkB])
                op(ACT, lambda e, fc=fc, bk=bk: e.activation(out=pT[:, fc, 0:T], in_=bk[:, 0:T], func=AF.Copy),
                   reads=[bkB], writes=pTB)

        def out_hist(src_ap3, srcB, n, dst):
            if CFG.get("dbg_no_out_hist"):
                return
            bk, bkB = next_bank()
            fns = []
            for m in range(4):
                fns.append(lambda e, m=m, bk=bk: e.transpose(bk[0:n, m * 128:(m + 1) * 128], src_ap3[:, m, :], ident[:, :]))
            grp(PE, fns, reads=srcB, writes=[bkB])
            hs, hsB, hsD = new_stat3()
            op(ACT, lambda e, bk=bk: e.activation(out=hs[0:n, :], in_=bk[0:n, :], func=AF.Copy), reads=[bkB], writes=[hsB])
            dma(SP, hsD, dst, hs[0:n, :], reads=[hsB])

        def load_hist(src, n, dst_ap, dstB):
            hs, hsB, hsD = new_stat3()
            dma(SP, hsD, hs[0:n, :], src, writes=[hsB])
            bk, bkB = next_bank()
            fns = []
            for m in range(4):
                fns.append(lambda e, m=m, bk=bk: e.transpose(bk[:, m * 32:m * 32 + n], hs[0:n, m * 128:(m + 1) * 128], ident[0:n, 0:n]))
            grp(PE, fns, reads=[hsB], writes=[bkB])
            src_v = bk[:, 0:128].rearrange("p (a b) -> p a b", a=4)[:, :, 0:n]
            op(ACT, lambda e: e.activation(out=dst_ap, in_=src_v, func=AF.Copy), reads=[bkB], writes=[dstB])

        hist_flag = set()

        def mixer_half(S, p, l, tiles):
            for (t, T, info) in tiles:
                yield rmsnorm(t, T, C_GMIX, l)
            hs = [None] * len(tiles)
            for q in range(2):
                w, wB = get_unit(S, p, l, "in_ag", q)
                Wa = wview(w, KD, 256); Wg = wview(w, KD, 256, off=KD * 256)
                for (t, T, info) in tiles:
                    hc, hcB = info["hc"][:, l, :, :], info["hcB"][l]
                    for mm in range(2):
                        m = 2 * q + mm
                        bka, bkaB = next_bank()
                        mm_group(bka, bkaB, T, [(Wa[:, k, mm * 128:(mm + 1) * 128], t.h[:, k, 0:T]) for k in range(KD)],
                                 reads=[wB], per=t.hB)
                        bkg, bkgB = next_bank()
                        mm_group(bkg, bkgB, T, [(Wg[:, k, mm * 128:(mm + 1) * 128], t.h[:, k, 0:T]) for k in range(KD)],
                                 reads=[wB], per=t.hB)
                        sg, sgB = new_sig()
                        op(ACT, lambda e, sg=sg, bkg=bkg, T=T: e.activation(out=sg[:, 0:T], in_=bkg[:, 0:T], func=AF.Sigmoid),
                           reads=[bkgB], writes=[sgB])
                        fns = [lambda e, t=t, m=m, bka=bka, sg=sg, T=T: e.tensor_tensor(
                            out=t.cbf[:, m, HC:HC + T], in0=bka[:, 0:T], in1=sg[:, 0:T], op=ALU.mult)]
                        wr = [t.cB[m]]
                        if info["last"]:
                            fns.append(lambda e, m=m, bka=bka, sg=sg, T=T: e.tensor_tensor(
                                out=tail32[:, m, :], in0=bka[:, T - HC:T], in1=sg[:, T - HC:T], op=ALU.mult))
                            wr = wr + [tail32B]
                        grp(DVE, fns, reads=[bkaB, sgB], writes=wr)
                        yield mmcost(16, T)
                    if q == 1:
                        op(ACT, lambda e, t=t, hc=hc: e.activation(out=t.cbf[:, :, 0:HC], in_=hc, func=AF.Copy),
                           reads=[hcB], writes=t.cB)
                        if info["last"]:
                            out_hist(tail32, [tail32B], HC, info["nc_out"][l])
                        else:
                            op(ACT, lambda e, t=t, T=T, hc=hc: e.activation(out=hc, in_=t.cbf[:, :, T:T + HC], func=AF.Copy),
                               reads=t.cB, writes=[hcB])
            w, wB = get_unit(S, p, l, "in_u", 0)
            Wu = wview(w, KD, DC); Wp = wview(w, 4, 128, off=KD * DC)
            for (t, T, info) in tiles:
                hu, huB = info["hu"][:, l, :, :], info["huB"][l]
                op(ACT, lambda e, t=t, hu=hu: e.activation(out=t.u[:, :, 0:HP], in_=hu, func=AF.Copy),
                   reads=[huB], writes=t.uB)
                for m in range(4):
                    bk, bkB = next_bank()
                    mm_group(bk, bkB, T, [(Wu[:, k, m * 128:(m + 1) * 128], t.h[:, k, 0:T]) for k in range(KD)],
                             reads=[wB], per=t.hB)
                    op(ACT, lambda e, t=t, m=m, bk=bk, T=T: e.activation(out=t.u[:, m, HP:HP + T], in_=bk[:, 0:T], func=AF.Copy),
                       reads=[bkB], writes=[t.uB[m]])
                    yield mmcost(8, T)
                op(ACT, lambda e, t=t, T=T, hu=hu: e.activation(out=hu, in_=t.u[:, :, T:T + HP], func=AF.Copy),
                   reads=t.uB, writes=[huB])
                if info["last"]:
                    out_hist(hu, [huB], HP, info["np_out"][l])
            for (t, T, info) in tiles:
                L = HP + T
                for g in range(4):
                    ug = t.u[:, g, :]
                    steps = [(psA, psAB, 1)]
                    if g >= 1:
                        steps.append((psB_, psBB, 2))
                    if g >= 2:
                        steps.append((psA, psAB, 4))
                    if g >= 3:
                        steps.append((psB_, psBB, 8))
                    src, srcB, lo = ug, t.uB[g], 0
                    for (dst, dstB, sh) in steps:
                        nlo = lo + sh
                        op(DVE, lambda e, dst=dst, src=src, nlo=nlo, sh=sh, L=L: e.tensor_tensor(
                            out=dst[:, nlo:L], in0=src[:, nlo:L], in1=src[:, nlo - sh:L - sh], op=ALU.add),
                           reads=[srcB], writes=[dstB])
                        src, srcB, lo = dst, dstB, nlo
                    w_ = POOL_W[g]
                    op(DVE, lambda e, src=src, ug=ug, w_=w_, t=t, g=g, T=T: e.scalar_tensor_tensor(
                        out=t.h[:, g, 0:T], in0=src[:, HP:HP + T], scalar=1.0 / w_, in1=ug[:, HP:HP + T],
                        op0=ALU.mult, op1=ALU.subtract), reads=[srcB, t.uB[g]], writes=[t.hB[g]])
                    if info["first"]:
                        nfix = w_ - 1
                        fx, fxB = new_stat()
                        op(DVE, lambda e, src=src, g=g, nfix=nfix, fx=fx: e.tensor_tensor(
                            out=fx[:, 0:nfix], in0=src[:, HP:HP + nfix], in1=invcnt[:, g, 0:nfix], op=ALU.mult),
                           reads=[srcB], writes=[fxB])
                        op(DVE, lambda e, ug=ug, t=t, g=g, nfix=nfix, fx=fx: e.tensor_tensor(
                            out=t.h[:, g, 0:nfix], in0=fx[:, 0:nfix], in1=ug[:, HP:HP + nfix], op=ALU.subtract),
                           reads=[fxB, t.uB[g]], writes=[t.hB[g]])
                    bk, bkB = next_bank()
                    mm_group(bk, bkB, T, [(Wp[:, g, :], t.h[:, g, 0:T])], reads=[wB, t.hB[g]])
                    op(ACT, lambda e, t=t, g=g, bk=bk, T=T: e.activation(out=t.h[:, 4 + g, 0:T], in_=bk[:, 0:T], func=AF.Copy,
                                                                        scale=tcol(l, C_PS, g)),
                       reads=[bkB], writes=[t.hB[4 + g]])
                    yield (0.65 * (len(steps) + 1)) if T == TT else ("s", 0.23 * (len(steps) + 1))
            def build_diag(m):
                b = diag_i[0]; diag_i[0] = (b + 1) % NDIAG
                fns = []
                for j in range(CW):
                    fns.append(lambda e, j=j, b=b: e.tensor_scalar_mul(out=diag[b][:, j, :], in0=ident[:, :], scalar1=cwcol(l, j, m)))
                grp(DVE, fns, writes=[diagB[b]])
                return b

            def conv_mm(t, T, m, db):
                bk, bkB = next_bank()
                mm_group(bk, bkB, T, [(diag[db][:, j, :], t.cbf[:, m, j:j + T]) for j in range(CW)],
                         reads=[diagB[db], t.cB[m]])
                op(ACT, lambda e: e.activation(out=t.cv[:, m, 0:T], in_=bk[:, 0:T], func=AF.Identity, bias=tcol(l, C_CB, m)),
                   reads=[bkB], writes=[t.hidB[2 * m], t.hidB[2 * m + 1]])

            def ln_pre(t, T):
                op(ACT, lambda e: e.activation(out=lnsc[:, :, 0:T], in_=t.cv[:, :, 0:T], func=AF.Copy),
                   reads=t.hidB, writes=[lnscB])
                op(ACT, lambda e: e.activation(out=t.h[:, 0:4, 0:T], in_=t.cv[:, :, 0:T], func=AF.Square),
                   reads=t.hidB, writes=t.hB[0:4])

            def ln_post(t, T):
                bkm, bkmB = next_bank()
                mm_group(bkm, bkmB, T, [(ones_c[:], lnsc[:, m, 0:T]) for m in range(4)], reads=[lnscB])
                bkq, bkqB = next_bank()
                mm_group(bkq, bkqB, T, [(ones_c[:], t.h[:, m, 0:T]) for m in range(4)], reads=t.hB[0:4])
                mean, meanB = new_stat()
                var, varB = new_stat()
                op(ACT, lambda e: e.activation(out=mean[:, 0:T], in_=bkm[:, 0:T], func=AF.Copy), reads=[bkmB], writes=[meanB])
                op(DVE, lambda e: e.tensor_tensor(out=var[:, 0:T], in0=mean[:, 0:T], in1=mean[:, 0:T], op=ALU.mult),
                   reads=[meanB], writes=[varB])
                op(DVE, lambda e: e.tensor_tensor(out=var[:, 0:T], in0=bkq[:, 0:T], in1=var[:, 0:T], op=ALU.subtract),
                   reads=[bkqB, varB], writes=[varB])
                op(DVE, lambda e: e.tensor_scalar(out=var[:, 0:T], in0=var[:, 0:T], scalar1=0.0, scalar2=EPS, op0=ALU.max, op1=ALU.add),
                   reads=[varB], writes=[varB])
                r, rB = rstd_from(var[:, 0:T], [varB], T, False)
                op(DVE, lambda e: e.scalar_tensor_tensor(out=mean[:, 0:T], in0=mean[:, 0:T], scalar=-1.0, in1=r[:, 0:T],
                                                        op0=ALU.mult, op1=ALU.mult), reads=[meanB, rB], writes=[meanB])
                for m in range(4):
                    hb = [t.hidB[2 * m], t.hidB[2 * m + 1]]
                    op(DVE, lambda e, m=m: e.tensor_tensor(out=t.cv[:, m, 0:T], in0=t.cv[:, m, 0:T], in1=r[:, 0:T], op=ALU.mult),
                       reads=hb + [rB], writes=hb)
                    op(DVE, lambda e, m=m: e.tensor_tensor(out=t.cv[:, m, 0:T], in0=t.cv[:, m, 0:T], in1=mean[:, 0:T], op=ALU.add),
                       reads=hb + [meanB], writes=hb)
                    op(ACT, lambda e, m=m: e.activation(out=t.h[:, m, 0:T], in_=t.cv[:, m, 0:T], func=AF.Silu,
                                                        bias=tcol(l, C_LB, m), scale=tcol(l, C_LG, m)), reads=hb, writes=[t.hB[m]])

            work = [(t, T, m) for (t, T, info) in tiles for m in range(4)]
            dbs = {}
            for i in range(min(2, len(work))):
                dbs[i] = build_diag(work[i][2])
            prev = None
            for i, (t, T, m) in enumerate(work):
                conv_mm(t, T, m, dbs[i])
                nxt = i + 2
                if m == 0 and prev is not None:
                    if nxt < len(work):
                        dbs[nxt] = build_diag(work[nxt][2])
                    ln_post(*prev)
                elif nxt < len(work):
                    dbs[nxt] = build_diag(work[nxt][2])
                if m == 3:
                    ln_pre(t, T)
                    prev = (t, T)
                yield 8.0
            ln_post(*prev)
            yield 12.0
            for q in range(2):
                w, wB = get_unit(S, p, l, "out", q)
                W = wview(w, KD, 512)
                for (t, T, info) in tiles:
                    for mm in range(4):
                        m = 4 * q + mm
                        bk, bkB = next_bank()
                        mm_group(bk, bkB, T, [(W[:, k, mm * 128:(mm + 1) * 128], t.h[:, k, 0:T]) for k in range(KD)],
                                 reads=[wB], per=t.hB)
                        op(DVE, lambda e, t=t, m=m, bk=bk, T=T: e.tensor_tensor(out=t.x[:, m, 0:T], in0=bk[:, 0:T], in1=t.x[:, m, 0:T], op=ALU.add),
                           reads=[bkB, t.xB[m]], writes=[t.xB[m]])
                        yield mmcost(8, T)

        def ffn_half(S, p, l, tiles):
            for (t, T, info) in tiles:
                yield rmsnorm(t, T, C_GFFN, l)
            for qq in range(NQ):
                for hf in range(2):
                    w, wB = get_unit(S, p, l, "ff1", 2 * qq + hf)
                    W1 = wview(w, KD, 512)
                    for (t, T, info) in tiles:
                        for mm in range(4):
                            m = 4 * hf + mm
                            bk, bkB = next_bank()
                            mm_group(bk, bkB, T, [(W1[:, k, mm * 128:(mm + 1) * 128], t.h[:, k, 0:T]) for k in range(KD)],
                                     reads=[wB], per=t.hB)
                            rl, rlB = new_relu()
                            op(ACT, lambda e, rl=rl, bk=bk, T=T: e.activation(out=rl[:, 0:T], in_=bk[:, 0:T], func=AF.Relu),
                               reads=[bkB], writes=[rlB])
                            op(ACT, lambda e, t=t, m=m, rl=rl, T=T: e.activation(out=t.hid[:, m, 0:T], in_=rl[:, 0:T], func=AF.Square),
                               reads=[rlB], writes=[t.hidB[m]])
                            yield mmcost(8, T)
                for hf in range(2):
                    w, wB = get_unit(S, p, l, "ff2", 2 * qq + hf)
                    W2 = wview(w, KD, 512)
                    for (t, T, info) in tiles:
                        for mm in range(4):
                            m = 4 * hf + mm
                            bk, bkB = next_bank()
                            mm_group(bk, bkB, T, [(W2[:, k, mm * 128:(mm + 1) * 128], t.hid[:, k, 0:T]) for k in range(KD)],
                                     reads=[wB], per=t.hidB)
                            op(DVE, lambda e, t=t, m=m, bk=bk, T=T: e.tensor_tensor(out=t.x[:, m, 0:T], in0=bk[:, 0:T], in1=t.x[:, m, 0:T], op=ALU.add),
                               reads=[bkB, t.xB[m]], writes=[t.xB[m]])
                            yield mmcost(8, T)
            for (t, T, info) in tiles:
                c = rmsnorm(t, T, C_GPLE, l)
                load_p(t, T, info["psrc"][l], info["tok0"], t.cbf, t.cB[0:2])
                yield (c + 2.0) if T == TT else ("s", 7.0)
            for q in range(2):
                w, wB = get_unit(S, p, l, "gp", q)
                Wg = wview(w, KD, 512); Wpl = wview(w, 2, 512, off=KD * 512)
                for (t, T, info) in tiles:
                    for mm in range(4):
                        m = 4 * q + mm
                        bkg, bkgB = next_bank()
                        mm_group(bkg, bkgB, T, [(Wg[:, k, mm * 128:(mm + 1) * 128], t.h[:, k, 0:T]) for k in range(KD)],
                                 reads=[wB], per=t.hB)
                        bkp, bkpB = next_bank()
                        mm_group(bkp, bkpB, T, [(Wpl[:, kc, mm * 128:(mm + 1) * 128], t.cbf[:, kc, 0:T]) for kc in range(2)],
                                 reads=[wB] + t.cB[0:2])
                        sg, sgB = new_sig()
                        op(ACT, lambda e, sg=sg, bkg=bkg, T=T: e.activation(out=sg[:, 0:T], in_=bkg[:, 0:T], func=AF.Sigmoid),
                           reads=[bkgB], writes=[sgB])
                        gt, gtB = new_gtmp()
                        op(DVE, lambda e, gt=gt, sg=sg, bkp=bkp, T=T: e.tensor_tensor(out=gt[:, 0:T], in0=bkp[:, 0:T], in1=sg[:, 0:T], op=ALU.mult),
                           reads=[bkpB, sgB], writes=[gtB])
                        op(DVE, lambda e, t=t, m=m, gt=gt, T=T: e.tensor_tensor(out=t.x[:, m, 0:T], in0=gt[:, 0:T], in1=t.x[:, m, 0:T], op=ALU.add),
                           reads=[gtB, t.xB[m]], writes=[t.xB[m]])
                        yield mmcost(10, T)

        def stream_gen(S):
            for p in range(n_pass):
                tiles = []
                for ti, tile in enumerate((tA, tB)):
                    tok0 = p * 2 * TT + ti * TT
                    info = {"first": tok0 == 0, "last": tok0 + TT == SEQ, "nc_out": ncp, "np_out": npp,
                            "tok0": tok0, "xsrc": xp, "psrc": pp, "ydst": yp,
                            "hc": hist_c[:, 0], "hcB": hist_cB[0], "hu": hist_u[:, 0], "huB": hist_uB[0]}
                    tiles.append((tile, TT, info))
                if p == 0:
                    for l in range(depth):
                        op(DVE, lambda e, l=l: e.memset(hist_c[:, 0, l, :, :], 0.0), writes=[hist_cB[0][l]])
                        op(DVE, lambda e, l=l: e.memset(hist_u[:, 0, l, :, :], 0.0), writes=[hist_uB[0][l]])
                    if tS is not None:
                        infoS = {"first": False, "last": True, "nc_out": ncs, "np_out": nps,
                                 "tok0": 0, "xsrc": xs, "psrc": psm, "ydst": ys,
                                 "hc": hist_c[:, 1], "hcB": hist_cB[1], "hu": hist_u[:, 1], "huB": hist_uB[1]}
                        tiles.append((tS, DEC, infoS))
                        for l in range(depth):
                            load_hist(cc[l], HC, hist_c[:, 1, l, :, :], hist_cB[1][l])
                            load_hist(cpl[l], HP, hist_u[:, 1, l, :, :], hist_uB[1][l])
                for (t, T, inf) in tiles:
                    yield from load_x(t, T, inf["xsrc"], inf["tok0"])
                for l in range(depth):
                    yield from mixer_half(S, p, l, tiles)
                    yield from ffn_half(S, p, l, tiles)
                for (t, T, inf) in tiles:
                    rmsnorm(t, T, None, None, out_f32=True)
                for (t, T, inf) in tiles:
                    yield from store_y(t, T, inf["ydst"], inf["tok0"])

        for _ in stream_gen(streams[0]):
            pass

        fin = [(d_, d_.n) for d_ in new_stat3.sems]
        for t_ in (tA, tB, tS):
            if t_ is not None:
                fin += [(d_, d_.n) for d_ in t_.xstD + t_.pstD]
        SP.wait(fin)
    return nc


_PROGRAM = [None]


def kernel(x_prompt, x_sample, p_prompt, p_sample, cache_conv, cache_pool, w_in, conv_w, conv_b,
           ln_g, ln_b, pool_w, pool_scale, w_out, g_mix, g_ffn, g_ple, w_ff1, w_ff2, w_ple, w_gate, g_final):
    f = lambda a: np.ascontiguousarray(np.asarray(a, dtype=np.float32))
    shared = {"w_in": f(w_in), "conv_w": f(conv_w), "conv_b": f(conv_b), "ln_g": f(ln_g), "ln_b": f(ln_b),
              "pool_w": f(pool_w), "pool_scale": f(pool_scale), "w_out": f(w_out), "g_mix": f(g_mix),
              "g_ffn": f(g_ffn), "g_ple": f(g_ple), "w_ff1": f(w_ff1), "w_ff2": f(w_ff2), "w_ple": f(w_ple),
              "w_gate": f(w_gate), "g_final": f(g_final)}
    in_maps = []
    for i in range(NCORES):
        m = dict(shared)
        m["xp"] = f(x_prompt[i]); m["xs"] = f(x_sample[i])
        m["pp"] = f(p_prompt[:, i]); m["psm"] = f(p_sample[:, i])
        m["cc"] = f(cache_conv[:, i]); m["cpl"] = f(cache_pool[:, i])
        in_maps.append(m)
    nc = build_program()
    res = run_bass_kernel_spmd(nc, in_maps, core_ids=list(range(NCORES)))
    r = res.results
    y_prompt = np.stack([r[i]["yp"] for i in range(NCORES)], 0)
    y_sample = np.stack([r[i]["ys"] for i in range(NCORES)], 0)
    ncp = np.stack([r[i]["ncp"] for i in range(NCORES)], 1)
    npp = np.stack([r[i]["npp"] for i in range(NCORES)], 1)
    ncs = np.stack([r[i]["ncs"] for i in range(NCORES)], 1)
    nps = np.stack([r[i]["nps"] for i in range(NCORES)], 1)
    return (y_prompt.astype(np.float32), y_sample.astype(np.float32), ncp.astype(np.float32),
            npp.astype(np.float32), ncs.astype(np.float32), nps.astype(np.float32))
```

```python
import contextlib
import numpy as np
import concourse.bass as bass
import concourse.mybir as mybir
from concourse.bass_utils import run_bass_kernel_spmd

F32 = mybir.dt.float32
BF16 = mybir.dt.bfloat16
ALU = mybir.AluOpType
AF = mybir.ActivationFunctionType

NCORES = 8
D = 1024
KD = D // 128
SEQ = 4096
DEC = 64
DEPTH = 4
DC = 512
CW = 31
HC = CW - 1
HP = 15
PLE = 256
DFF = 4096
EPS = 1e-6
TT = 512
NQ = 4
SLOT = 5 * 1024
NSLOT = 3
POOL_W = (2, 4, 8, 16)

CFG = {"n_pass": 4, "do_sample": True, "depth": DEPTH}


class SemObj:
    def __init__(self, nc, stack, name):
        self.sem = stack.enter_context(nc.semaphore(name))
        self.n = 0


class Eng(SemObj):
    def __init__(self, nc, stack, eng, name):
        super().__init__(nc, stack, "s_" + name)
        self.eng = eng
        self.seen = {}

    def wait(self, deps):
        best = {}
        for (S, c) in deps:
            if c > best.get(S, 0):
                best[S] = c
        for S, c in best.items():
            if c > self.seen.get(S, 0):
                self.eng.wait_ge(S.sem, c)
                self.seen[S] = c


class Buf:
    __slots__ = ("w", "r", "ro")

    def __init__(self):
        self.w = None
        self.r = {}
        self.ro = False


def _deps(reads, writes):
    deps = []
    for b in reads:
        if b.w is not None:
            deps.append(b.w)
    for b in writes:
        if b.w is not None:
            deps.append(b.w)
        deps.extend(b.r.items())
    return deps


def _commit(tok, reads, writes):
    S, c = tok
    for b in reads:
        if not b.ro:
            b.r[S] = c
    for b in writes:
        b.w = tok
        b.r = {}


def grp(E, fns, reads=(), writes=(), nosame=False):
    deps = _deps(reads, writes)
    if nosame:
        deps = [d for d in deps if d[0] is not E]
    E.wait(deps)
    inst = None
    for fn in fns:
        inst = fn(E.eng)
    E.n += 1
    inst.then_inc(E.sem, 1)
    _commit((E, E.n), reads, writes)


def op(E, fn, reads=(), writes=()):
    grp(E, [fn], reads, writes)


def dma(Q, Dm, out, in_, reads=(), writes=()):
    Q.wait(_deps(reads, writes))
    Q.eng.dma_start(out=out, in_=in_).then_inc(Dm.sem, 16)
    Dm.n += 16
    _commit((Dm, Dm.n), reads, writes)


class TileCtx:
    pass


class Stream:
    pass


def build_program():
    nc = bass.Bass("TRN2", target_bir_lowering=False)
    depth = CFG["depth"]
    n_pass = CFG["n_pass"]

    def din(name, shape):
        return nc.dram_tensor(name, list(shape), F32, kind="ExternalInput").ap()

    def dout(name, shape):
        return nc.dram_tensor(name, list(shape), F32, kind="ExternalOutput").ap()

    xp = din("xp", [SEQ, D]); xs = din("xs", [DEC, D])
    pp = din("pp", [DEPTH, SEQ, PLE]); psm = din("psm", [DEPTH, DEC, PLE])
    cc = din("cc", [DEPTH, HC, DC]); cpl = din("cpl", [DEPTH, HP, DC])
    w_in = din("w_in", [DEPTH, D, 3 * DC]); conv_w = din("conv_w", [DEPTH, CW, DC])
    conv_b = din("conv_b", [DEPTH, DC]); ln_g = din("ln_g", [DEPTH, DC]); ln_b = din("ln_b", [DEPTH, DC])
    pool_w = din("pool_w", [DEPTH, 4, 128, 128]); pool_scale = din("pool_scale", [DEPTH, DC])
    w_out = din("w_out", [DEPTH, D, D]); g_mix = din("g_mix", [DEPTH, D]); g_ffn = din("g_ffn", [DEPTH, D])
    g_ple = din("g_ple", [DEPTH, D]); w_ff1 = din("w_ff1", [DEPTH, D, DFF]); w_ff2 = din("w_ff2", [DEPTH, DFF, D])
    w_ple = din("w_ple", [DEPTH, PLE, D]); w_gate = din("w_gate", [DEPTH, D, D]); g_final = din("g_final", [D])
    yp = dout("yp", [SEQ, D]); ys = dout("ys", [DEC, D])
    ncp = dout("ncp", [DEPTH, HC, DC]); npp = dout("npp", [DEPTH, HP, DC])
    ncs = dout("ncs", [DEPTH, HC, DC]); nps = dout("nps", [DEPTH, HP, DC])

    with contextlib.ExitStack() as st:
        def sb(name, shape, dt):
            return st.enter_context(nc.sbuf_tensor(name, list(shape), dt))

        PE = Eng(nc, st, nc.tensor, "pe"); ACT = Eng(nc, st, nc.scalar, "act")
        DVE = Eng(nc, st, nc.vector, "dve"); POOL = Eng(nc, st, nc.gpsimd, "pool")
        SP = Eng(nc, st, nc.sync, "sp")

        ident = sb("ident", [128, 128], F32); identB = Buf()
        ones_d = sb("ones_d", [128, 128], BF16)
        ones_c = sb("ones_c", [128, 128], BF16)
        epsb = sb("epsb", [128, 1], F32)
        constB = Buf()
        invcnt = sb("invcnt", [128, 4, 16], F32)
        NV = 3 * KD + 4 * 4
        C_GMIX, C_GFFN, C_GPLE, C_CB, C_LG, C_LB, C_PS = 0, 8, 16, 24, 28, 32, 36
        C_GFIN = DEPTH * NV
        C_CW = C_GFIN + KD
        NCOL = C_CW + DEPTH * CW * 4
        tab = sb("tab", [128, NCOL], F32); tabB = Buf()

        def tcol(l, base, k):
            c = l * NV + base + k
            return tab[:, c:c + 1]

        def cwcol(l, j, m):
            c = C_CW + l * CW * 4 + j * 4 + m
            return tab[:, c:c + 1]

        banks = [st.enter_context(nc.psum_tensor("bank%d" % i, [128, 512], F32)) for i in range(8)]
        bankB = [Buf() for _ in range(8)]
        bank_i = [0]

        def next_bank():
            i = bank_i[0]
            bank_i[0] = (i + 1) % 8
            return banks[i], bankB[i]

        def pool_of(name, n, shape, dt, with_sem=False):
            aps = [sb("%s%d" % (name, i), shape, dt) for i in range(n)]
            bufs = [Buf() for _ in range(n)]
            sems = [SemObj(nc, st, "%sd%d" % (name, i)) for i in range(n)] if with_sem else None
            idx = [0]

            def nxt():
                i = idx[0]
                idx[0] = (i + 1) % n
                if with_sem:
                    return aps[i], bufs[i], sems[i]
                return aps[i], bufs[i]
            nxt.sems = sems
            return nxt

        lnsc = sb("lnsc", [128, 4, TT], BF16); lnscB = Buf()
        diagL = sb("diagL", [128, 4, CW, 128], BF16)
        diagLB = [Buf() for _ in range(4)]

        def build_diag_chunk(l, m):
            fns = []
            for j in range(CW):
                fns.append(lambda e, j=j: e.tensor_scalar_mul(out=diagL[:, m, j, :], in0=ident[:, :], scalar1=cwcol(l, j, m)))
            grp(DVE, fns, writes=[diagLB[m]])
        new_stat3 = pool_of("stat", 6, [128, TT], F32, with_sem=True)

        def new_stat():
            a, b, _ = new_stat3()
            return a, b
        new_sig = pool_of("sig", 3, [128, TT], F32)
        new_relu = pool_of("relu", 3, [128, TT], BF16)
        new_gtmp = pool_of("gtmp", 2, [128, TT], F32)
        psA = sb("psA", [128, HP + TT], F32); psB_ = sb("psB", [128, HP + TT], F32)
        psAB = Buf(); psBB = Buf()
        tail32 = sb("tail32", [128, 4, HC], F32); tail32B = Buf()
        hist_c = sb("hist_c", [128, 2, DEPTH, 4, HC], BF16); hist_cB = [[Buf() for _ in range(DEPTH)] for _ in range(2)]
        hist_u = sb("hist_u", [128, 2, DEPTH, 4, HP], F32); hist_uB = [[Buf() for _ in range(DEPTH)] for _ in range(2)]

        def make_tile(name, T):
            t = TileCtx()
            t.T = T
            t.x = sb(name + "_x", [128, KD, T], F32); t.xB = [Buf() for _ in range(KD)]
            t.h = sb(name + "_h", [128, KD, T], BF16); t.hB = [Buf() for _ in range(KD)]
            t.cbf = sb(name + "_c", [128, 4, HC + T], BF16); t.cB = [Buf() for _ in range(4)]
            t.u = sb(name + "_u", [128, 4, HP + T], F32); t.uB = [Buf() for _ in range(4)]
            t.cv = sb(name + "_cv", [128, 4, T], F32)
            t.hid = t.cv.bitcast(BF16).reshape([128, KD, T])
            t.hidB = [Buf() for _ in range(KD)]
            t.xstD = [SemObj(nc, st, name + "_xd%d" % i) for i in range(2)]
            t.pstD = [SemObj(nc, st, name + "_pd%d" % i) for i in range(2)]
            if T == TT:
                t.xst = [t.cv[:, 2 * b:2 * b + 2, :].rearrange("p a t -> p (a t)") for b in range(2)]
                t.xstB = [t.hidB[0:4], t.hidB[4:8]]
                t.pst = [t.u[:, a, 0:512].rearrange("p (b f) -> p b f", f=PLE) for a in range(2)]
                t.pstB = t.uB[0:2]
            else:
                t.xst_t = sb(name + "_xst", [128, D], F32)
                t.xst = [t.xst_t[:, :], t.xst_t[:, :]]
                b_ = Buf(); t.xstB = [[b_], [b_]]
                t.xstD = [t.xstD[0], t.xstD[0]]
                t.pst_t = sb(name + "_pst", [128, 1, PLE], F32)
                t.pst = [t.pst_t[:, :, :]]
                t.pstB = [Buf()]
            return t

        tA = make_tile("tA", TT); tB = make_tile("tB", TT)
        tS = make_tile("tS", DEC) if CFG["do_sample"] else None

        streams = []
        for sid in range(1):
            S = Stream()
            S.sid = sid
            S.ring = [sb("ring%d_%d" % (sid, i), [128, SLOT], BF16) for i in range(NSLOT)]
            S.ringB = [Buf() for _ in range(NSLOT)]
            S.ringD = [SemObj(nc, st, "ringd%d_%d" % (sid, i)) for i in range(NSLOT)]
            S.issued = 0
            streams.append(S)

        setupD = SemObj(nc, st, "setupd")
        op(POOL, lambda e: e.memset(ident[:], 0.0), writes=[identB])
        op(POOL, lambda e: e.affine_select(out=ident[:], in_=ident[:], compare_op=ALU.not_equal, fill=1.0,
                                           base=0, pattern=[[-1, 128]], channel_multiplier=1),
           reads=[identB], writes=[identB])
        grp(POOL, [lambda e: e.memset(ones_d[:], 1.0 / D), lambda e: e.memset(ones_c[:], 1.0 / DC),
                   lambda e: e.memset(epsb[:], EPS)], writes=[constB])
        fns = []
        for g, w in enumerate(POOL_W):
            for pos in range(16):
                v = 1.0 / min(w, pos + 1)
                fns.append(lambda e, g=g, pos=pos, v=v: e.memset(invcnt[:, g, pos:pos + 1], v))
        grp(POOL, fns, writes=[constB])

        stg = tA.x
        stgB = Buf()
        vecs = [(g_mix, C_GMIX, KD), (g_ffn, C_GFFN, KD), (g_ple, C_GPLE, KD), (conv_b, C_CB, 4),
                (ln_g, C_LG, 4), (ln_b, C_LB, 4), (pool_scale, C_PS, 4)]
        for l in range(DEPTH):
            ti, r0 = l // 2, (l % 2) * NV
            for (v, base, nk) in vecs:
                src = v[l].rearrange("(k p) -> k p", p=128)
                dma(SP, setupD, stg[r0 + base:r0 + base + nk, ti, 0:128], src, writes=[stgB])
        dma(SP, setupD, stg[2 * NV:2 * NV + KD, 1, 0:128], g_final.rearrange("(k p) -> k p", p=128), writes=[stgB])
        for l in range(DEPTH):
            dma(SP, setupD, stg[0:CW * 4, 2 + l, 0:128], conv_w[l].rearrange("j (m p) -> (j m) p", p=128), writes=[stgB])
        stgB.w = (setupD, setupD.n)
        tr_plan = [(0, 2 * NV, 0), (1, 2 * NV + KD, 2 * NV)] + [(2 + l, CW * 4, C_CW + l * CW * 4) for l in range(DEPTH)]
        for (ti, rows, col0) in tr_plan:
            bk, bkB = next_bank()
            grp(PE, [lambda e, ti=ti, rows=rows, bk=bk: e.transpose(bk[:, 0:rows], stg[0:rows, ti, 0:128], ident[0:rows, 0:rows])],
                reads=[stgB, identB], writes=[bkB])
            op(ACT, lambda e, rows=rows, col0=col0, bk=bk: e.activation(out=tab[:, col0:col0 + rows], in_=bk[:, 0:rows], func=AF.Copy),
               reads=[bkB], writes=[tabB])
        for E in (PE, ACT, DVE, POOL):
            E.wait([tabB.w, constB.w, identB.w])
        for b in (tabB, constB, identB):
            b.ro = True
        for k in range(KD):
            tA.xB[k].r = dict(stgB.r)
            tA.xB[k].w = stgB.w

        def wv(w2d):
            return w2d.rearrange("(k p) n -> p k n", p=128)

        def unit_srcs(l, kind, q):
            if kind == "in_ag":
                W = wv(w_in[l])
                return [(0, W[:, :, q * 256:(q + 1) * 256], KD, 256),
                        (KD * 256, W[:, :, DC + q * 256:DC + (q + 1) * 256], KD, 256)]
            if kind == "in_u":
                return [(0, wv(w_in[l])[:, :, 2 * DC:3 * DC], KD, DC),
                        (KD * DC, pool_w[l].rearrange("g c d -> c g d"), 4, 128)]
            if kind == "out":
                return [(0, wv(w_out[l])[:, :, q * 512:(q + 1) * 512], KD, 512)]
            if kind == "ff1":
                return [(0, wv(w_ff1[l])[:, :, q * 512:(q + 1) * 512], KD, 512)]
            if kind == "ff2":
                qq, hf = q // 2, q % 2
                return [(0, wv(w_ff2[l][qq * D:(qq + 1) * D, :])[:, :, hf * 512:(hf + 1) * 512], KD, 512)]
            if kind == "gp":
                return [(0, wv(w_gate[l])[:, :, q * 512:(q + 1) * 512], KD, 512),
                        (KD * 512, wv(w_ple[l])[:, :, q * 512:(q + 1) * 512], 2, 512)]
            raise ValueError(kind)

        layer_units = [("in_ag", 0), ("in_ag", 1), ("in_u", 0), ("out", 0), ("out", 1)]
        for qq in range(NQ):
            layer_units += [("ff1", 2 * qq), ("ff1", 2 * qq + 1), ("ff2", 2 * qq), ("ff2", 2 * qq + 1)]
        layer_units += [("gp", 0), ("gp", 1)]
        NU = len(layer_units)
        unit_index = {ku: i for i, ku in enumerate(layer_units)}
        n_units_total = n_pass * depth * NU

        def issue_loads(S, upto):
            upto = min(upto, n_units_total - 1)
            while S.issued <= upto:
                i = S.issued
                l = (i // NU) % depth
                kind, q = layer_units[i % NU]
                s = i % NSLOT
                for (off, src, nk, nn) in unit_srcs(l, kind, q):
                    dst = S.ring[s][:, off:off + nk * nn].rearrange("p (k n) -> p k n", k=nk)
                    half = max(1, nk // 2) if nk * nn >= 2048 else nk
                    for k0 in range(0, nk, half):
                        dma(POOL, S.ringD[s], dst[:, k0:k0 + half, :], src[:, k0:k0 + half, :], writes=[S.ringB[s]])
                S.issued += 1

        def get_unit(S, p, l, kind, q):
            i = (p * depth + l) * NU + unit_index[(kind, q)]
            issue_loads(S, i + NSLOT - 1)
            s = i % NSLOT
            return S.ring[s], S.ringB[s]

        def wview(slot, nk, nn, off=0):
            return slot[:, off:off + nk * nn].rearrange("p (k n) -> p k n", k=nk)

        def mmcost(n, T):
            return n * 0.25 if T == TT else ("s", n * 0.11)

        def mm_group(bk, bkB, T, pairs, reads, per=None):
            n = len(pairs)
            PE.wait(_deps(reads, [bkB]))
            inst = None
            for i, (lhsT, rhs) in enumerate(pairs):
                if per is not None:
                    PE.wait(_deps([per[i]], []))
                inst = PE.eng.matmul(bk[:, 0:T], lhsT=lhsT, rhs=rhs, start=(i == 0), stop=(i == n - 1))
            PE.n += 1
            inst.then_inc(PE.sem, 1)
            _commit((PE, PE.n), list(reads) + (list(per) if per is not None else []), [bkB])

        def rstd_from(src_ap, srcB, T, bias):
            s, sB = new_stat()
            if bias:
                op(ACT, lambda e: e.activation(out=s[:, 0:T], in_=src_ap, func=AF.Sqrt, bias=epsb[:, 0:1]),
                   reads=srcB, writes=[sB])
            else:
                op(ACT, lambda e: e.activation(out=s[:, 0:T], in_=src_ap, func=AF.Sqrt), reads=srcB, writes=[sB])
            r, rB = new_stat()
            op(DVE, lambda e: e.reciprocal(out=r[:, 0:T], in_=s[:, 0:T]), reads=[sB], writes=[rB])
            return r, rB

        def rmsnorm(t, T, gbase, l, out_f32=False):
            op(ACT, lambda e: e.activation(out=t.h[:, :, 0:T], in_=t.x[:, :, 0:T], func=AF.Square),
               reads=t.xB, writes=t.hB)
            bk, bkB = next_bank()
            mm_group(bk, bkB, T, [(ones_d[:], t.h[:, k, 0:T]) for k in range(KD)], reads=t.hB)
            r, rB = rstd_from(bk[:, 0:T], [bkB], T, True)
            fns = []
            for k in range(KD):
                gcol = tab[:, C_GFIN + k:C_GFIN + k + 1] if l is None else tcol(l, gbase, k)
                dst = t.x[:, k, 0:T] if out_f32 else t.h[:, k, 0:T]
                fns.append(lambda e, k=k, gcol=gcol, dst=dst: e.scalar_tensor_tensor(
                    out=dst, in0=t.x[:, k, 0:T], scalar=gcol, in1=r[:, 0:T], op0=ALU.mult, op1=ALU.mult))
            grp(DVE, fns, reads=[rB] + t.xB, writes=(t.xB if out_f32 else t.hB))
            return 12.0 if T == TT else ("s", 5.0)

        def load_x(t, T, src, tok0):
            nch = max(1, T // 128)
            rows = min(128, T)
            for tc in range(nch):
                b = tc % 2
                dma(SP, t.xstD[b], t.xst[b][0:rows, :], src[tok0 + tc * 128:tok0 + tc * 128 + rows, :], writes=t.xstB[b])
                for half in range(2):
                    bk, bkB = next_bank()
                    fns = []
                    for kk in range(4):
                        k = half * 4 + kk
                        fns.append(lambda e, k=k, kk=kk, b=b, bk=bk: e.transpose(
                            bk[:, kk * 128:kk * 128 + rows], t.xst[b][0:rows, k * 128:(k + 1) * 128], ident[0:rows, 0:rows]))
                    grp(PE, fns, reads=t.xstB[b], writes=[bkB])
                    src_v = bk[:, :].rearrange("p (a b) -> p a b", a=4)[:, :, 0:rows]
                    op(ACT, lambda e, half=half, tc=tc, src_v=src_v: e.activation(
                        out=t.x[:, half * 4:half * 4 + 4, tc * 128:tc * 128 + rows], in_=src_v, func=AF.Copy),
                       reads=[bkB], writes=t.xB[half * 4:half * 4 + 4])
                yield 2.5 if T == TT else ("s", 2.5)

        def store_y(t, T, dst, tok0):
            nch = max(1, T // 128)
            rows = min(128, T)
            for tc in range(nch):
                b = tc % 2
                for half in range(2):
                    bk, bkB = next_bank()
                    fns = []
                    for kk in range(4):
                        k = half * 4 + kk
                        fns.append(lambda e, k=k, kk=kk, bk=bk, tc=tc: e.transpose(
                            bk[0:rows, kk * 128:(kk + 1) * 128], t.x[:, k, tc * 128:tc * 128 + rows], ident[:, :]))
                    grp(PE, fns, reads=t.xB[half * 4:half * 4 + 4], writes=[bkB])
                    op(ACT, lambda e, half=half, b=b, bk=bk: e.activation(
                        out=t.xst[b][0:rows, half * 512:(half + 1) * 512], in_=bk[0:rows, :], func=AF.Copy),
                       reads=[bkB], writes=t.xstB[b])
                dma(SP, t.xstD[b], dst[tok0 + tc * 128:tok0 + tc * 128 + rows, :], t.xst[b][0:rows, :], reads=t.xstB[b])
                yield 2.5 if T == TT else ("s", 2.5)

        def load_p(t, T, src_l, tok0, pT, pTB):
            nch = max(1, T // 128)
            rows = min(128, T)
            if nch > 1:
                for a in range(2):
                    dma(SP, t.pstD[a], t.pst[a], src_l[tok0 + a * 256:tok0 + (a + 1) * 256, :].rearrange("(c p) f -> p c f", p=128),
                        writes=[t.pstB[a]])
                pch = [(t.pst[tc // 2][:, tc % 2, :], t.pstB[tc // 2]) for tc in range(nch)]
            else:
                dma(SP, t.pstD[0], t.pst[0][0:rows, 0, :], src_l[tok0:tok0 + rows, :], writes=[t.pstB[0]])
                pch = [(t.pst[0][:, 0, :], t.pstB[0])]
            for fc in range(2):
                bk, bkB = next_bank()
                fns = []
                for tc in range(nch):
                    fns.append(lambda e, tc=tc, fc=fc, bk=bk: e.transpose(
                        bk[:, tc * 128:tc * 128 + rows], pch[tc][0][0:rows, fc * 128:(fc + 1) * 128], ident[0:rows, 0:rows]))
                grp(PE, fns, reads=[b for (_, b) in pch], writes=[bkB])
                op(ACT, lambda e, fc=fc, bk=bk: e.activation(out=pT[:, fc, 0:T], in_=bk[:, 0:T], func=AF.Copy),
                   reads=[bkB], writes=pTB)

        def out_hist(src_ap3, srcB, n, dst):
            if CFG.get("dbg_no_out_hist"):
                return
            bk, bkB = next_bank()
            fns = []
            for m in range(4):
                fns.append(lambda e, m=m, bk=bk: e.transpose(bk[0:n, m * 128:(m + 1) * 128], src_ap3[:, m, :], ident[:, :]))
            grp(PE, fns, reads=srcB, writes=[bkB])
            hs, hsB, hsD = new_stat3()
            op(ACT, lambda e, bk=bk: e.activation(out=hs[0:n, :], in_=bk[0:n, :], func=AF.Copy), reads=[bkB], writes=[hsB])
            dma(SP, hsD, dst, hs[0:n, :], reads=[hsB])

        def load_hist(src, n, dst_ap, dstB):
            hs, hsB, hsD = new_stat3()
            dma(SP, hsD, hs[0:n, :], src, writes=[hsB])
            bk, bkB = next_bank()
            fns = []
            for m in range(4):
                fns.append(lambda e, m=m, bk=bk: e.transpose(bk[:, m * 32:m * 32 + n], hs[0:n, m * 128:(m + 1) * 128], ident[0:n, 0:n]))
            grp(PE, fns, reads=[hsB], writes=[bkB])
            src_v = bk[:, 0:128].rearrange("p (a b) -> p a b", a=4)[:, :, 0:n]
            op(ACT, lambda e: e.activation(out=dst_ap, in_=src_v, func=AF.Copy), reads=[bkB], writes=[dstB])

        hist_flag = set()

        def mixer_half(S, p, l, tiles):
            for (t, T, info) in tiles:
                yield rmsnorm(t, T, C_GMIX, l)
            hs = [None] * len(tiles)
            for q in range(2):
                w, wB = get_unit(S, p, l, "in_ag", q)
                Wa = wview(w, KD, 256); Wg = wview(w, KD, 256, off=KD * 256)
                for (t, T, info) in tiles:
                    hc, hcB = info["hc"][:, l, :, :], info["hcB"][l]
                    for mm in range(2):
                        m = 2 * q + mm
                        bka, bkaB = next_bank()
                        mm_group(bka, bkaB, T, [(Wa[:, k, mm * 128:(mm + 1) * 128], t.h[:, k, 0:T]) for k in range(KD)],
                                 reads=[wB], per=t.hB)
                        bkg, bkgB = next_bank()
                        mm_group(bkg, bkgB, T, [(Wg[:, k, mm * 128:(mm + 1) * 128], t.h[:, k, 0:T]) for k in range(KD)],
                                 reads=[wB], per=t.hB)
                        sg, sgB = new_sig()
                        op(ACT, lambda e, sg=sg, bkg=bkg, T=T: e.activation(out=sg[:, 0:T], in_=bkg[:, 0:T], func=AF.Sigmoid),
                           reads=[bkgB], writes=[sgB])
                        fns = [lambda e, t=t, m=m, bka=bka, sg=sg, T=T: e.tensor_tensor(
                            out=t.cbf[:, m, HC:HC + T], in0=bka[:, 0:T], in1=sg[:, 0:T], op=ALU.mult)]
                        wr = [t.cB[m]]
                        if info["last"]:
                            fns.append(lambda e, m=m, bka=bka, sg=sg, T=T: e.tensor_tensor(
                                out=tail32[:, m, :], in0=bka[:, T - HC:T], in1=sg[:, T - HC:T], op=ALU.mult))
                            wr = wr + [tail32B]
                        grp(DVE, fns, reads=[bkaB, sgB], writes=wr)
                        yield mmcost(16, T)
                    if q == 1:
                        op(ACT, lambda e, t=t, hc=hc: e.activation(out=t.cbf[:, :, 0:HC], in_=hc, func=AF.Copy),
                           reads=[hcB], writes=t.cB)
                        if info["last"]:
                            out_hist(tail32, [tail32B], HC, info["nc_out"][l])
                        else:
                            op(ACT, lambda e, t=t, T=T, hc=hc: e.activation(out=hc, in_=t.cbf[:, :, T:T + HC], func=AF.Copy),
                               reads=t.cB, writes=[hcB])
            w, wB = get_unit(S, p, l, "in_u", 0)
            Wu = wview(w, KD, DC); Wp = wview(w, 4, 128, off=KD * DC)
            for (t, T, info) in tiles:
                hu, huB = info["hu"][:, l, :, :], info["huB"][l]
                op(ACT, lambda e, t=t, hu=hu: e.activation(out=t.u[:, :, 0:HP], in_=hu, func=AF.Copy),
                   reads=[huB], writes=t.uB)
                for m in range(4):
                    bk, bkB = next_bank()
                    mm_group(bk, bkB, T, [(Wu[:, k, m * 128:(m + 1) * 128], t.h[:, k, 0:T]) for k in range(KD)],
                             reads=[wB], per=t.hB)
                    op(ACT, lambda e, t=t, m=m, bk=bk, T=T: e.activation(out=t.u[:, m, HP:HP + T], in_=bk[:, 0:T], func=AF.Copy),
                       reads=[bkB], writes=[t.uB[m]])
                    yield mmcost(8, T)
                op(ACT, lambda e, t=t, T=T, hu=hu: e.activation(out=hu, in_=t.u[:, :, T:T + HP], func=AF.Copy),
                   reads=t.uB, writes=[huB])
                if info["last"]:
                    out_hist(hu, [huB], HP, info["np_out"][l])
            for (t, T, info) in tiles:
                L = HP + T
                for g in range(4):
                    ug = t.u[:, g, :]
                    steps = [(psA, psAB, 1)]
                    if g >= 1:
                        steps.append((psB_, psBB, 2))
                    if g >= 2:
                        steps.append((psA, psAB, 4))
                    if g >= 3:
                        steps.append((psB_, psBB, 8))
                    src, srcB, lo = ug, t.uB[g], 0
                    for (dst, dstB, sh) in steps:
                        nlo = lo + sh
                        op(DVE, lambda e, dst=dst, src=src, nlo=nlo, sh=sh, L=L: e.tensor_tensor(
                            out=dst[:, nlo:L], in0=src[:, nlo:L], in1=src[:, nlo - sh:L - sh], op=ALU.add),
                           reads=[srcB], writes=[dstB])
                        src, srcB, lo = dst, dstB, nlo
                    w_ = POOL_W[g]
                    op(DVE, lambda e, src=src, ug=ug, w_=w_, t=t, g=g, T=T: e.scalar_tensor_tensor(
                        out=t.h[:, g, 0:T], in0=src[:, HP:HP + T], scalar=1.0 / w_, in1=ug[:, HP:HP + T],
                        op0=ALU.mult, op1=ALU.subtract), reads=[srcB, t.uB[g]], writes=[t.hB[g]])
                    if info["first"]:
                        nfix = w_ - 1
                        fx, fxB = new_stat()
                        op(DVE, lambda e, src=src, g=g, nfix=nfix, fx=fx: e.tensor_tensor(
                            out=fx[:, 0:nfix], in0=src[:, HP:HP + nfix], in1=invcnt[:, g, 0:nfix], op=ALU.mult),
                           reads=[srcB], writes=[fxB])
                        op(DVE, lambda e, ug=ug, t=t, g=g, nfix=nfix, fx=fx: e.tensor_tensor(
                            out=t.h[:, g, 0:nfix], in0=fx[:, 0:nfix], in1=ug[:, HP:HP + nfix], op=ALU.subtract),
                           reads=[fxB, t.uB[g]], writes=[t.hB[g]])
                    bk, bkB = next_bank()
                    mm_group(bk, bkB, T, [(Wp[:, g, :], t.h[:, g, 0:T])], reads=[wB, t.hB[g]])
                    op(ACT, lambda e, t=t, g=g, bk=bk, T=T: e.activation(out=t.h[:, 4 + g, 0:T], in_=bk[:, 0:T], func=AF.Copy,
                                                                        scale=tcol(l, C_PS, g)),
                       reads=[bkB], writes=[t.hB[4 + g]])
                    yield (0.65 * (len(steps) + 1)) if T == TT else ("s", 0.23 * (len(steps) + 1))
            def conv_mm(t, T, m):
                bk, bkB = next_bank()
                mm_group(bk, bkB, T, [(diagL[:, m, j, :], t.cbf[:, m, j:j + T]) for j in range(CW)],
                         reads=[diagLB[m], t.cB[m]])
                op(ACT, lambda e: e.activation(out=t.cv[:, m, 0:T], in_=bk[:, 0:T], func=AF.Identity, bias=tcol(l, C_CB, m)),
                   reads=[bkB], writes=[t.hidB[2 * m], t.hidB[2 * m + 1]])

            def ln_pre(t, T):
                op(ACT, lambda e: e.activation(out=lnsc[:, :, 0:T], in_=t.cv[:, :, 0:T], func=AF.Copy),
                   reads=t.hidB, writes=[lnscB])
                op(ACT, lambda e: e.activation(out=t.h[:, 0:4, 0:T], in_=t.cv[:, :, 0:T], func=AF.Square),
                   reads=t.hidB, writes=t.hB[0:4])

            def ln_post(t, T):
                bkm, bkmB = next_bank()
                mm_group(bkm, bkmB, T, [(ones_c[:], lnsc[:, m, 0:T]) for m in range(4)], reads=[lnscB])
                bkq, bkqB = next_bank()
                mm_group(bkq, bkqB, T, [(ones_c[:], t.h[:, m, 0:T]) for m in range(4)], reads=t.hB[0:4])
                mean, meanB = new_stat()
                var, varB = new_stat()
                op(ACT, lambda e: e.activation(out=mean[:, 0:T], in_=bkm[:, 0:T], func=AF.Copy), reads=[bkmB], writes=[meanB])
                op(DVE, lambda e: e.tensor_tensor(out=var[:, 0:T], in0=mean[:, 0:T], in1=mean[:, 0:T], op=ALU.mult),
                   reads=[meanB], writes=[varB])
                op(DVE, lambda e: e.tensor_tensor(out=var[:, 0:T], in0=bkq[:, 0:T], in1=var[:, 0:T], op=ALU.subtract),
                   reads=[bkqB, varB], writes=[varB])
                op(DVE, lambda e: e.tensor_scalar(out=var[:, 0:T], in0=var[:, 0:T], scalar1=0.0, scalar2=EPS, op0=ALU.max, op1=ALU.add),
                   reads=[varB], writes=[varB])
                r, rB = rstd_from(var[:, 0:T], [varB], T, False)
                op(DVE, lambda e: e.scalar_tensor_tensor(out=mean[:, 0:T], in0=mean[:, 0:T], scalar=-1.0, in1=r[:, 0:T],
                                                        op0=ALU.mult, op1=ALU.mult), reads=[meanB, rB], writes=[meanB])
                for m in range(4):
                    hb = [t.hidB[2 * m], t.hidB[2 * m + 1]]
                    op(DVE, lambda e, m=m: e.tensor_tensor(out=t.cv[:, m, 0:T], in0=t.cv[:, m, 0:T], in1=r[:, 0:T], op=ALU.mult),
                       reads=hb + [rB], writes=hb)
                    op(DVE, lambda e, m=m: e.tensor_tensor(out=t.cv[:, m, 0:T], in0=t.cv[:, m, 0:T], in1=mean[:, 0:T], op=ALU.add),
                       reads=hb + [meanB], writes=hb)
                    op(ACT, lambda e, m=m: e.activation(out=t.h[:, m, 0:T], in_=t.cv[:, m, 0:T], func=AF.Silu,
                                                        bias=tcol(l, C_LB, m), scale=tcol(l, C_LG, m)), reads=hb, writes=[t.hB[m]])

            work = [(t, T, m) for (t, T, info) in tiles for m in range(4)]
            prev = None
            for i, (t, T, m) in enumerate(work):
                conv_mm(t, T, m)
                if m == 0 and prev is not None:
                    ln_post(*prev)
                if m == 3:
                    ln_pre(t, T)
                    prev = (t, T)
                yield 8.0
            ln_post(*prev)
            yield 12.0
            for q in range(2):
                w, wB = get_unit(S, p, l, "out", q)
                W = wview(w, KD, 512)
                for (t, T, info) in tiles:
                    for mm in range(4):
                        m = 4 * q + mm
                        bk, bkB = next_bank()
                        mm_group(bk, bkB, T, [(W[:, k, mm * 128:(mm + 1) * 128], t.h[:, k, 0:T]) for k in range(KD)],
                                 reads=[wB], per=t.hB)
                        op(DVE, lambda e, t=t, m=m, bk=bk, T=T: e.tensor_tensor(out=t.x[:, m, 0:T], in0=bk[:, 0:T], in1=t.x[:, m, 0:T], op=ALU.add),
                           reads=[bkB, t.xB[m]], writes=[t.xB[m]])
                        yield mmcost(8, T)

        def ffn_half(S, p, l, tiles):
            for (t, T, info) in tiles:
                yield rmsnorm(t, T, C_GFFN, l)
            for qq in range(NQ):
                if not (p == n_pass - 1 and l == depth - 1):
                    build_diag_chunk((l + 1) % depth, qq)
                for hf in range(2):
                    w, wB = get_unit(S, p, l, "ff1", 2 * qq + hf)
                    W1 = wview(w, KD, 512)
                    for (t, T, info) in tiles:
                        for mm in range(4):
                            m = 4 * hf + mm
                            bk, bkB = next_bank()
                            mm_group(bk, bkB, T, [(W1[:, k, mm * 128:(mm + 1) * 128], t.h[:, k, 0:T]) for k in range(KD)],
                                     reads=[wB], per=t.hB)
                            rl, rlB = new_relu()
                            op(ACT, lambda e, rl=rl, bk=bk, T=T: e.activation(out=rl[:, 0:T], in_=bk[:, 0:T], func=AF.Relu),
                               reads=[bkB], writes=[rlB])
                            op(ACT, lambda e, t=t, m=m, rl=rl, T=T: e.activation(out=t.hid[:, m, 0:T], in_=rl[:, 0:T], func=AF.Square),
                               reads=[rlB], writes=[t.hidB[m]])
                            yield mmcost(8, T)
                for hf in range(2):
                    w, wB = get_unit(S, p, l, "ff2", 2 * qq + hf)
                    W2 = wview(w, KD, 512)
                    for (t, T, info) in tiles:
                        for mm in range(4):
                            m = 4 * hf + mm
                            bk, bkB = next_bank()
                            mm_group(bk, bkB, T, [(W2[:, k, mm * 128:(mm + 1) * 128], t.hid[:, k, 0:T]) for k in range(KD)],
                                     reads=[wB], per=t.hidB)
                            op(DVE, lambda e, t=t, m=m, bk=bk, T=T: e.tensor_tensor(out=t.x[:, m, 0:T], in0=bk[:, 0:T], in1=t.x[:, m, 0:T], op=ALU.add),
                               reads=[bkB, t.xB[m]], writes=[t.xB[m]])
                            yield mmcost(8, T)
            for (t, T, info) in tiles:
                c = rmsnorm(t, T, C_GPLE, l)
                load_p(t, T, info["psrc"][l], info["tok0"], t.cbf, t.cB[0:2])
                yield (c + 2.0) if T == TT else ("s", 7.0)
            for q in range(2):
                w, wB = get_unit(S, p, l, "gp", q)
                Wg = wview(w, KD, 512); Wpl = wview(w, 2, 512, off=KD * 512)
                for (t, T, info) in tiles:
                    for mm in range(4):
                        m = 4 * q + mm
                        bkg, bkgB = next_bank()
                        mm_group(bkg, bkgB, T, [(Wg[:, k, mm * 128:(mm + 1) * 128], t.h[:, k, 0:T]) for k in range(KD)],
                                 reads=[wB], per=t.hB)
                        bkp, bkpB = next_bank()
                        mm_group(bkp, bkpB, T, [(Wpl[:, kc, mm * 128:(mm + 1) * 128], t.cbf[:, kc, 0:T]) for kc in range(2)],
                                 reads=[wB] + t.cB[0:2])
                        sg, sgB = new_sig()
                        op(ACT, lambda e, sg=sg, bkg=bkg, T=T: e.activation(out=sg[:, 0:T], in_=bkg[:, 0:T], func=AF.Sigmoid),
                           reads=[bkgB], writes=[sgB])
                        gt, gtB = new_gtmp()
                        op(DVE, lambda e, gt=gt, sg=sg, bkp=bkp, T=T: e.tensor_tensor(out=gt[:, 0:T], in0=bkp[:, 0:T], in1=sg[:, 0:T], op=ALU.mult),
                           reads=[bkpB, sgB], writes=[gtB])
                        op(DVE, lambda e, t=t, m=m, gt=gt, T=T: e.tensor_tensor(out=t.x[:, m, 0:T], in0=gt[:, 0:T], in1=t.x[:, m, 0:T], op=ALU.add),
                           reads=[gtB, t.xB[m]], writes=[t.xB[m]])
                        yield mmcost(10, T)

        def stream_gen(S):
            for p in range(n_pass):
                tiles = []
                for ti, tile in enumerate((tA, tB)):
                    tok0 = p * 2 * TT + ti * TT
                    info = {"first": tok0 == 0, "last": tok0 + TT == SEQ, "nc_out": ncp, "np_out": npp,
                            "tok0": tok0, "xsrc": xp, "psrc": pp, "ydst": yp,
                            "hc": hist_c[:, 0], "hcB": hist_cB[0], "hu": hist_u[:, 0], "huB": hist_uB[0]}
                    tiles.append((tile, TT, info))
                if p == 0:
                    for l in range(depth):
                        op(DVE, lambda e, l=l: e.memset(hist_c[:, 0, l, :, :], 0.0), writes=[hist_cB[0][l]])
                        op(DVE, lambda e, l=l: e.memset(hist_u[:, 0, l, :, :], 0.0), writes=[hist_uB[0][l]])
                    if tS is not None:
                        infoS = {"first": False, "last": True, "nc_out": ncs, "np_out": nps,
                                 "tok0": 0, "xsrc": xs, "psrc": psm, "ydst": ys,
                                 "hc": hist_c[:, 1], "hcB": hist_cB[1], "hu": hist_u[:, 1], "huB": hist_uB[1]}
                        tiles.append((tS, DEC, infoS))
                        for l in range(depth):
                            load_hist(cc[l], HC, hist_c[:, 1, l, :, :], hist_cB[1][l])
                            load_hist(cpl[l], HP, hist_u[:, 1, l, :, :], hist_uB[1][l])
                for (t, T, inf) in tiles:
                    yield from load_x(t, T, inf["xsrc"], inf["tok0"])
                for l in range(depth):
                    yield from mixer_half(S, p, l, tiles)
                    yield from ffn_half(S, p, l, tiles)
                for (t, T, inf) in tiles:
                    rmsnorm(t, T, None, None, out_f32=True)
                for (t, T, inf) in tiles:
                    yield from store_y(t, T, inf["ydst"], inf["tok0"])

        for m_ in range(4):
            build_diag_chunk(0, m_)
        for _ in stream_gen(streams[0]):
            pass

        fin = [(d_, d_.n) for d_ in new_stat3.sems]
        for t_ in (tA, tB, tS):
            if t_ is not None:
                fin += [(d_, d_.n) for d_ in t_.xstD + t_.pstD]
        SP.wait(fin)
    return nc


_PROGRAM = [None]


def kernel(x_prompt, x_sample, p_prompt, p_sample, cache_conv, cache_pool, w_in, conv_w, conv_b,
           ln_g, ln_b, pool_w, pool_scale, w_out, g_mix, g_ffn, g_ple, w_ff1, w_ff2, w_ple, w_gate, g_final):
    f = lambda a: np.ascontiguousarray(np.asarray(a, dtype=np.float32))
    shared = {"w_in": f(w_in), "conv_w": f(conv_w), "conv_b": f(conv_b), "ln_g": f(ln_g), "ln_b": f(ln_b),
              "pool_w": f(pool_w), "pool_scale": f(pool_scale), "w_out": f(w_out), "g_mix": f(g_mix),
              "g_ffn": f(g_ffn), "g_ple": f(g_ple), "w_ff1": f(w_ff1), "w_ff2": f(w_ff2), "w_ple": f(w_ple),
              "w_gate": f(w_gate), "g_final": f(g_final)}
    in_maps = []
    for i in range(NCORES):
        m = dict(shared)
        m["xp"] = f(x_prompt[i]); m["xs"] = f(x_sample[i])
        m["pp"] = f(p_prompt[:, i]); m["psm"] = f(p_sample[:, i])
        m["cc"] = f(cache_conv[:, i]); m["cpl"] = f(cache_pool[:, i])
        in_maps.append(m)
    nc = build_program()
    res = run_bass_kernel_spmd(nc, in_maps, core_ids=list(range(NCORES)))
    r = res.results
    y_prompt = np.stack([r[i]["yp"] for i in range(NCORES)], 0)
    y_sample = np.stack([r[i]["ys"] for i in range(NCORES)], 0)
    ncp = np.stack([r[i]["ncp"] for i in range(NCORES)], 1)
    npp = np.stack([r[i]["npp"] for i in range(NCORES)], 1)
    ncs = np.stack([r[i]["ncs"] for i in range(NCORES)], 1)
    nps = np.stack([r[i]["nps"] for i in range(NCORES)], 1)
    return (y_prompt.astype(np.float32), y_sample.astype(np.float32), ncp.astype(np.float32),
            npp.astype(np.float32), ncs.astype(np.float32), nps.astype(np.float32))
```

```python
import contextlib
import numpy as np
import concourse.bass as bass
import concourse.mybir as mybir
from concourse.bass_utils import run_bass_kernel_spmd

F32 = mybir.dt.float32
BF16 = mybir.dt.bfloat16
ALU = mybir.AluOpType
AF = mybir.ActivationFunctionType

NCORES = 8
D = 1024
KD = D // 128
SEQ = 4096
DEC = 64
DEPTH = 4
DC = 512
CW = 31
HC = CW - 1
HP = 15
PLE = 256
DFF = 4096
EPS = 1e-6
TT = 512
NQ = 4
SLOT = 5 * 1024
NSLOT = 3
POOL_W = (2, 4, 8, 16)

CFG = {"n_pass": 4, "do_sample": True, "depth": DEPTH}


class SemObj:
    def __init__(self, nc, stack, name):
        self.sem = stack.enter_context(nc.semaphore(name))
        self.n = 0


class Eng(SemObj):
    def __init__(self, nc, stack, eng, name):
        super().__init__(nc, stack, "s_" + name)
        self.eng = eng
        self.seen = {}

    def wait(self, deps):
        best = {}
        for (S, c) in deps:
            if c > best.get(S, 0):
                best[S] = c
        for S, c in best.items():
            if c > self.seen.get(S, 0):
                self.eng.wait_ge(S.sem, c)
                self.seen[S] = c


class Buf:
    __slots__ = ("w", "r", "ro")

    def __init__(self):
        self.w = None
        self.r = {}
        self.ro = False


def _deps(reads, writes):
    deps = []
    for b in reads:
        if b.w is not None:
            deps.append(b.w)
    for b in writes:
        if b.w is not None:
            deps.append(b.w)
        deps.extend(b.r.items())
    return deps


def _commit(tok, reads, writes):
    S, c = tok
    for b in reads:
        if not b.ro:
            b.r[S] = c
    for b in writes:
        b.w = tok
        b.r = {}


def grp(E, fns, reads=(), writes=(), nosame=False):
    deps = _deps(reads, writes)
    if nosame:
        deps = [d for d in deps if d[0] is not E]
    E.wait(deps)
    inst = None
    for fn in fns:
        inst = fn(E.eng)
    E.n += 1
    inst.then_inc(E.sem, 1)
    _commit((E, E.n), reads, writes)


def op(E, fn, reads=(), writes=()):
    grp(E, [fn], reads, writes)


def dma(Q, Dm, out, in_, reads=(), writes=()):
    Q.wait(_deps(reads, writes))
    Q.eng.dma_start(out=out, in_=in_).then_inc(Dm.sem, 16)
    Dm.n += 16
    _commit((Dm, Dm.n), reads, writes)


class TileCtx:
    pass


class Stream:
    pass


def build_program():
    nc = bass.Bass("TRN2", target_bir_lowering=False)
    depth = CFG["depth"]
    n_pass = CFG["n_pass"]

    def din(name, shape):
        return nc.dram_tensor(name, list(shape), F32, kind="ExternalInput").ap()

    def dout(name, shape):
        return nc.dram_tensor(name, list(shape), F32, kind="ExternalOutput").ap()

    xp = din("xp", [SEQ, D]); xs = din("xs", [DEC, D])
    pp = din("pp", [DEPTH, SEQ, PLE]); psm = din("psm", [DEPTH, DEC, PLE])
    cc = din("cc", [DEPTH, HC, DC]); cpl = din("cpl", [DEPTH, HP, DC])
    w_in = din("w_in", [DEPTH, D, 3 * DC]); conv_w = din("conv_w", [DEPTH, CW, DC])
    conv_b = din("conv_b", [DEPTH, DC]); ln_g = din("ln_g", [DEPTH, DC]); ln_b = din("ln_b", [DEPTH, DC])
    pool_w = din("pool_w", [DEPTH, 4, 128, 128]); pool_scale = din("pool_scale", [DEPTH, DC])
    w_out = din("w_out", [DEPTH, D, D]); g_mix = din("g_mix", [DEPTH, D]); g_ffn = din("g_ffn", [DEPTH, D])
    g_ple = din("g_ple", [DEPTH, D]); w_ff1 = din("w_ff1", [DEPTH, D, DFF]); w_ff2 = din("w_ff2", [DEPTH, DFF, D])
    w_ple = din("w_ple", [DEPTH, PLE, D]); w_gate = din("w_gate", [DEPTH, D, D]); g_final = din("g_final", [D])
    yp = dout("yp", [SEQ, D]); ys = dout("ys", [DEC, D])
    ncp = dout("ncp", [DEPTH, HC, DC]); npp = dout("npp", [DEPTH, HP, DC])
    ncs = dout("ncs", [DEPTH, HC, DC]); nps = dout("nps", [DEPTH, HP, DC])

    with contextlib.ExitStack() as st:
        def sb(name, shape, dt):
            return st.enter_context(nc.sbuf_tensor(name, list(shape), dt))

        PE = Eng(nc, st, nc.tensor, "pe"); ACT = Eng(nc, st, nc.scalar, "act")
        DVE = Eng(nc, st, nc.vector, "dve"); POOL = Eng(nc, st, nc.gpsimd, "pool")
        SP = Eng(nc, st, nc.sync, "sp")

        ident = sb("ident", [128, 128], F32); identB = Buf()
        ones_d = sb("ones_d", [128, 128], BF16)
        ones_c = sb("ones_c", [128, 128], BF16)
        epsb = sb("epsb", [128, 1], F32)
        constB = Buf()
        invcnt = sb("invcnt", [128, 4, 16], F32)
        NV = 3 * KD + 4 * 4
        C_GMIX, C_GFFN, C_GPLE, C_CB, C_LG, C_LB, C_PS = 0, 8, 16, 24, 28, 32, 36
        C_GFIN = DEPTH * NV
        C_CW = C_GFIN + KD
        NCOL = C_CW + DEPTH * CW * 4
        tab = sb("tab", [128, NCOL], F32); tabB = Buf()

        def tcol(l, base, k):
            c = l * NV + base + k
            return tab[:, c:c + 1]

        def cwcol(l, j, m):
            c = C_CW + l * CW * 4 + j * 4 + m
            return tab[:, c:c + 1]

        banks = [st.enter_context(nc.psum_tensor("bank%d" % i, [128, 512], F32)) for i in range(8)]
        bankB = [Buf() for _ in range(8)]
        bank_i = [0]

        def next_bank():
            i = bank_i[0]
            bank_i[0] = (i + 1) % 8
            return banks[i], bankB[i]

        def pool_of(name, n, shape, dt, with_sem=False):
            aps = [sb("%s%d" % (name, i), shape, dt) for i in range(n)]
            bufs = [Buf() for _ in range(n)]
            sems = [SemObj(nc, st, "%sd%d" % (name, i)) for i in range(n)] if with_sem else None
            idx = [0]

            def nxt():
                i = idx[0]
                idx[0] = (i + 1) % n
                if with_sem:
                    return aps[i], bufs[i], sems[i]
                return aps[i], bufs[i]
            nxt.sems = sems
            return nxt

        lnsc = sb("lnsc", [128, 4, TT], BF16); lnscB = Buf()
        diagL = sb("diagL", [128, 4, CW, 128], BF16)
        diagLB = [Buf() for _ in range(4)]

        def build_diag_chunk(l, m):
            fns = []
            for j in range(CW):
                fns.append(lambda e, j=j: e.tensor_scalar_mul(out=diagL[:, m, j, :], in0=ident[:, :], scalar1=cwcol(l, j, m)))
            grp(DVE, fns, writes=[diagLB[m]])
        new_stat3 = pool_of("stat", 6, [128, TT], F32, with_sem=True)

        def new_stat():
            a, b, _ = new_stat3()
            return a, b
        new_sig = pool_of("sig", 3, [128, TT], F32)
        new_relu = pool_of("relu", 3, [128, TT], BF16)
        new_gtmp = pool_of("gtmp", 2, [128, TT], F32)
        psA = sb("psA", [128, HP + TT], F32); psB_ = sb("psB", [128, HP + TT], F32)
        psAB = Buf(); psBB = Buf()
        tail32 = sb("tail32", [128, 4, HC], F32); tail32B = Buf()
        hist_c = sb("hist_c", [128, 2, DEPTH, 4, HC], BF16); hist_cB = [[Buf() for _ in range(DEPTH)] for _ in range(2)]
        hist_u = sb("hist_u", [128, 2, DEPTH, 4, HP], F32); hist_uB = [[Buf() for _ in range(DEPTH)] for _ in range(2)]

        def make_tile(name, T):
            t = TileCtx()
            t.T = T
            t.x = sb(name + "_x", [128, KD, T], F32); t.xB = [Buf() for _ in range(KD)]
            t.h = sb(name + "_h", [128, KD, T], BF16); t.hB = [Buf() for _ in range(KD)]
            t.cbf = sb(name + "_c", [128, 4, HC + T], BF16); t.cB = [Buf() for _ in range(4)]
            t.u = sb(name + "_u", [128, 4, HP + T], F32); t.uB = [Buf() for _ in range(4)]
            t.cv = sb(name + "_cv", [128, 4, T], F32)
            t.hid = t.cv.bitcast(BF16).reshape([128, KD, T])
            t.hidB = [Buf() for _ in range(KD)]
            t.xstD = [SemObj(nc, st, name + "_xd%d" % i) for i in range(2)]
            t.pstD = [SemObj(nc, st, name + "_pd%d" % i) for i in range(2)]
            if T == TT:
                t.xst = [t.cv[:, 2 * b:2 * b + 2, :].rearrange("p a t -> p (a t)") for b in range(2)]
                t.xstB = [t.hidB[0:4], t.hidB[4:8]]
                t.pst = [t.u[:, a, 0:512].rearrange("p (b f) -> p b f", f=PLE) for a in range(2)]
                t.pstB = t.uB[0:2]
            else:
                t.xst_t = sb(name + "_xst", [128, D], F32)
                t.xst = [t.xst_t[:, :], t.xst_t[:, :]]
                b_ = Buf(); t.xstB = [[b_], [b_]]
                t.xstD = [t.xstD[0], t.xstD[0]]
                t.pst_t = sb(name + "_pst", [128, 1, PLE], F32)
                t.pst = [t.pst_t[:, :, :]]
                t.pstB = [Buf()]
            return t

        tA = make_tile("tA", TT); tB = make_tile("tB", TT)
        tS = make_tile("tS", DEC) if CFG["do_sample"] else None

        streams = []
        for sid in range(1):
            S = Stream()
            S.sid = sid
            S.ring = [sb("ring%d_%d" % (sid, i), [128, SLOT], BF16) for i in range(NSLOT)]
            S.ringB = [Buf() for _ in range(NSLOT)]
            S.ringD = [SemObj(nc, st, "ringd%d_%d" % (sid, i)) for i in range(NSLOT)]
            S.issued = 0
            streams.append(S)

        setupD = SemObj(nc, st, "setupd")
        op(POOL, lambda e: e.memset(ident[:], 0.0), writes=[identB])
        op(POOL, lambda e: e.affine_select(out=ident[:], in_=ident[:], compare_op=ALU.not_equal, fill=1.0,
                                           base=0, pattern=[[-1, 128]], channel_multiplier=1),
           reads=[identB], writes=[identB])
        grp(POOL, [lambda e: e.memset(ones_d[:], 1.0 / D), lambda e: e.memset(ones_c[:], 1.0 / DC),
                   lambda e: e.memset(epsb[:], EPS)], writes=[constB])
        fns = []
        for g, w in enumerate(POOL_W):
            for pos in range(16):
                v = 1.0 / min(w, pos + 1)
                fns.append(lambda e, g=g, pos=pos, v=v: e.memset(invcnt[:, g, pos:pos + 1], v))
        grp(POOL, fns, writes=[constB])

        stg = tA.x
        stgB = Buf()
        vecs = [(g_mix, C_GMIX, KD), (g_ffn, C_GFFN, KD), (g_ple, C_GPLE, KD), (conv_b, C_CB, 4),
                (ln_g, C_LG, 4), (ln_b, C_LB, 4), (pool_scale, C_PS, 4)]
        for l in range(DEPTH):
            ti, r0 = l // 2, (l % 2) * NV
            for (v, base, nk) in vecs:
                src = v[l].rearrange("(k p) -> k p", p=128)
                dma(SP, setupD, stg[r0 + base:r0 + base + nk, ti, 0:128], src, writes=[stgB])
        dma(SP, setupD, stg[2 * NV:2 * NV + KD, 1, 0:128], g_final.rearrange("(k p) -> k p", p=128), writes=[stgB])
        for l in range(DEPTH):
            dma(SP, setupD, stg[0:CW * 4, 2 + l, 0:128], conv_w[l].rearrange("j (m p) -> (j m) p", p=128), writes=[stgB])
        stgB.w = (setupD, setupD.n)
        tr_plan = [(0, 2 * NV, 0), (1, 2 * NV + KD, 2 * NV)] + [(2 + l, CW * 4, C_CW + l * CW * 4) for l in range(DEPTH)]
        for (ti, rows, col0) in tr_plan:
            bk, bkB = next_bank()
            grp(PE, [lambda e, ti=ti, rows=rows, bk=bk: e.transpose(bk[:, 0:rows], stg[0:rows, ti, 0:128], ident[0:rows, 0:rows])],
                reads=[stgB, identB], writes=[bkB])
            op(ACT, lambda e, rows=rows, col0=col0, bk=bk: e.activation(out=tab[:, col0:col0 + rows], in_=bk[:, 0:rows], func=AF.Copy),
               reads=[bkB], writes=[tabB])
        for E in (PE, ACT, DVE, POOL):
            E.wait([tabB.w, constB.w, identB.w])
        for b in (tabB, constB, identB):
            b.ro = True
        for k in range(KD):
            tA.xB[k].r = dict(stgB.r)
            tA.xB[k].w = stgB.w

        def wv(w2d):
            return w2d.rearrange("(k p) n -> p k n", p=128)

        def unit_srcs(l, kind, q):
            if kind == "in_ag":
                W = wv(w_in[l])
                return [(0, W[:, :, q * 256:(q + 1) * 256], KD, 256),
                        (KD * 256, W[:, :, DC + q * 256:DC + (q + 1) * 256], KD, 256)]
            if kind == "in_u":
                return [(0, wv(w_in[l])[:, :, 2 * DC:3 * DC], KD, DC),
                        (KD * DC, pool_w[l].rearrange("g c d -> c g d"), 4, 128)]
            if kind == "out":
                return [(0, wv(w_out[l])[:, :, q * 512:(q + 1) * 512], KD, 512)]
            if kind == "ff1":
                return [(0, wv(w_ff1[l])[:, :, q * 512:(q + 1) * 512], KD, 512)]
            if kind == "ff2":
                qq, hf = q // 2, q % 2
                return [(0, wv(w_ff2[l][qq * D:(qq + 1) * D, :])[:, :, hf * 512:(hf + 1) * 512], KD, 512)]
            if kind == "gp":
                return [(0, wv(w_gate[l])[:, :, q * 512:(q + 1) * 512], KD, 512),
                        (KD * 512, wv(w_ple[l])[:, :, q * 512:(q + 1) * 512], 2, 512)]
            raise ValueError(kind)

        layer_units = [("in_ag", 0), ("in_ag", 1), ("in_u", 0), ("out", 0), ("out", 1)]
        for qq in range(NQ):
            layer_units += [("ff1", 2 * qq), ("ff1", 2 * qq + 1), ("ff2", 2 * qq), ("ff2", 2 * qq + 1)]
        layer_units += [("gp", 0), ("gp", 1)]
        NU = len(layer_units)
        unit_index = {ku: i for i, ku in enumerate(layer_units)}
        n_units_total = n_pass * depth * NU

        def issue_loads(S, upto):
            upto = min(upto, n_units_total - 1)
            while S.issued <= upto:
                i = S.issued
                l = (i // NU) % depth
                kind, q = layer_units[i % NU]
                s = i % NSLOT
                for (off, src, nk, nn) in unit_srcs(l, kind, q):
                    dst = S.ring[s][:, off:off + nk * nn].rearrange("p (k n) -> p k n", k=nk)
                    half = max(1, nk // 2) if nk * nn >= 2048 else nk
                    for k0 in range(0, nk, half):
                        dma(POOL, S.ringD[s], dst[:, k0:k0 + half, :], src[:, k0:k0 + half, :], writes=[S.ringB[s]])
                S.issued += 1

        def get_unit(S, p, l, kind, q):
            i = (p * depth + l) * NU + unit_index[(kind, q)]
            issue_loads(S, i + NSLOT - 1)
            s = i % NSLOT
            return S.ring[s], S.ringB[s]

        def wview(slot, nk, nn, off=0):
            return slot[:, off:off + nk * nn].rearrange("p (k n) -> p k n", k=nk)

        def mmcost(n, T):
            return n * 0.25 if T == TT else ("s", n * 0.11)

        def mm_group(bk, bkB, T, pairs, reads, per=None):
            n = len(pairs)
            PE.wait(_deps(reads, [bkB]))
            inst = None
            for i, (lhsT, rhs) in enumerate(pairs):
                if per is not None:
                    PE.wait(_deps([per[i]], []))
                inst = PE.eng.matmul(bk[:, 0:T], lhsT=lhsT, rhs=rhs, start=(i == 0), stop=(i == n - 1))
            PE.n += 1
            inst.then_inc(PE.sem, 1)
            _commit((PE, PE.n), list(reads) + (list(per) if per is not None else []), [bkB])

        def rstd_from(src_ap, srcB, T, bias):
            s, sB = new_stat()
            if bias:
                op(ACT, lambda e: e.activation(out=s[:, 0:T], in_=src_ap, func=AF.Sqrt, bias=epsb[:, 0:1]),
                   reads=srcB, writes=[sB])
            else:
                op(ACT, lambda e: e.activation(out=s[:, 0:T], in_=src_ap, func=AF.Sqrt), reads=srcB, writes=[sB])
            r, rB = new_stat()
            op(DVE, lambda e: e.reciprocal(out=r[:, 0:T], in_=s[:, 0:T]), reads=[sB], writes=[rB])
            return r, rB

        def rmsnorm(t, T, gbase, l, out_f32=False):
            for hh in range(2):
                op(ACT, lambda e, hh=hh: e.activation(out=t.h[:, 4 * hh:4 * hh + 4, 0:T], in_=t.x[:, 4 * hh:4 * hh + 4, 0:T], func=AF.Square),
                   reads=t.xB[4 * hh:4 * hh + 4], writes=t.hB[4 * hh:4 * hh + 4])
            bk, bkB = next_bank()
            mm_group(bk, bkB, T, [(ones_d[:], t.h[:, k, 0:T]) for k in range(KD)], reads=[], per=t.hB)
            r, rB = rstd_from(bk[:, 0:T], [bkB], T, True)
            fns = []
            for k in range(KD):
                gcol = tab[:, C_GFIN + k:C_GFIN + k + 1] if l is None else tcol(l, gbase, k)
                dst = t.x[:, k, 0:T] if out_f32 else t.h[:, k, 0:T]
                fns.append(lambda e, k=k, gcol=gcol, dst=dst: e.scalar_tensor_tensor(
                    out=dst, in0=t.x[:, k, 0:T], scalar=gcol, in1=r[:, 0:T], op0=ALU.mult, op1=ALU.mult))
            grp(DVE, fns, reads=[rB] + t.xB, writes=(t.xB if out_f32 else t.hB))
            return 12.0 if T == TT else ("s", 5.0)

        def load_x(t, T, src, tok0):
            nch = max(1, T // 128)
            rows = min(128, T)
            for tc in range(nch):
                b = tc % 2
                dma(SP, t.xstD[b], t.xst[b][0:rows, :], src[tok0 + tc * 128:tok0 + tc * 128 + rows, :], writes=t.xstB[b])
                for half in range(2):
                    bk, bkB = next_bank()
                    fns = []
                    for kk in range(4):
                        k = half * 4 + kk
                        fns.append(lambda e, k=k, kk=kk, b=b, bk=bk: e.transpose(
                            bk[:, kk * 128:kk * 128 + rows], t.xst[b][0:rows, k * 128:(k + 1) * 128], ident[0:rows, 0:rows]))
                    grp(PE, fns, reads=t.xstB[b], writes=[bkB])
                    src_v = bk[:, :].rearrange("p (a b) -> p a b", a=4)[:, :, 0:rows]
                    op(ACT, lambda e, half=half, tc=tc, src_v=src_v: e.activation(
                        out=t.x[:, half * 4:half * 4 + 4, tc * 128:tc * 128 + rows], in_=src_v, func=AF.Copy),
                       reads=[bkB], writes=t.xB[half * 4:half * 4 + 4])
                yield 2.5 if T == TT else ("s", 2.5)

        def store_y(t, T, dst, tok0):
            nch = max(1, T // 128)
            rows = min(128, T)
            for tc in range(nch):
                b = tc % 2
                for half in range(2):
                    bk, bkB = next_bank()
                    fns = []
                    for kk in range(4):
                        k = half * 4 + kk
                        fns.append(lambda e, k=k, kk=kk, bk=bk, tc=tc: e.transpose(
                            bk[0:rows, kk * 128:(kk + 1) * 128], t.x[:, k, tc * 128:tc * 128 + rows], ident[:, :]))
                    grp(PE, fns, reads=t.xB[half * 4:half * 4 + 4], writes=[bkB])
                    op(ACT, lambda e, half=half, b=b, bk=bk: e.activation(
                        out=t.xst[b][0:rows, half * 512:(half + 1) * 512], in_=bk[0:rows, :], func=AF.Copy),
                       reads=[bkB], writes=t.xstB[b])
                dma(SP, t.xstD[b], dst[tok0 + tc * 128:tok0 + tc * 128 + rows, :], t.xst[b][0:rows, :], reads=t.xstB[b])
                yield 2.5 if T == TT else ("s", 2.5)

        def load_p(t, T, src_l, tok0, pT, pTB):
            nch = max(1, T // 128)
            rows = min(128, T)
            if nch > 1:
                for a in range(2):
                    dma(SP, t.pstD[a], t.pst[a], src_l[tok0 + a * 256:tok0 + (a + 1) * 256, :].rearrange("(c p) f -> p c f", p=128),
                        writes=[t.pstB[a]])
                pch = [(t.pst[tc // 2][:, tc % 2, :], t.pstB[tc // 2]) for tc in range(nch)]
            else:
                dma(SP, t.pstD[0], t.pst[0][0:rows, 0, :], src_l[tok0:tok0 + rows, :], writes=[t.pstB[0]])
                pch = [(t.pst[0][:, 0, :], t.pstB[0])]
            for fc in range(2):
                bk, bkB = next_bank()
                fns = []
                for tc in range(nch):
                    fns.append(lambda e, tc=tc, fc=fc, bk=bk: e.transpose(
                        bk[:, tc * 128:tc * 128 + rows], pch[tc][0][0:rows, fc * 128:(fc + 1) * 128], ident[0:rows, 0:rows]))
                grp(PE, fns, reads=[b for (_, b) in pch], writes=[bkB])
                op(ACT, lambda e, fc=fc, bk=bk: e.activation(out=pT[:, fc, 0:T], in_=bk[:, 0:T], func=AF.Copy),
                   reads=[bkB], writes=pTB)

        def out_hist(src_ap3, srcB, n, dst):
            if CFG.get("dbg_no_out_hist"):
                return
            bk, bkB = next_bank()
            fns = []
            for m in range(4):
                fns.append(lambda e, m=m, bk=bk: e.transpose(bk[0:n, m * 128:(m + 1) * 128], src_ap3[:, m, :], ident[:, :]))
            grp(PE, fns, reads=srcB, writes=[bkB])
            hs, hsB, hsD = new_stat3()
            op(ACT, lambda e, bk=bk: e.activation(out=hs[0:n, :], in_=bk[0:n, :], func=AF.Copy), reads=[bkB], writes=[hsB])
            dma(SP, hsD, dst, hs[0:n, :], reads=[hsB])

        def load_hist(src, n, dst_ap, dstB):
            hs, hsB, hsD = new_stat3()
            dma(SP, hsD, hs[0:n, :], src, writes=[hsB])
            bk, bkB = next_bank()
            fns = []
            for m in range(4):
                fns.append(lambda e, m=m, bk=bk: e.transpose(bk[:, m * 32:m * 32 + n], hs[0:n, m * 128:(m + 1) * 128], ident[0:n, 0:n]))
            grp(PE, fns, reads=[hsB], writes=[bkB])
            src_v = bk[:, 0:128].rearrange("p (a b) -> p a b", a=4)[:, :, 0:n]
            op(ACT, lambda e: e.activation(out=dst_ap, in_=src_v, func=AF.Copy), reads=[bkB], writes=[dstB])

        hist_flag = set()

        def mixer_half(S, p, l, tiles):
            for (t, T, info) in tiles:
                yield rmsnorm(t, T, C_GMIX, l)
            hs = [None] * len(tiles)
            for q in range(2):
                w, wB = get_unit(S, p, l, "in_ag", q)
                Wa = wview(w, KD, 256); Wg = wview(w, KD, 256, off=KD * 256)
                for (t, T, info) in tiles:
                    hc, hcB = info["hc"][:, l, :, :], info["hcB"][l]
                    for mm in range(2):
                        m = 2 * q + mm
                        bka, bkaB = next_bank()
                        mm_group(bka, bkaB, T, [(Wa[:, k, mm * 128:(mm + 1) * 128], t.h[:, k, 0:T]) for k in range(KD)],
                                 reads=[wB], per=t.hB)
                        bkg, bkgB = next_bank()
                        mm_group(bkg, bkgB, T, [(Wg[:, k, mm * 128:(mm + 1) * 128], t.h[:, k, 0:T]) for k in range(KD)],
                                 reads=[wB], per=t.hB)
                        sg, sgB = new_sig()
                        op(ACT, lambda e, sg=sg, bkg=bkg, T=T: e.activation(out=sg[:, 0:T], in_=bkg[:, 0:T], func=AF.Sigmoid),
                           reads=[bkgB], writes=[sgB])
                        fns = [lambda e, t=t, m=m, bka=bka, sg=sg, T=T: e.tensor_tensor(
                            out=t.cbf[:, m, HC:HC + T], in0=bka[:, 0:T], in1=sg[:, 0:T], op=ALU.mult)]
                        wr = [t.cB[m]]
                        if info["last"]:
                            fns.append(lambda e, m=m, bka=bka, sg=sg, T=T: e.tensor_tensor(
                                out=tail32[:, m, :], in0=bka[:, T - HC:T], in1=sg[:, T - HC:T], op=ALU.mult))
                            wr = wr + [tail32B]
                        grp(DVE, fns, reads=[bkaB, sgB], writes=wr)
                        yield mmcost(16, T)
                    if q == 1:
                        op(ACT, lambda e, t=t, hc=hc: e.activation(out=t.cbf[:, :, 0:HC], in_=hc, func=AF.Copy),
                           reads=[hcB], writes=t.cB)
                        if info["last"]:
                            out_hist(tail32, [tail32B], HC, info["nc_out"][l])
                        else:
                            op(ACT, lambda e, t=t, T=T, hc=hc: e.activation(out=hc, in_=t.cbf[:, :, T:T + HC], func=AF.Copy),
                               reads=t.cB, writes=[hcB])
            w, wB = get_unit(S, p, l, "in_u", 0)
            Wu = wview(w, KD, DC); Wp = wview(w, 4, 128, off=KD * DC)
            for (t, T, info) in tiles:
                hu, huB = info["hu"][:, l, :, :], info["huB"][l]
                op(ACT, lambda e, t=t, hu=hu: e.activation(out=t.u[:, :, 0:HP], in_=hu, func=AF.Copy),
                   reads=[huB], writes=t.uB)
                for m in range(4):
                    bk, bkB = next_bank()
                    mm_group(bk, bkB, T, [(Wu[:, k, m * 128:(m + 1) * 128], t.h[:, k, 0:T]) for k in range(KD)],
                             reads=[wB], per=t.hB)
                    op(ACT, lambda e, t=t, m=m, bk=bk, T=T: e.activation(out=t.u[:, m, HP:HP + T], in_=bk[:, 0:T], func=AF.Copy),
                       reads=[bkB], writes=[t.uB[m]])
                    yield mmcost(8, T)
                op(ACT, lambda e, t=t, T=T, hu=hu: e.activation(out=hu, in_=t.u[:, :, T:T + HP], func=AF.Copy),
                   reads=t.uB, writes=[huB])
                if info["last"]:
                    out_hist(hu, [huB], HP, info["np_out"][l])
            for (t, T, info) in tiles:
                L = HP + T
                for g in range(4):
                    ug = t.u[:, g, :]
                    steps = [(psA, psAB, 1)]
                    if g >= 1:
                        steps.append((psB_, psBB, 2))
                    if g >= 2:
                        steps.append((psA, psAB, 4))
                    if g >= 3:
                        steps.append((psB_, psBB, 8))
                    src, srcB, lo = ug, t.uB[g], 0
                    for (dst, dstB, sh) in steps:
                        nlo = lo + sh
                        op(DVE, lambda e, dst=dst, src=src, nlo=nlo, sh=sh, L=L: e.tensor_tensor(
                            out=dst[:, nlo:L], in0=src[:, nlo:L], in1=src[:, nlo - sh:L - sh], op=ALU.add),
                           reads=[srcB], writes=[dstB])
                        src, srcB, lo = dst, dstB, nlo
                    w_ = POOL_W[g]
                    op(DVE, lambda e, src=src, ug=ug, w_=w_, t=t, g=g, T=T: e.scalar_tensor_tensor(
                        out=t.h[:, g, 0:T], in0=src[:, HP:HP + T], scalar=1.0 / w_, in1=ug[:, HP:HP + T],
                        op0=ALU.mult, op1=ALU.subtract), reads=[srcB, t.uB[g]], writes=[t.hB[g]])
                    if info["first"]:
                        nfix = w_ - 1
                        fx, fxB = new_stat()
                        op(DVE, lambda e, src=src, g=g, nfix=nfix, fx=fx: e.tensor_tensor(
                            out=fx[:, 0:nfix], in0=src[:, HP:HP + nfix], in1=invcnt[:, g, 0:nfix], op=ALU.mult),
                           reads=[srcB], writes=[fxB])
                        op(DVE, lambda e, ug=ug, t=t, g=g, nfix=nfix, fx=fx: e.tensor_tensor(
                            out=t.h[:, g, 0:nfix], in0=fx[:, 0:nfix], in1=ug[:, HP:HP + nfix], op=ALU.subtract),
                           reads=[fxB, t.uB[g]], writes=[t.hB[g]])
                    bk, bkB = next_bank()
                    mm_group(bk, bkB, T, [(Wp[:, g, :], t.h[:, g, 0:T])], reads=[wB, t.hB[g]])
                    op(ACT, lambda e, t=t, g=g, bk=bk, T=T: e.activation(out=t.h[:, 4 + g, 0:T], in_=bk[:, 0:T], func=AF.Copy,
                                                                        scale=tcol(l, C_PS, g)),
                       reads=[bkB], writes=[t.hB[4 + g]])
                    yield (0.65 * (len(steps) + 1)) if T == TT else ("s", 0.23 * (len(steps) + 1))
            def conv_mm(t, T, m):
                bk, bkB = next_bank()
                mm_group(bk, bkB, T, [(diagL[:, m, j, :], t.cbf[:, m, j:j + T]) for j in range(CW)],
                         reads=[diagLB[m], t.cB[m]])
                op(ACT, lambda e: e.activation(out=t.cv[:, m, 0:T], in_=bk[:, 0:T], func=AF.Identity, bias=tcol(l, C_CB, m)),
                   reads=[bkB], writes=[t.hidB[2 * m], t.hidB[2 * m + 1]])

            def ln_pre(t, T):
                op(ACT, lambda e: e.activation(out=lnsc[:, :, 0:T], in_=t.cv[:, :, 0:T], func=AF.Copy),
                   reads=t.hidB, writes=[lnscB])
                op(ACT, lambda e: e.activation(out=t.h[:, 0:4, 0:T], in_=t.cv[:, :, 0:T], func=AF.Square),
                   reads=t.hidB, writes=t.hB[0:4])

            def ln_post(t, T):
                bkm, bkmB = next_bank()
                mm_group(bkm, bkmB, T, [(ones_c[:], lnsc[:, m, 0:T]) for m in range(4)], reads=[lnscB])
                bkq, bkqB = next_bank()
                mm_group(bkq, bkqB, T, [(ones_c[:], t.h[:, m, 0:T]) for m in range(4)], reads=t.hB[0:4])
                mean, meanB = new_stat()
                var, varB = new_stat()
                op(ACT, lambda e: e.activation(out=mean[:, 0:T], in_=bkm[:, 0:T], func=AF.Copy), reads=[bkmB], writes=[meanB])
                op(DVE, lambda e: e.tensor_tensor(out=var[:, 0:T], in0=mean[:, 0:T], in1=mean[:, 0:T], op=ALU.mult),
                   reads=[meanB], writes=[varB])
                op(DVE, lambda e: e.tensor_tensor(out=var[:, 0:T], in0=bkq[:, 0:T], in1=var[:, 0:T], op=ALU.subtract),
                   reads=[bkqB, varB], writes=[varB])
                op(DVE, lambda e: e.tensor_scalar(out=var[:, 0:T], in0=var[:, 0:T], scalar1=0.0, scalar2=EPS, op0=ALU.max, op1=ALU.add),
                   reads=[varB], writes=[varB])
                r, rB = rstd_from(var[:, 0:T], [varB], T, False)
                op(DVE, lambda e: e.scalar_tensor_tensor(out=mean[:, 0:T], in0=mean[:, 0:T], scalar=-1.0, in1=r[:, 0:T],
                                                        op0=ALU.mult, op1=ALU.mult), reads=[meanB, rB], writes=[meanB])
                for m in range(4):
                    hb = [t.hidB[2 * m], t.hidB[2 * m + 1]]
                    op(DVE, lambda e, m=m: e.tensor_tensor(out=t.cv[:, m, 0:T], in0=t.cv[:, m, 0:T], in1=r[:, 0:T], op=ALU.mult),
                       reads=hb + [rB], writes=hb)
                    op(DVE, lambda e, m=m: e.tensor_tensor(out=t.cv[:, m, 0:T], in0=t.cv[:, m, 0:T], in1=mean[:, 0:T], op=ALU.add),
                       reads=hb + [meanB], writes=hb)
                    op(ACT, lambda e, m=m: e.activation(out=t.h[:, m, 0:T], in_=t.cv[:, m, 0:T], func=AF.Silu,
                                                        bias=tcol(l, C_LB, m), scale=tcol(l, C_LG, m)), reads=hb, writes=[t.hB[m]])

            work = [(t, T, m) for (t, T, info) in tiles for m in range(4)]
            prev = None
            for i, (t, T, m) in enumerate(work):
                conv_mm(t, T, m)
                if m == 0 and prev is not None:
                    ln_post(*prev)
                if m == 3:
                    ln_pre(t, T)
                    prev = (t, T)
                yield 8.0
            ln_post(*prev)
            yield 12.0
            for q in range(2):
                w, wB = get_unit(S, p, l, "out", q)
                W = wview(w, KD, 512)
                for (t, T, info) in tiles:
                    for mm in range(4):
                        m = 4 * q + mm
                        bk, bkB = next_bank()
                        mm_group(bk, bkB, T, [(W[:, k, mm * 128:(mm + 1) * 128], t.h[:, k, 0:T]) for k in range(KD)],
                                 reads=[wB], per=t.hB)
                        op(DVE, lambda e, t=t, m=m, bk=bk, T=T: e.tensor_tensor(out=t.x[:, m, 0:T], in0=bk[:, 0:T], in1=t.x[:, m, 0:T], op=ALU.add),
                           reads=[bkB, t.xB[m]], writes=[t.xB[m]])
                        yield mmcost(8, T)

        def ffn_half(S, p, l, tiles):
            for (t, T, info) in tiles:
                yield rmsnorm(t, T, C_GFFN, l)
            for qq in range(NQ):
                if not (p == n_pass - 1 and l == depth - 1):
                    build_diag_chunk((l + 1) % depth, qq)
                for hf in range(2):
                    w, wB = get_unit(S, p, l, "ff1", 2 * qq + hf)
                    W1 = wview(w, KD, 512)
                    for (t, T, info) in tiles:
                        for mm in range(4):
                            m = 4 * hf + mm
                            bk, bkB = next_bank()
                            mm_group(bk, bkB, T, [(W1[:, k, mm * 128:(mm + 1) * 128], t.h[:, k, 0:T]) for k in range(KD)],
                                     reads=[wB], per=t.hB)
                            rl, rlB = new_relu()
                            op(ACT, lambda e, rl=rl, bk=bk, T=T: e.activation(out=rl[:, 0:T], in_=bk[:, 0:T], func=AF.Relu),
                               reads=[bkB], writes=[rlB])
                            op(ACT, lambda e, t=t, m=m, rl=rl, T=T: e.activation(out=t.hid[:, m, 0:T], in_=rl[:, 0:T], func=AF.Square),
                               reads=[rlB], writes=[t.hidB[m]])
                            yield mmcost(8, T)
                for hf in range(2):
                    w, wB = get_unit(S, p, l, "ff2", 2 * qq + hf)
                    W2 = wview(w, KD, 512)
                    for (t, T, info) in tiles:
                        for mm in range(4):
                            m = 4 * hf + mm
                            bk, bkB = next_bank()
                            mm_group(bk, bkB, T, [(W2[:, k, mm * 128:(mm + 1) * 128], t.hid[:, k, 0:T]) for k in range(KD)],
                                     reads=[wB], per=t.hidB)
                            op(DVE, lambda e, t=t, m=m, bk=bk, T=T: e.tensor_tensor(out=t.x[:, m, 0:T], in0=bk[:, 0:T], in1=t.x[:, m, 0:T], op=ALU.add),
                               reads=[bkB, t.xB[m]], writes=[t.xB[m]])
                            yield mmcost(8, T)
            for (t, T, info) in tiles:
                c = rmsnorm(t, T, C_GPLE, l)
                load_p(t, T, info["psrc"][l], info["tok0"], t.cbf, t.cB[0:2])
                yield (c + 2.0) if T == TT else ("s", 7.0)
            for q in range(2):
                w, wB = get_unit(S, p, l, "gp", q)
                Wg = wview(w, KD, 512); Wpl = wview(w, 2, 512, off=KD * 512)
                for (t, T, info) in tiles:
                    for mm in range(4):
                        m = 4 * q + mm
                        bkg, bkgB = next_bank()
                        mm_group(bkg, bkgB, T, [(Wg[:, k, mm * 128:(mm + 1) * 128], t.h[:, k, 0:T]) for k in range(KD)],
                                 reads=[wB], per=t.hB)
                        bkp, bkpB = next_bank()
                        mm_group(bkp, bkpB, T, [(Wpl[:, kc, mm * 128:(mm + 1) * 128], t.cbf[:, kc, 0:T]) for kc in range(2)],
                                 reads=[wB] + t.cB[0:2])
                        sg, sgB = new_sig()
                        op(ACT, lambda e, sg=sg, bkg=bkg, T=T: e.activation(out=sg[:, 0:T], in_=bkg[:, 0:T], func=AF.Sigmoid),
                           reads=[bkgB], writes=[sgB])
                        gt, gtB = new_gtmp()
                        op(DVE, lambda e, gt=gt, sg=sg, bkp=bkp, T=T: e.tensor_tensor(out=gt[:, 0:T], in0=bkp[:, 0:T], in1=sg[:, 0:T], op=ALU.mult),
                           reads=[bkpB, sgB], writes=[gtB])
                        op(DVE, lambda e, t=t, m=m, gt=gt, T=T: e.tensor_tensor(out=t.x[:, m, 0:T], in0=gt[:, 0:T], in1=t.x[:, m, 0:T], op=ALU.add),
                           reads=[gtB, t.xB[m]], writes=[t.xB[m]])
                        yield mmcost(10, T)

        def stream_gen(S):
            for p in range(n_pass):
                tiles = []
                for ti, tile in enumerate((tA, tB)):
                    tok0 = p * 2 * TT + ti * TT
                    info = {"first": tok0 == 0, "last": tok0 + TT == SEQ, "nc_out": ncp, "np_out": npp,
                            "tok0": tok0, "xsrc": xp, "psrc": pp, "ydst": yp,
                            "hc": hist_c[:, 0], "hcB": hist_cB[0], "hu": hist_u[:, 0], "huB": hist_uB[0]}
                    tiles.append((tile, TT, info))
                if p == 0:
                    for l in range(depth):
                        op(DVE, lambda e, l=l: e.memset(hist_c[:, 0, l, :, :], 0.0), writes=[hist_cB[0][l]])
                        op(DVE, lambda e, l=l: e.memset(hist_u[:, 0, l, :, :], 0.0), writes=[hist_uB[0][l]])
                    if tS is not None:
                        infoS = {"first": False, "last": True, "nc_out": ncs, "np_out": nps,
                                 "tok0": 0, "xsrc": xs, "psrc": psm, "ydst": ys,
                                 "hc": hist_c[:, 1], "hcB": hist_cB[1], "hu": hist_u[:, 1], "huB": hist_uB[1]}
                        tiles.append((tS, DEC, infoS))
                        for l in range(depth):
                            load_hist(cc[l], HC, hist_c[:, 1, l, :, :], hist_cB[1][l])
                            load_hist(cpl[l], HP, hist_u[:, 1, l, :, :], hist_uB[1][l])
                for (t, T, inf) in tiles:
                    yield from load_x(t, T, inf["xsrc"], inf["tok0"])
                for l in range(depth):
                    yield from mixer_half(S, p, l, tiles)
                    yield from ffn_half(S, p, l, tiles)
                for (t, T, inf) in tiles:
                    rmsnorm(t, T, None, None, out_f32=True)
                for (t, T, inf) in tiles:
                    yield from store_y(t, T, inf["ydst"], inf["tok0"])

        for m_ in range(4):
            build_diag_chunk(0, m_)
        for _ in stream_gen(streams[0]):
            pass

        fin = [(d_, d_.n) for d_ in new_stat3.sems]
        for t_ in (tA, tB, tS):
            if t_ is not None:
                fin += [(d_, d_.n) for d_ in t_.xstD + t_.pstD]
        SP.wait(fin)
    return nc


_PROGRAM = [None]


def kernel(x_prompt, x_sample, p_prompt, p_sample, cache_conv, cache_pool, w_in, conv_w, conv_b,
           ln_g, ln_b, pool_w, pool_scale, w_out, g_mix, g_ffn, g_ple, w_ff1, w_ff2, w_ple, w_gate, g_final):
    f = lambda a: np.ascontiguousarray(np.asarray(a, dtype=np.float32))
    shared = {"w_in": f(w_in), "conv_w": f(conv_w), "conv_b": f(conv_b), "ln_g": f(ln_g), "ln_b": f(ln_b),
              "pool_w": f(pool_w), "pool_scale": f(pool_scale), "w_out": f(w_out), "g_mix": f(g_mix),
              "g_ffn": f(g_ffn), "g_ple": f(g_ple), "w_ff1": f(w_ff1), "w_ff2": f(w_ff2), "w_ple": f(w_ple),
              "w_gate": f(w_gate), "g_final": f(g_final)}
    in_maps = []
    for i in range(NCORES):
        m = dict(shared)
        m["xp"] = f(x_prompt[i]); m["xs"] = f(x_sample[i])
        m["pp"] = f(p_prompt[:, i]); m["psm"] = f(p_sample[:, i])
        m["cc"] = f(cache_conv[:, i]); m["cpl"] = f(cache_pool[:, i])
        in_maps.append(m)
    nc = build_program()
    res = run_bass_kernel_spmd(nc, in_maps, core_ids=list(range(NCORES)))
    r = res.results
    y_prompt = np.stack([r[i]["yp"] for i in range(NCORES)], 0)
    y_sample = np.stack([r[i]["ys"] for i in range(NCORES)], 0)
    ncp = np.stack([r[i]["ncp"] for i in range(NCORES)], 1)
    npp = np.stack([r[i]["npp"] for i in range(NCORES)], 1)
    ncs = np.stack([r[i]["ncs"] for i in range(NCORES)], 1)
    nps = np.stack([r[i]["nps"] for i in range(NCORES)], 1)
    return (y_prompt.astype(np.float32), y_sample.astype(np.float32), ncp.astype(np.float32),
            npp.astype(np.float32), ncs.astype(np.float32), nps.astype(np.float32))
```

```python
import contextlib
import numpy as np
import concourse.bass as bass
import concourse.mybir as mybir
from concourse.bass_utils import run_bass_kernel_spmd

F32 = mybir.dt.float32
BF16 = mybir.dt.bfloat16
ALU = mybir.AluOpType
AF = mybir.ActivationFunctionType

NCORES = 8
D = 1024
KD = D // 128
SEQ = 4096
DEC = 64
DEPTH = 4
DC = 512
CW = 31
HC = CW - 1
HP = 15
PLE = 256
DFF = 4096
EPS = 1e-6
TT = 512
NQ = 4
SLOT = 5 * 1024
NSLOT = 3
POOL_W = (2, 4, 8, 16)

CFG = {"n_pass": 4, "do_sample": True, "depth": DEPTH}


class SemObj:
    def __init__(self, nc, stack, name):
        self.sem = stack.enter_context(nc.semaphore(name))
        self.n = 0


class Eng(SemObj):
    def __init__(self, nc, stack, eng, name):
        super().__init__(nc, stack, "s_" + name)
        self.eng = eng
        self.seen = {}

    def wait(self, deps):
        best = {}
        for (S, c) in deps:
            if c > best.get(S, 0):
                best[S] = c
        for S, c in best.items():
            if c > self.seen.get(S, 0):
                self.eng.wait_ge(S.sem, c)
                self.seen[S] = c


class Buf:
    __slots__ = ("w", "r", "ro")

    def __init__(self):
        self.w = None
        self.r = {}
        self.ro = False


def _deps(reads, writes):
    deps = []
    for b in reads:
        if b.w is not None:
            deps.append(b.w)
    for b in writes:
        if b.w is not None:
            deps.append(b.w)
        deps.extend(b.r.items())
    return deps


def _commit(tok, reads, writes):
    S, c = tok
    for b in reads:
        if not b.ro:
            b.r[S] = c
    for b in writes:
        b.w = tok
        b.r = {}


def grp(E, fns, reads=(), writes=(), nosame=False):
    deps = _deps(reads, writes)
    if nosame:
        deps = [d for d in deps if d[0] is not E]
    E.wait(deps)
    inst = None
    for fn in fns:
        inst = fn(E.eng)
    E.n += 1
    inst.then_inc(E.sem, 1)
    _commit((E, E.n), reads, writes)


def op(E, fn, reads=(), writes=()):
    grp(E, [fn], reads, writes)


def dma(Q, Dm, out, in_, reads=(), writes=()):
    Q.wait(_deps(reads, writes))
    Q.eng.dma_start(out=out, in_=in_).then_inc(Dm.sem, 16)
    Dm.n += 16
    _commit((Dm, Dm.n), reads, writes)


class TileCtx:
    pass


class Stream:
    pass


def build_program():
    nc = bass.Bass("TRN2", target_bir_lowering=False)
    depth = CFG["depth"]
    n_pass = CFG["n_pass"]

    def din(name, shape):
        return nc.dram_tensor(name, list(shape), F32, kind="ExternalInput").ap()

    def dout(name, shape):
        return nc.dram_tensor(name, list(shape), F32, kind="ExternalOutput").ap()

    xp = din("xp", [SEQ, D]); xs = din("xs", [DEC, D])
    pp = din("pp", [DEPTH, SEQ, PLE]); psm = din("psm", [DEPTH, DEC, PLE])
    cc = din("cc", [DEPTH, HC, DC]); cpl = din("cpl", [DEPTH, HP, DC])
    w_in = din("w_in", [DEPTH, D, 3 * DC]); conv_w = din("conv_w", [DEPTH, CW, DC])
    conv_b = din("conv_b", [DEPTH, DC]); ln_g = din("ln_g", [DEPTH, DC]); ln_b = din("ln_b", [DEPTH, DC])
    pool_w = din("pool_w", [DEPTH, 4, 128, 128]); pool_scale = din("pool_scale", [DEPTH, DC])
    w_out = din("w_out", [DEPTH, D, D]); g_mix = din("g_mix", [DEPTH, D]); g_ffn = din("g_ffn", [DEPTH, D])
    g_ple = din("g_ple", [DEPTH, D]); w_ff1 = din("w_ff1", [DEPTH, D, DFF]); w_ff2 = din("w_ff2", [DEPTH, DFF, D])
    w_ple = din("w_ple", [DEPTH, PLE, D]); w_gate = din("w_gate", [DEPTH, D, D]); g_final = din("g_final", [D])
    yp = dout("yp", [SEQ, D]); ys = dout("ys", [DEC, D])
    ncp = dout("ncp", [DEPTH, HC, DC]); npp = dout("npp", [DEPTH, HP, DC])
    ncs = dout("ncs", [DEPTH, HC, DC]); nps = dout("nps", [DEPTH, HP, DC])

    with contextlib.ExitStack() as st:
        def sb(name, shape, dt):
            return st.enter_context(nc.sbuf_tensor(name, list(shape), dt))

        PE = Eng(nc, st, nc.tensor, "pe"); ACT = Eng(nc, st, nc.scalar, "act")
        DVE = Eng(nc, st, nc.vector, "dve"); POOL = Eng(nc, st, nc.gpsimd, "pool")
        SP = Eng(nc, st, nc.sync, "sp")

        ident = sb("ident", [128, 128], F32); identB = Buf()
        ones_d = sb("ones_d", [128, 128], BF16)
        ones_c = sb("ones_c", [128, 128], BF16)
        epsb = sb("epsb", [128, 1], F32)
        constB = Buf()
        invcnt = sb("invcnt", [128, 4, 16], F32)
        NV = 3 * KD + 4 * 4
        C_GMIX, C_GFFN, C_GPLE, C_CB, C_LG, C_LB, C_PS = 0, 8, 16, 24, 28, 32, 36
        C_GFIN = DEPTH * NV
        C_CW = C_GFIN + KD
        NCOL = C_CW + DEPTH * CW * 4
        tab = sb("tab", [128, NCOL], F32); tabB = Buf()

        def tcol(l, base, k):
            c = l * NV + base + k
            return tab[:, c:c + 1]

        def cwcol(l, j, m):
            c = C_CW + l * CW * 4 + j * 4 + m
            return tab[:, c:c + 1]

        banks = [st.enter_context(nc.psum_tensor("bank%d" % i, [128, 512], F32)) for i in range(8)]
        bankB = [Buf() for _ in range(8)]
        bank_i = [0]

        def next_bank():
            i = bank_i[0]
            bank_i[0] = (i + 1) % 8
            return banks[i], bankB[i]

        def pool_of(name, n, shape, dt, with_sem=False):
            aps = [sb("%s%d" % (name, i), shape, dt) for i in range(n)]
            bufs = [Buf() for _ in range(n)]
            sems = [SemObj(nc, st, "%sd%d" % (name, i)) for i in range(n)] if with_sem else None
            idx = [0]

            def nxt():
                i = idx[0]
                idx[0] = (i + 1) % n
                if with_sem:
                    return aps[i], bufs[i], sems[i]
                return aps[i], bufs[i]
            nxt.sems = sems
            return nxt

        lnsc = sb("lnsc", [128, 4, TT], BF16); lnscB = Buf()
        diagL = sb("diagL", [128, 4, CW, 128], BF16)
        diagLB = [Buf() for _ in range(4)]

        def build_diag_chunk(l, m):
            fns = []
            for j in range(CW):
                fns.append(lambda e, j=j: e.tensor_scalar_mul(out=diagL[:, m, j, :], in0=ident[:, :], scalar1=cwcol(l, j, m)))
            grp(DVE, fns, writes=[diagLB[m]])
        new_stat3 = pool_of("stat", 6, [128, TT], F32, with_sem=True)

        def new_stat():
            a, b, _ = new_stat3()
            return a, b
        new_sig = pool_of("sig", 3, [128, TT], F32)
        new_relu = pool_of("relu", 3, [128, TT], BF16)
        new_gtmp = pool_of("gtmp", 2, [128, TT], F32)
        psA = sb("psA", [128, HP + TT], F32); psB_ = sb("psB", [128, HP + TT], F32)
        psAB = Buf(); psBB = Buf()
        tail32 = sb("tail32", [128, 4, HC], F32); tail32B = Buf()
        hist_c = sb("hist_c", [128, 2, DEPTH, 4, HC], BF16); hist_cB = [[Buf() for _ in range(DEPTH)] for _ in range(2)]
        hist_u = sb("hist_u", [128, 2, DEPTH, 4, HP], F32); hist_uB = [[Buf() for _ in range(DEPTH)] for _ in range(2)]

        def make_tile(name, T):
            t = TileCtx()
            t.T = T
            t.x = sb(name + "_x", [128, KD, T], F32); t.xB = [Buf() for _ in range(KD)]
            t.h = sb(name + "_h", [128, KD, T], BF16); t.hB = [Buf() for _ in range(KD)]
            t.cbf = sb(name + "_c", [128, 4, HC + T], BF16); t.cB = [Buf() for _ in range(4)]
            t.u = sb(name + "_u", [128, 4, HP + T], F32); t.uB = [Buf() for _ in range(4)]
            t.cv = sb(name + "_cv", [128, 4, T], F32)
            t.hid = t.cv.bitcast(BF16).reshape([128, KD, T])
            t.hidB = [Buf() for _ in range(KD)]
            t.xstD = [SemObj(nc, st, name + "_xd%d" % i) for i in range(2)]
            t.pstD = [SemObj(nc, st, name + "_pd%d" % i) for i in range(2)]
            if T == TT:
                t.xst = [t.cv[:, 2 * b:2 * b + 2, :].rearrange("p a t -> p (a t)") for b in range(2)]
                t.xstB = [t.hidB[0:4], t.hidB[4:8]]
                t.pst = [t.u[:, a, 0:512].rearrange("p (b f) -> p b f", f=PLE) for a in range(2)]
                t.pstB = t.uB[0:2]
            else:
                t.xst_t = sb(name + "_xst", [128, D], F32)
                t.xst = [t.xst_t[:, :], t.xst_t[:, :]]
                b_ = Buf(); t.xstB = [[b_], [b_]]
                t.xstD = [t.xstD[0], t.xstD[0]]
                t.pst_t = sb(name + "_pst", [128, 1, PLE], F32)
                t.pst = [t.pst_t[:, :, :]]
                t.pstB = [Buf()]
            return t

        tA = make_tile("tA", TT); tB = make_tile("tB", TT)
        tS = make_tile("tS", DEC) if CFG["do_sample"] else None

        streams = []
        for sid in range(1):
            S = Stream()
            S.sid = sid
            S.ring = [sb("ring%d_%d" % (sid, i), [128, SLOT], BF16) for i in range(NSLOT)]
            S.ringB = [Buf() for _ in range(NSLOT)]
            S.ringD = [SemObj(nc, st, "ringd%d_%d" % (sid, i)) for i in range(NSLOT)]
            S.issued = 0
            streams.append(S)

        setupD = SemObj(nc, st, "setupd")
        op(POOL, lambda e: e.memset(ident[:], 0.0), writes=[identB])
        op(POOL, lambda e: e.affine_select(out=ident[:], in_=ident[:], compare_op=ALU.not_equal, fill=1.0,
                                           base=0, pattern=[[-1, 128]], channel_multiplier=1),
           reads=[identB], writes=[identB])
        grp(POOL, [lambda e: e.memset(ones_d[:], 1.0 / D), lambda e: e.memset(ones_c[:], 1.0 / DC),
                   lambda e: e.memset(epsb[:], EPS)], writes=[constB])
        fns = []
        for g, w in enumerate(POOL_W):
            for pos in range(16):
                v = 1.0 / min(w, pos + 1)
                fns.append(lambda e, g=g, pos=pos, v=v: e.memset(invcnt[:, g, pos:pos + 1], v))
        grp(POOL, fns, writes=[constB])

        stg = tA.x
        stgB = Buf()
        vecs = [(g_mix, C_GMIX, KD), (g_ffn, C_GFFN, KD), (g_ple, C_GPLE, KD), (conv_b, C_CB, 4),
                (ln_g, C_LG, 4), (ln_b, C_LB, 4), (pool_scale, C_PS, 4)]
        for l in range(DEPTH):
            ti, r0 = l // 2, (l % 2) * NV
            for (v, base, nk) in vecs:
                src = v[l].rearrange("(k p) -> k p", p=128)
                dma(SP, setupD, stg[r0 + base:r0 + base + nk, ti, 0:128], src, writes=[stgB])
        dma(SP, setupD, stg[2 * NV:2 * NV + KD, 1, 0:128], g_final.rearrange("(k p) -> k p", p=128), writes=[stgB])
        for l in range(DEPTH):
            dma(SP, setupD, stg[0:CW * 4, 2 + l, 0:128], conv_w[l].rearrange("j (m p) -> (j m) p", p=128), writes=[stgB])
        stgB.w = (setupD, setupD.n)
        tr_plan = [(0, 2 * NV, 0), (1, 2 * NV + KD, 2 * NV)] + [(2 + l, CW * 4, C_CW + l * CW * 4) for l in range(DEPTH)]
        for (ti, rows, col0) in tr_plan:
            bk, bkB = next_bank()
            grp(PE, [lambda e, ti=ti, rows=rows, bk=bk: e.transpose(bk[:, 0:rows], stg[0:rows, ti, 0:128], ident[0:rows, 0:rows])],
                reads=[stgB, identB], writes=[bkB])
            op(ACT, lambda e, rows=rows, col0=col0, bk=bk: e.activation(out=tab[:, col0:col0 + rows], in_=bk[:, 0:rows], func=AF.Copy),
               reads=[bkB], writes=[tabB])
        for E in (PE, ACT, DVE, POOL):
            E.wait([tabB.w, constB.w, identB.w])
        for b in (tabB, constB, identB):
            b.ro = True
        for k in range(KD):
            tA.xB[k].r = dict(stgB.r)
            tA.xB[k].w = stgB.w

        def wv(w2d):
            return w2d.rearrange("(k p) n -> p k n", p=128)

        def unit_srcs(l, kind, q):
            if kind == "in_ag":
                W = wv(w_in[l])
                return [(0, W[:, :, q * 256:(q + 1) * 256], KD, 256),
                        (KD * 256, W[:, :, DC + q * 256:DC + (q + 1) * 256], KD, 256)]
            if kind == "in_u":
                return [(0, wv(w_in[l])[:, :, 2 * DC:3 * DC], KD, DC),
                        (KD * DC, pool_w[l].rearrange("g c d -> c g d"), 4, 128)]
            if kind == "out":
                return [(0, wv(w_out[l])[:, :, q * 512:(q + 1) * 512], KD, 512)]
            if kind == "ff1":
                return [(0, wv(w_ff1[l])[:, :, q * 512:(q + 1) * 512], KD, 512)]
            if kind == "ff2":
                qq, hf = q // 2, q % 2
                return [(0, wv(w_ff2[l][qq * D:(qq + 1) * D, :])[:, :, hf * 512:(hf + 1) * 512], KD, 512)]
            if kind == "gp":
                return [(0, wv(w_gate[l])[:, :, q * 512:(q + 1) * 512], KD, 512),
                        (KD * 512, wv(w_ple[l])[:, :, q * 512:(q + 1) * 512], 2, 512)]
            raise ValueError(kind)

        layer_units = [("in_ag", 0), ("in_ag", 1), ("in_u", 0), ("out", 0), ("out", 1)]
        for qq in range(NQ):
            layer_units += [("ff1", 2 * qq), ("ff1", 2 * qq + 1), ("ff2", 2 * qq), ("ff2", 2 * qq + 1)]
        layer_units += [("gp", 0), ("gp", 1)]
        NU = len(layer_units)
        unit_index = {ku: i for i, ku in enumerate(layer_units)}
        n_units_total = n_pass * depth * NU

        def issue_loads(S, upto):
            upto = min(upto, n_units_total - 1)
            while S.issued <= upto:
                i = S.issued
                l = (i // NU) % depth
                kind, q = layer_units[i % NU]
                s = i % NSLOT
                for (off, src, nk, nn) in unit_srcs(l, kind, q):
                    dst = S.ring[s][:, off:off + nk * nn].rearrange("p (k n) -> p k n", k=nk)
                    half = max(1, nk // 2) if nk * nn >= 2048 else nk
                    for k0 in range(0, nk, half):
                        dma(POOL, S.ringD[s], dst[:, k0:k0 + half, :], src[:, k0:k0 + half, :], writes=[S.ringB[s]])
                S.issued += 1

        def get_unit(S, p, l, kind, q):
            i = (p * depth + l) * NU + unit_index[(kind, q)]
            issue_loads(S, i + NSLOT - 1)
            s = i % NSLOT
            return S.ring[s], S.ringB[s]

        def wview(slot, nk, nn, off=0):
            return slot[:, off:off + nk * nn].rearrange("p (k n) -> p k n", k=nk)

        def mmcost(n, T):
            return n * 0.25 if T == TT else ("s", n * 0.11)

        def mm_group(bk, bkB, T, pairs, reads, per=None):
            n = len(pairs)
            PE.wait(_deps(reads, [bkB]))
            inst = None
            for i, (lhsT, rhs) in enumerate(pairs):
                if per is not None:
                    PE.wait(_deps([per[i]], []))
                inst = PE.eng.matmul(bk[:, 0:T], lhsT=lhsT, rhs=rhs, start=(i == 0), stop=(i == n - 1))
            PE.n += 1
            inst.then_inc(PE.sem, 1)
            _commit((PE, PE.n), list(reads) + (list(per) if per is not None else []), [bkB])

        def rstd_from(src_ap, srcB, T, bias):
            s_, sB = new_stat()
            if bias:
                op(ACT, lambda e: e.activation(out=s_[:, 0:T], in_=src_ap, func=AF.Ln, bias=epsb[:, 0:1]),
                   reads=srcB, writes=[sB])
            else:
                op(ACT, lambda e: e.activation(out=s_[:, 0:T], in_=src_ap, func=AF.Ln), reads=srcB, writes=[sB])
            r, rB = new_stat()
            op(ACT, lambda e: e.activation(out=r[:, 0:T], in_=s_[:, 0:T], func=AF.Exp, scale=-0.5), reads=[sB], writes=[rB])
            return r, rB

        def rmsnorm(t, T, gbase, l, out_f32=False):
            for hh in range(2):
                op(ACT, lambda e, hh=hh: e.activation(out=t.h[:, 4 * hh:4 * hh + 4, 0:T], in_=t.x[:, 4 * hh:4 * hh + 4, 0:T], func=AF.Square),
                   reads=t.xB[4 * hh:4 * hh + 4], writes=t.hB[4 * hh:4 * hh + 4])
            bk, bkB = next_bank()
            mm_group(bk, bkB, T, [(ones_d[:], t.h[:, k, 0:T]) for k in range(KD)], reads=[], per=t.hB)
            r, rB = rstd_from(bk[:, 0:T], [bkB], T, True)
            fns = []
            for k in range(KD):
                gcol = tab[:, C_GFIN + k:C_GFIN + k + 1] if l is None else tcol(l, gbase, k)
                dst = t.x[:, k, 0:T] if out_f32 else t.h[:, k, 0:T]
                fns.append(lambda e, k=k, gcol=gcol, dst=dst: e.scalar_tensor_tensor(
                    out=dst, in0=t.x[:, k, 0:T], scalar=gcol, in1=r[:, 0:T], op0=ALU.mult, op1=ALU.mult))
            grp(DVE, fns, reads=[rB] + t.xB, writes=(t.xB if out_f32 else t.hB))
            return 12.0 if T == TT else ("s", 5.0)

        def load_x(t, T, src, tok0):
            nch = max(1, T // 128)
            rows = min(128, T)
            for tc in range(nch):
                b = tc % 2
                dma(SP, t.xstD[b], t.xst[b][0:rows, :], src[tok0 + tc * 128:tok0 + tc * 128 + rows, :], writes=t.xstB[b])
                for half in range(2):
                    bk, bkB = next_bank()
                    fns = []
                    for kk in range(4):
                        k = half * 4 + kk
                        fns.append(lambda e, k=k, kk=kk, b=b, bk=bk: e.transpose(
                            bk[:, kk * 128:kk * 128 + rows], t.xst[b][0:rows, k * 128:(k + 1) * 128], ident[0:rows, 0:rows]))
                    grp(PE, fns, reads=t.xstB[b], writes=[bkB])
                    src_v = bk[:, :].rearrange("p (a b) -> p a b", a=4)[:, :, 0:rows]
                    op(ACT, lambda e, half=half, tc=tc, src_v=src_v: e.activation(
                        out=t.x[:, half * 4:half * 4 + 4, tc * 128:tc * 128 + rows], in_=src_v, func=AF.Copy),
                       reads=[bkB], writes=t.xB[half * 4:half * 4 + 4])
                yield 2.5 if T == TT else ("s", 2.5)

        def store_y(t, T, dst, tok0):
            nch = max(1, T // 128)
            rows = min(128, T)
            for tc in range(nch):
                b = tc % 2
                for half in range(2):
                    bk, bkB = next_bank()
                    fns = []
                    for kk in range(4):
                        k = half * 4 + kk
                        fns.append(lambda e, k=k, kk=kk, bk=bk, tc=tc: e.transpose(
                            bk[0:rows, kk * 128:(kk + 1) * 128], t.x[:, k, tc * 128:tc * 128 + rows], ident[:, :]))
                    grp(PE, fns, reads=t.xB[half * 4:half * 4 + 4], writes=[bkB])
                    op(ACT, lambda e, half=half, b=b, bk=bk: e.activation(
                        out=t.xst[b][0:rows, half * 512:(half + 1) * 512], in_=bk[0:rows, :], func=AF.Copy),
                       reads=[bkB], writes=t.xstB[b])
                dma(SP, t.xstD[b], dst[tok0 + tc * 128:tok0 + tc * 128 + rows, :], t.xst[b][0:rows, :], reads=t.xstB[b])
                yield 2.5 if T == TT else ("s", 2.5)

        def load_p(t, T, src_l, tok0, pT, pTB):
            nch = max(1, T // 128)
            rows = min(128, T)
            if nch > 1:
                for a in range(2):
                    dma(SP, t.pstD[a], t.pst[a], src_l[tok0 + a * 256:tok0 + (a + 1) * 256, :].rearrange("(c p) f -> p c f", p=128),
                        writes=[t.pstB[a]])
                pch = [(t.pst[tc // 2][:, tc % 2, :], t.pstB[tc // 2]) for tc in range(nch)]
            else:
                dma(SP, t.pstD[0], t.pst[0][0:rows, 0, :], src_l[tok0:tok0 + rows, :], writes=[t.pstB[0]])
                pch = [(t.pst[0][:, 0, :], t.pstB[0])]
            for fc in range(2):
                bk, bkB = next_bank()
                fns = []
                for tc in range(nch):
                    fns.append(lambda e, tc=tc, fc=fc, bk=bk: e.transpose(
                        bk[:, tc * 128:tc * 128 + rows], pch[tc][0][0:rows, fc * 128:(fc + 1) * 128], ident[0:rows, 0:rows]))
                grp(PE, fns, reads=[b for (_, b) in pch], writes=[bkB])
                op(ACT, lambda e, fc=fc, bk=bk: e.activation(out=pT[:, fc, 0:T], in_=bk[:, 0:T], func=AF.Copy),
                   reads=[bkB], writes=pTB)

        def out_hist(src_ap3, srcB, n, dst):
            if CFG.get("dbg_no_out_hist"):
                return
            bk, bkB = next_bank()
            fns = []
            for m in range(4):
                fns.append(lambda e, m=m, bk=bk: e.transpose(bk[0:n, m * 128:(m + 1) * 128], src_ap3[:, m, :], ident[:, :]))
            grp(PE, fns, reads=srcB, writes=[bkB])
            hs, hsB, hsD = new_stat3()
            op(ACT, lambda e, bk=bk: e.activation(out=hs[0:n, :], in_=bk[0:n, :], func=AF.Copy), reads=[bkB], writes=[hsB])
            dma(SP, hsD, dst, hs[0:n, :], reads=[hsB])

        def load_hist(src, n, dst_ap, dstB):
            hs, hsB, hsD = new_stat3()
            dma(SP, hsD, hs[0:n, :], src, writes=[hsB])
            bk, bkB = next_bank()
            fns = []
            for m in range(4):
                fns.append(lambda e, m=m, bk=bk: e.transpose(bk[:, m * 32:m * 32 + n], hs[0:n, m * 128:(m + 1) * 128], ident[0:n, 0:n]))
            grp(PE, fns, reads=[hsB], writes=[bkB])
            src_v = bk[:, 0:128].rearrange("p (a b) -> p a b", a=4)[:, :, 0:n]
            op(ACT, lambda e: e.activation(out=dst_ap, in_=src_v, func=AF.Copy), reads=[bkB], writes=[dstB])

        hist_flag = set()

        def mixer_half(S, p, l, tiles):
            for (t, T, info) in tiles:
                yield rmsnorm(t, T, C_GMIX, l)
            hs = [None] * len(tiles)
            for q in range(2):
                w, wB = get_unit(S, p, l, "in_ag", q)
                Wa = wview(w, KD, 256); Wg = wview(w, KD, 256, off=KD * 256)
                for (t, T, info) in tiles:
                    hc, hcB = info["hc"][:, l, :, :], info["hcB"][l]
                    for mm in range(2):
                        m = 2 * q + mm
                        bka, bkaB = next_bank()
                        mm_group(bka, bkaB, T, [(Wa[:, k, mm * 128:(mm + 1) * 128], t.h[:, k, 0:T]) for k in range(KD)],
                                 reads=[wB], per=t.hB)
                        bkg, bkgB = next_bank()
                        mm_group(bkg, bkgB, T, [(Wg[:, k, mm * 128:(mm + 1) * 128], t.h[:, k, 0:T]) for k in range(KD)],
                                 reads=[wB], per=t.hB)
                        sg, sgB = new_sig()
                        op(ACT, lambda e, sg=sg, bkg=bkg, T=T: e.activation(out=sg[:, 0:T], in_=bkg[:, 0:T], func=AF.Sigmoid),
                           reads=[bkgB], writes=[sgB])
                        fns = [lambda e, t=t, m=m, bka=bka, sg=sg, T=T: e.tensor_tensor(
                            out=t.cbf[:, m, HC:HC + T], in0=bka[:, 0:T], in1=sg[:, 0:T], op=ALU.mult)]
                        wr = [t.cB[m]]
                        if info["last"]:
                            fns.append(lambda e, m=m, bka=bka, sg=sg, T=T: e.tensor_tensor(
                                out=tail32[:, m, :], in0=bka[:, T - HC:T], in1=sg[:, T - HC:T], op=ALU.mult))
                            wr = wr + [tail32B]
                        grp(DVE, fns, reads=[bkaB, sgB], writes=wr)
                        yield mmcost(16, T)
                    if q == 1:
                        op(ACT, lambda e, t=t, hc=hc: e.activation(out=t.cbf[:, :, 0:HC], in_=hc, func=AF.Copy),
                           reads=[hcB], writes=t.cB)
                        if info["last"]:
                            out_hist(tail32, [tail32B], HC, info["nc_out"][l])
                        else:
                            op(ACT, lambda e, t=t, T=T, hc=hc: e.activation(out=hc, in_=t.cbf[:, :, T:T + HC], func=AF.Copy),
                               reads=t.cB, writes=[hcB])
            w, wB = get_unit(S, p, l, "in_u", 0)
            Wu = wview(w, KD, DC); Wp = wview(w, 4, 128, off=KD * DC)
            for (t, T, info) in tiles:
                hu, huB = info["hu"][:, l, :, :], info["huB"][l]
                op(ACT, lambda e, t=t, hu=hu: e.activation(out=t.u[:, :, 0:HP], in_=hu, func=AF.Copy),
                   reads=[huB], writes=t.uB)
                for m in range(4):
                    bk, bkB = next_bank()
                    mm_group(bk, bkB, T, [(Wu[:, k, m * 128:(m + 1) * 128], t.h[:, k, 0:T]) for k in range(KD)],
                             reads=[wB], per=t.hB)
                    op(ACT, lambda e, t=t, m=m, bk=bk, T=T: e.activation(out=t.u[:, m, HP:HP + T], in_=bk[:, 0:T], func=AF.Copy),
                       reads=[bkB], writes=[t.uB[m]])
                    yield mmcost(8, T)
                op(ACT, lambda e, t=t, T=T, hu=hu: e.activation(out=hu, in_=t.u[:, :, T:T + HP], func=AF.Copy),
                   reads=t.uB, writes=[huB])
                if info["last"]:
                    out_hist(hu, [huB], HP, info["np_out"][l])
            for (t, T, info) in tiles:
                L = HP + T
                for g in range(4):
                    ug = t.u[:, g, :]
                    steps = [(psA, psAB, 1)]
                    if g >= 1:
                        steps.append((psB_, psBB, 2))
                    if g >= 2:
                        steps.append((psA, psAB, 4))
                    if g >= 3:
                        steps.append((psB_, psBB, 8))
                    src, srcB, lo = ug, t.uB[g], 0
                    for (dst, dstB, sh) in steps:
                        nlo = lo + sh
                        op(DVE, lambda e, dst=dst, src=src, nlo=nlo, sh=sh, L=L: e.tensor_tensor(
                            out=dst[:, nlo:L], in0=src[:, nlo:L], in1=src[:, nlo - sh:L - sh], op=ALU.add),
                           reads=[srcB], writes=[dstB])
                        src, srcB, lo = dst, dstB, nlo
                    w_ = POOL_W[g]
                    op(DVE, lambda e, src=src, ug=ug, w_=w_, t=t, g=g, T=T: e.scalar_tensor_tensor(
                        out=t.h[:, g, 0:T], in0=src[:, HP:HP + T], scalar=1.0 / w_, in1=ug[:, HP:HP + T],
                        op0=ALU.mult, op1=ALU.subtract), reads=[srcB, t.uB[g]], writes=[t.hB[g]])
                    if info["first"]:
                        nfix = w_ - 1
                        fx, fxB = new_stat()
                        op(DVE, lambda e, src=src, g=g, nfix=nfix, fx=fx: e.tensor_tensor(
                            out=fx[:, 0:nfix], in0=src[:, HP:HP + nfix], in1=invcnt[:, g, 0:nfix], op=ALU.mult),
                           reads=[srcB], writes=[fxB])
                        op(DVE, lambda e, ug=ug, t=t, g=g, nfix=nfix, fx=fx: e.tensor_tensor(
                            out=t.h[:, g, 0:nfix], in0=fx[:, 0:nfix], in1=ug[:, HP:HP + nfix], op=ALU.subtract),
                           reads=[fxB, t.uB[g]], writes=[t.hB[g]])
                    bk, bkB = next_bank()
                    mm_group(bk, bkB, T, [(Wp[:, g, :], t.h[:, g, 0:T])], reads=[wB, t.hB[g]])
                    op(ACT, lambda e, t=t, g=g, bk=bk, T=T: e.activation(out=t.h[:, 4 + g, 0:T], in_=bk[:, 0:T], func=AF.Copy,
                                                                        scale=tcol(l, C_PS, g)),
                       reads=[bkB], writes=[t.hB[4 + g]])
                    yield (0.65 * (len(steps) + 1)) if T == TT else ("s", 0.23 * (len(steps) + 1))
            def conv_mm(t, T, m):
                bk, bkB = next_bank()
                mm_group(bk, bkB, T, [(diagL[:, m, j, :], t.cbf[:, m, j:j + T]) for j in range(CW)],
                         reads=[diagLB[m], t.cB[m]])
                op(ACT, lambda e: e.activation(out=t.cv[:, m, 0:T], in_=bk[:, 0:T], func=AF.Identity, bias=tcol(l, C_CB, m)),
                   reads=[bkB], writes=[t.hidB[2 * m], t.hidB[2 * m + 1]])

            def ln_pre(t, T):
                op(ACT, lambda e: e.activation(out=lnsc[:, :, 0:T], in_=t.cv[:, :, 0:T], func=AF.Copy),
                   reads=t.hidB, writes=[lnscB])
                op(ACT, lambda e: e.activation(out=t.h[:, 0:4, 0:T], in_=t.cv[:, :, 0:T], func=AF.Square),
                   reads=t.hidB, writes=t.hB[0:4])

            def ln_post(t, T):
                bkm, bkmB = next_bank()
                mm_group(bkm, bkmB, T, [(ones_c[:], lnsc[:, m, 0:T]) for m in range(4)], reads=[lnscB])
                bkq, bkqB = next_bank()
                mm_group(bkq, bkqB, T, [(ones_c[:], t.h[:, m, 0:T]) for m in range(4)], reads=t.hB[0:4])
                mean, meanB = new_stat()
                var, varB = new_stat()
                op(ACT, lambda e: e.activation(out=mean[:, 0:T], in_=bkm[:, 0:T], func=AF.Copy), reads=[bkmB], writes=[meanB])
                op(DVE, lambda e: e.tensor_tensor(out=var[:, 0:T], in0=mean[:, 0:T], in1=mean[:, 0:T], op=ALU.mult),
                   reads=[meanB], writes=[varB])
                op(DVE, lambda e: e.tensor_tensor(out=var[:, 0:T], in0=bkq[:, 0:T], in1=var[:, 0:T], op=ALU.subtract),
                   reads=[bkqB, varB], writes=[varB])
                op(DVE, lambda e: e.tensor_scalar(out=var[:, 0:T], in0=var[:, 0:T], scalar1=0.0, scalar2=EPS, op0=ALU.max, op1=ALU.add),
                   reads=[varB], writes=[varB])
                r, rB = rstd_from(var[:, 0:T], [varB], T, False)
                op(DVE, lambda e: e.scalar_tensor_tensor(out=mean[:, 0:T], in0=mean[:, 0:T], scalar=-1.0, in1=r[:, 0:T],
                                                        op0=ALU.mult, op1=ALU.mult), reads=[meanB, rB], writes=[meanB])
                for m in range(4):
                    hb = [t.hidB[2 * m], t.hidB[2 * m + 1]]
                    op(DVE, lambda e, m=m: e.tensor_tensor(out=t.cv[:, m, 0:T], in0=t.cv[:, m, 0:T], in1=r[:, 0:T], op=ALU.mult),
                       reads=hb + [rB], writes=hb)
                    op(DVE, lambda e, m=m: e.tensor_tensor(out=t.cv[:, m, 0:T], in0=t.cv[:, m, 0:T], in1=mean[:, 0:T], op=ALU.add),
                       reads=hb + [meanB], writes=hb)
                    op(ACT, lambda e, m=m: e.activation(out=t.h[:, m, 0:T], in_=t.cv[:, m, 0:T], func=AF.Silu,
                                                        bias=tcol(l, C_LB, m), scale=tcol(l, C_LG, m)), reads=hb, writes=[t.hB[m]])

            work = [(t, T, m) for (t, T, info) in tiles for m in range(4)]
            prev = None
            for i, (t, T, m) in enumerate(work):
                conv_mm(t, T, m)
                if m == 0 and prev is not None:
                    ln_post(*prev)
                if m == 3:
                    ln_pre(t, T)
                    prev = (t, T)
                yield 8.0
            ln_post(*prev)
            yield 12.0
            for q in range(2):
                w, wB = get_unit(S, p, l, "out", q)
                W = wview(w, KD, 512)
                for (t, T, info) in tiles:
                    for mm in range(4):
                        m = 4 * q + mm
                        bk, bkB = next_bank()
                        mm_group(bk, bkB, T, [(W[:, k, mm * 128:(mm + 1) * 128], t.h[:, k, 0:T]) for k in range(KD)],
                                 reads=[wB], per=t.hB)
                        op(DVE, lambda e, t=t, m=m, bk=bk, T=T: e.tensor_tensor(out=t.x[:, m, 0:T], in0=bk[:, 0:T], in1=t.x[:, m, 0:T], op=ALU.add),
                           reads=[bkB, t.xB[m]], writes=[t.xB[m]])
                        yield mmcost(8, T)

        def ffn_half(S, p, l, tiles):
            for (t, T, info) in tiles:
                yield rmsnorm(t, T, C_GFFN, l)
            for qq in range(NQ):
                if not (p == n_pass - 1 and l == depth - 1):
                    build_diag_chunk((l + 1) % depth, qq)
                for hf in range(2):
                    w, wB = get_unit(S, p, l, "ff1", 2 * qq + hf)
                    W1 = wview(w, KD, 512)
                    for (t, T, info) in tiles:
                        for mm in range(4):
                            m = 4 * hf + mm
                            bk, bkB = next_bank()
                            mm_group(bk, bkB, T, [(W1[:, k, mm * 128:(mm + 1) * 128], t.h[:, k, 0:T]) for k in range(KD)],
                                     reads=[wB], per=t.hB)
                            rl, rlB = new_relu()
                            op(ACT, lambda e, rl=rl, bk=bk, T=T: e.activation(out=rl[:, 0:T], in_=bk[:, 0:T], func=AF.Relu),
                               reads=[bkB], writes=[rlB])
                            op(ACT, lambda e, t=t, m=m, rl=rl, T=T: e.activation(out=t.hid[:, m, 0:T], in_=rl[:, 0:T], func=AF.Square),
                               reads=[rlB], writes=[t.hidB[m]])
                            yield mmcost(8, T)
                for hf in range(2):
                    w, wB = get_unit(S, p, l, "ff2", 2 * qq + hf)
                    W2 = wview(w, KD, 512)
                    for (t, T, info) in tiles:
                        for mm in range(4):
                            m = 4 * hf + mm
                            bk, bkB = next_bank()
                            mm_group(bk, bkB, T, [(W2[:, k, mm * 128:(mm + 1) * 128], t.hid[:, k, 0:T]) for k in range(KD)],
                                     reads=[wB], per=t.hidB)
                            op(DVE, lambda e, t=t, m=m, bk=bk, T=T: e.tensor_tensor(out=t.x[:, m, 0:T], in0=bk[:, 0:T], in1=t.x[:, m, 0:T], op=ALU.add),
                               reads=[bkB, t.xB[m]], writes=[t.xB[m]])
                            yield mmcost(8, T)
            for (t, T, info) in tiles:
                c = rmsnorm(t, T, C_GPLE, l)
                load_p(t, T, info["psrc"][l], info["tok0"], t.cbf, t.cB[0:2])
                yield (c + 2.0) if T == TT else ("s", 7.0)
            for q in range(2):
                w, wB = get_unit(S, p, l, "gp", q)
                Wg = wview(w, KD, 512); Wpl = wview(w, 2, 512, off=KD * 512)
                for (t, T, info) in tiles:
                    for mm in range(4):
                        m = 4 * q + mm
                        bkg, bkgB = next_bank()
                        mm_group(bkg, bkgB, T, [(Wg[:, k, mm * 128:(mm + 1) * 128], t.h[:, k, 0:T]) for k in range(KD)],
                                 reads=[wB], per=t.hB)
                        bkp, bkpB = next_bank()
                        mm_group(bkp, bkpB, T, [(Wpl[:, kc, mm * 128:(mm + 1) * 128], t.cbf[:, kc, 0:T]) for kc in range(2)],
                                 reads=[wB] + t.cB[0:2])
                        sg, sgB = new_sig()
                        op(ACT, lambda e, sg=sg, bkg=bkg, T=T: e.activation(out=sg[:, 0:T], in_=bkg[:, 0:T], func=AF.Sigmoid),
                           reads=[bkgB], writes=[sgB])
                        gt, gtB = new_gtmp()
                        op(DVE, lambda e, gt=gt, sg=sg, bkp=bkp, T=T: e.tensor_tensor(out=gt[:, 0:T], in0=bkp[:, 0:T], in1=sg[:, 0:T], op=ALU.mult),
                           reads=[bkpB, sgB], writes=[gtB])
                        op(DVE, lambda e, t=t, m=m, gt=gt, T=T: e.tensor_tensor(out=t.x[:, m, 0:T], in0=gt[:, 0:T], in1=t.x[:, m, 0:T], op=ALU.add),
                           reads=[gtB, t.xB[m]], writes=[t.xB[m]])
                        yield mmcost(10, T)

        def stream_gen(S):
            for p in range(n_pass):
                tiles = []
                for ti, tile in enumerate((tA, tB)):
                    tok0 = p * 2 * TT + ti * TT
                    info = {"first": tok0 == 0, "last": tok0 + TT == SEQ, "nc_out": ncp, "np_out": npp,
                            "tok0": tok0, "xsrc": xp, "psrc": pp, "ydst": yp,
                            "hc": hist_c[:, 0], "hcB": hist_cB[0], "hu": hist_u[:, 0], "huB": hist_uB[0]}
                    tiles.append((tile, TT, info))
                if p == 0:
                    for l in range(depth):
                        op(DVE, lambda e, l=l: e.memset(hist_c[:, 0, l, :, :], 0.0), writes=[hist_cB[0][l]])
                        op(DVE, lambda e, l=l: e.memset(hist_u[:, 0, l, :, :], 0.0), writes=[hist_uB[0][l]])
                    if tS is not None:
                        infoS = {"first": False, "last": True, "nc_out": ncs, "np_out": nps,
                                 "tok0": 0, "xsrc": xs, "psrc": psm, "ydst": ys,
                                 "hc": hist_c[:, 1], "hcB": hist_cB[1], "hu": hist_u[:, 1], "huB": hist_uB[1]}
                        tiles.append((tS, DEC, infoS))
                        for l in range(depth):
                            load_hist(cc[l], HC, hist_c[:, 1, l, :, :], hist_cB[1][l])
                            load_hist(cpl[l], HP, hist_u[:, 1, l, :, :], hist_uB[1][l])
                for (t, T, inf) in tiles:
                    yield from load_x(t, T, inf["xsrc"], inf["tok0"])
                for l in range(depth):
                    yield from mixer_half(S, p, l, tiles)
                    yield from ffn_half(S, p, l, tiles)
                for (t, T, inf) in tiles:
                    rmsnorm(t, T, None, None, out_f32=True)
                for (t, T, inf) in tiles:
                    yield from store_y(t, T, inf["ydst"], inf["tok0"])

        for m_ in range(4):
            build_diag_chunk(0, m_)
        for _ in stream_gen(streams[0]):
            pass

        fin = [(d_, d_.n) for d_ in new_stat3.sems]
        for t_ in (tA, tB, tS):
            if t_ is not None:
                fin += [(d_, d_.n) for d_ in t_.xstD + t_.pstD]
        SP.wait(fin)
    return nc


_PROGRAM = [None]


def kernel(x_prompt, x_sample, p_prompt, p_sample, cache_conv, cache_pool, w_in, conv_w, conv_b,
           ln_g, ln_b, pool_w, pool_scale, w_out, g_mix, g_ffn, g_ple, w_ff1, w_ff2, w_ple, w_gate, g_final):
    f = lambda a: np.ascontiguousarray(np.asarray(a, dtype=np.float32))
    shared = {"w_in": f(w_in), "conv_w": f(conv_w), "conv_b": f(conv_b), "ln_g": f(ln_g), "ln_b": f(ln_b),
              "pool_w": f(pool_w), "pool_scale": f(pool_scale), "w_out": f(w_out), "g_mix": f(g_mix),
              "g_ffn": f(g_ffn), "g_ple": f(g_ple), "w_ff1": f(w_ff1), "w_ff2": f(w_ff2), "w_ple": f(w_ple),
              "w_gate": f(w_gate), "g_final": f(g_final)}
    in_maps = []
    for i in range(NCORES):
        m = dict(shared)
        m["xp"] = f(x_prompt[i]); m["xs"] = f(x_sample[i])
        m["pp"] = f(p_prompt[:, i]); m["psm"] = f(p_sample[:, i])
        m["cc"] = f(cache_conv[:, i]); m["cpl"] = f(cache_pool[:, i])
        in_maps.append(m)
    nc = build_program()
    res = run_bass_kernel_spmd(nc, in_maps, core_ids=list(range(NCORES)))
    r = res.results
    y_prompt = np.stack([r[i]["yp"] for i in range(NCORES)], 0)
    y_sample = np.stack([r[i]["ys"] for i in range(NCORES)], 0)
    ncp = np.stack([r[i]["ncp"] for i in range(NCORES)], 1)
    npp = np.stack([r[i]["npp"] for i in range(NCORES)], 1)
    ncs = np.stack([r[i]["ncs"] for i in range(NCORES)], 1)
    nps = np.stack([r[i]["nps"] for i in range(NCORES)], 1)
    return (y_prompt.astype(np.float32), y_sample.astype(np.float32), ncp.astype(np.float32),
            npp.astype(np.float32), ncs.astype(np.float32), nps.astype(np.float32))
```

```python
import contextlib
import numpy as np
import concourse.bass as bass
import concourse.mybir as mybir
from concourse.bass_utils import run_bass_kernel_spmd

F32 = mybir.dt.float32
BF16 = mybir.dt.bfloat16
ALU = mybir.AluOpType
AF = mybir.ActivationFunctionType

NCORES = 8
D = 1024
KD = D // 128
SEQ = 4096
DEC = 64
DEPTH = 4
DC = 512
CW = 31
HC = CW - 1
HP = 15
PLE = 256
DFF = 4096
EPS = 1e-6
TT = 512
NQ = 4
SLOT = 5 * 1024
NSLOT = 3
POOL_W = (2, 4, 8, 16)

CFG = {"n_pass": 4, "do_sample": True, "depth": DEPTH}


class SemObj:
    def __init__(self, nc, stack, name):
        self.sem = stack.enter_context(nc.semaphore(name))
        self.n = 0


class Eng(SemObj):
    def __init__(self, nc, stack, eng, name):
        super().__init__(nc, stack, "s_" + name)
        self.eng = eng
        self.seen = {}

    def wait(self, deps):
        best = {}
        for (S, c) in deps:
            if c > best.get(S, 0):
                best[S] = c
        for S, c in best.items():
            if c > self.seen.get(S, 0):
                self.eng.wait_ge(S.sem, c)
                self.seen[S] = c


class Buf:
    __slots__ = ("w", "r", "ro")

    def __init__(self):
        self.w = None
        self.r = {}
        self.ro = False


def _deps(reads, writes):
    deps = []
    for b in reads:
        if b.w is not None:
            deps.append(b.w)
    for b in writes:
        if b.w is not None:
            deps.append(b.w)
        deps.extend(b.r.items())
    return deps


def _commit(tok, reads, writes):
    S, c = tok
    for b in reads:
        if not b.ro:
            b.r[S] = c
    for b in writes:
        b.w = tok
        b.r = {}


def grp(E, fns, reads=(), writes=(), nosame=False):
    deps = _deps(reads, writes)
    if nosame:
        deps = [d for d in deps if d[0] is not E]
    E.wait(deps)
    inst = None
    for fn in fns:
        inst = fn(E.eng)
    E.n += 1
    inst.then_inc(E.sem, 1)
    _commit((E, E.n), reads, writes)


def op(E, fn, reads=(), writes=()):
    grp(E, [fn], reads, writes)


def dma(Q, Dm, out, in_, reads=(), writes=()):
    Q.wait(_deps(reads, writes))
    Q.eng.dma_start(out=out, in_=in_).then_inc(Dm.sem, 16)
    Dm.n += 16
    _commit((Dm, Dm.n), reads, writes)


class TileCtx:
    pass


class Stream:
    pass


def build_program():
    nc = bass.Bass("TRN2", target_bir_lowering=False)
    depth = CFG["depth"]
    n_pass = CFG["n_pass"]

    def din(name, shape):
        return nc.dram_tensor(name, list(shape), F32, kind="ExternalInput").ap()

    def dout(name, shape):
        return nc.dram_tensor(name, list(shape), F32, kind="ExternalOutput").ap()

    xp = din("xp", [SEQ, D]); xs = din("xs", [DEC, D])
    pp = din("pp", [DEPTH, SEQ, PLE]); psm = din("psm", [DEPTH, DEC, PLE])
    cc = din("cc", [DEPTH, HC, DC]); cpl = din("cpl", [DEPTH, HP, DC])
    w_in = din("w_in", [DEPTH, D, 3 * DC]); conv_w = din("conv_w", [DEPTH, CW, DC])
    conv_b = din("conv_b", [DEPTH, DC]); ln_g = din("ln_g", [DEPTH, DC]); ln_b = din("ln_b", [DEPTH, DC])
    pool_w = din("pool_w", [DEPTH, 4, 128, 128]); pool_scale = din("pool_scale", [DEPTH, DC])
    w_out = din("w_out", [DEPTH, D, D]); g_mix = din("g_mix", [DEPTH, D]); g_ffn = din("g_ffn", [DEPTH, D])
    g_ple = din("g_ple", [DEPTH, D]); w_ff1 = din("w_ff1", [DEPTH, D, DFF]); w_ff2 = din("w_ff2", [DEPTH, DFF, D])
    w_ple = din("w_ple", [DEPTH, PLE, D]); w_gate = din("w_gate", [DEPTH, D, D]); g_final = din("g_final", [D])
    yp = dout("yp", [SEQ, D]); ys = dout("ys", [DEC, D])
    ncp = dout("ncp", [DEPTH, HC, DC]); npp = dout("npp", [DEPTH, HP, DC])
    ncs = dout("ncs", [DEPTH, HC, DC]); nps = dout("nps", [DEPTH, HP, DC])

    with contextlib.ExitStack() as st:
        def sb(name, shape, dt):
            return st.enter_context(nc.sbuf_tensor(name, list(shape), dt))

        PE = Eng(nc, st, nc.tensor, "pe"); ACT = Eng(nc, st, nc.scalar, "act")
        DVE = Eng(nc, st, nc.vector, "dve"); POOL = Eng(nc, st, nc.gpsimd, "pool")
        SP = Eng(nc, st, nc.sync, "sp")

        ident = sb("ident", [128, 128], F32); identB = Buf()
        ones_d = sb("ones_d", [128, 128], BF16)
        ones_c = sb("ones_c", [128, 128], BF16)
        epsb = sb("epsb", [128, 1], F32)
        constB = Buf()
        invcnt = sb("invcnt", [128, 4, 16], F32)
        NV = 3 * KD + 4 * 4
        C_GMIX, C_GFFN, C_GPLE, C_CB, C_LG, C_LB, C_PS = 0, 8, 16, 24, 28, 32, 36
        C_GFIN = DEPTH * NV
        C_CW = C_GFIN + KD
        NCOL = C_CW + DEPTH * CW * 4
        tab = sb("tab", [128, NCOL], F32); tabB = Buf()

        def tcol(l, base, k):
            c = l * NV + base + k
            return tab[:, c:c + 1]

        def cwcol(l, j, m):
            c = C_CW + l * CW * 4 + j * 4 + m
            return tab[:, c:c + 1]

        banks = [st.enter_context(nc.psum_tensor("bank%d" % i, [128, 512], F32)) for i in range(8)]
        bankB = [Buf() for _ in range(8)]
        bank_i = [0]

        def next_bank():
            i = bank_i[0]
            bank_i[0] = (i + 1) % 8
            return banks[i], bankB[i]

        def pool_of(name, n, shape, dt, with_sem=False):
            aps = [sb("%s%d" % (name, i), shape, dt) for i in range(n)]
            bufs = [Buf() for _ in range(n)]
            sems = [SemObj(nc, st, "%sd%d" % (name, i)) for i in range(n)] if with_sem else None
            idx = [0]

            def nxt():
                i = idx[0]
                idx[0] = (i + 1) % n
                if with_sem:
                    return aps[i], bufs[i], sems[i]
                return aps[i], bufs[i]
            nxt.sems = sems
            return nxt

        lnsc = sb("lnsc", [128, 4, TT], BF16); lnscB = Buf()
        diagL = sb("diagL", [128, 4, CW, 128], BF16)
        diagLB = [Buf() for _ in range(4)]

        def build_diag_chunk(l, m):
            fns = []
            for j in range(CW):
                fns.append(lambda e, j=j: e.tensor_scalar_mul(out=diagL[:, m, j, :], in0=ident[:, :], scalar1=cwcol(l, j, m)))
            grp(DVE, fns, writes=[diagLB[m]])
        new_stat3 = pool_of("stat", 6, [128, TT], F32, with_sem=True)

        def new_stat():
            a, b, _ = new_stat3()
            return a, b
        new_sig = pool_of("sig", 3, [128, TT], F32)
        new_relu = pool_of("relu", 3, [128, TT], BF16)
        new_gtmp = pool_of("gtmp", 2, [128, TT], F32)
        psA = sb("psA", [128, HP + TT], F32); psB_ = sb("psB", [128, HP + TT], F32)
        psAB = Buf(); psBB = Buf()
        tail32 = sb("tail32", [128, 4, HC], F32); tail32B = Buf()
        hist_c = sb("hist_c", [128, 2, DEPTH, 4, HC], BF16); hist_cB = [[Buf() for _ in range(DEPTH)] for _ in range(2)]
        hist_u = sb("hist_u", [128, 2, DEPTH, 4, HP], F32); hist_uB = [[Buf() for _ in range(DEPTH)] for _ in range(2)]

        def make_tile(name, T):
            t = TileCtx()
            t.T = T
            t.x = sb(name + "_x", [128, KD, T], F32); t.xB = [Buf() for _ in range(KD)]
            t.h = sb(name + "_h", [128, KD, T], BF16); t.hB = [Buf() for _ in range(KD)]
            t.cbf = sb(name + "_c", [128, 4, HC + T], BF16); t.cB = [Buf() for _ in range(4)]
            t.u = sb(name + "_u", [128, 4, HP + T], F32); t.uB = [Buf() for _ in range(4)]
            t.cv = sb(name + "_cv", [128, 4, T], F32)
            t.hid = t.cv.bitcast(BF16).reshape([128, KD, T])
            t.hidB = [Buf() for _ in range(KD)]
            t.xstD = [SemObj(nc, st, name + "_xd%d" % i) for i in range(2)]
            t.pstD = [SemObj(nc, st, name + "_pd%d" % i) for i in range(2)]
            if T == TT:
                t.xst = [t.cv[:, 2 * b:2 * b + 2, :].rearrange("p a t -> p (a t)") for b in range(2)]
                t.xstB = [t.hidB[0:4], t.hidB[4:8]]
                t.pst = [t.u[:, a, 0:512].rearrange("p (b f) -> p b f", f=PLE) for a in range(2)]
                t.pstB = t.uB[0:2]
            else:
                t.xst_t = sb(name + "_xst", [128, D], F32)
                t.xst = [t.xst_t[:, :], t.xst_t[:, :]]
                b_ = Buf(); t.xstB = [[b_], [b_]]
                t.xstD = [t.xstD[0], t.xstD[0]]
                t.pst_t = sb(name + "_pst", [128, 1, PLE], F32)
                t.pst = [t.pst_t[:, :, :]]
                t.pstB = [Buf()]
            return t

        tA = make_tile("tA", TT); tB = make_tile("tB", TT)
        tS = make_tile("tS", DEC) if CFG["do_sample"] else None

        streams = []
        for sid in range(1):
            S = Stream()
            S.sid = sid
            S.ring = [sb("ring%d_%d" % (sid, i), [128, SLOT], BF16) for i in range(NSLOT)]
            S.ringB = [Buf() for _ in range(NSLOT)]
            S.ringD = [SemObj(nc, st, "ringd%d_%d" % (sid, i)) for i in range(NSLOT)]
            S.issued = 0
            streams.append(S)

        setupD = SemObj(nc, st, "setupd")
        op(POOL, lambda e: e.memset(ident[:], 0.0), writes=[identB])
        op(POOL, lambda e: e.affine_select(out=ident[:], in_=ident[:], compare_op=ALU.not_equal, fill=1.0,
                                           base=0, pattern=[[-1, 128]], channel_multiplier=1),
           reads=[identB], writes=[identB])
        grp(POOL, [lambda e: e.memset(ones_d[:], 1.0 / D), lambda e: e.memset(ones_c[:], 1.0 / DC),
                   lambda e: e.memset(epsb[:], EPS)], writes=[constB])
        fns = []
        for g, w in enumerate(POOL_W):
            for pos in range(16):
                v = 1.0 / min(w, pos + 1)
                fns.append(lambda e, g=g, pos=pos, v=v: e.memset(invcnt[:, g, pos:pos + 1], v))
        grp(POOL, fns, writes=[constB])

        stg = tA.x
        stgB = Buf()
        vecs = [(g_mix, C_GMIX, KD), (g_ffn, C_GFFN, KD), (g_ple, C_GPLE, KD), (conv_b, C_CB, 4),
                (ln_g, C_LG, 4), (ln_b, C_LB, 4), (pool_scale, C_PS, 4)]
        for l in range(DEPTH):
            ti, r0 = l // 2, (l % 2) * NV
            for (v, base, nk) in vecs:
                src = v[l].rearrange("(k p) -> k p", p=128)
                dma(SP, setupD, stg[r0 + base:r0 + base + nk, ti, 0:128], src, writes=[stgB])
        dma(SP, setupD, stg[2 * NV:2 * NV + KD, 1, 0:128], g_final.rearrange("(k p) -> k p", p=128), writes=[stgB])
        for l in range(DEPTH):
            dma(SP, setupD, stg[0:CW * 4, 2 + l, 0:128], conv_w[l].rearrange("j (m p) -> (j m) p", p=128), writes=[stgB])
        stgB.w = (setupD, setupD.n)
        tr_plan = [(0, 2 * NV, 0), (1, 2 * NV + KD, 2 * NV)] + [(2 + l, CW * 4, C_CW + l * CW * 4) for l in range(DEPTH)]
        for (ti, rows, col0) in tr_plan:
            bk, bkB = next_bank()
            grp(PE, [lambda e, ti=ti, rows=rows, bk=bk: e.transpose(bk[:, 0:rows], stg[0:rows, ti, 0:128], ident[0:rows, 0:rows])],
                reads=[stgB, identB], writes=[bkB])
            op(ACT, lambda e, rows=rows, col0=col0, bk=bk: e.activation(out=tab[:, col0:col0 + rows], in_=bk[:, 0:rows], func=AF.Copy),
               reads=[bkB], writes=[tabB])
        for E in (PE, ACT, DVE, POOL):
            E.wait([tabB.w, constB.w, identB.w])
        for b in (tabB, constB, identB):
            b.ro = True
        for k in range(KD):
            tA.xB[k].r = dict(stgB.r)
            tA.xB[k].w = stgB.w

        def wv(w2d):
            return w2d.rearrange("(k p) n -> p k n", p=128)

        def unit_srcs(l, kind, q):
            if kind == "in_ag":
                W = wv(w_in[l])
                return [(0, W[:, :, q * 256:(q + 1) * 256], KD, 256),
                        (KD * 256, W[:, :, DC + q * 256:DC + (q + 1) * 256], KD, 256)]
            if kind == "in_u":
                return [(0, wv(w_in[l])[:, :, 2 * DC:3 * DC], KD, DC),
                        (KD * DC, pool_w[l].rearrange("g c d -> c g d"), 4, 128)]
            if kind == "out":
                return [(0, wv(w_out[l])[:, :, q * 512:(q + 1) * 512], KD, 512)]
            if kind == "ff1":
                return [(0, wv(w_ff1[l])[:, :, q * 512:(q + 1) * 512], KD, 512)]
            if kind == "ff2":
                qq, hf = q // 2, q % 2
                return [(0, wv(w_ff2[l][qq * D:(qq + 1) * D, :])[:, :, hf * 512:(hf + 1) * 512], KD, 512)]
            if kind == "gp":
                return [(0, wv(w_gate[l])[:, :, q * 512:(q + 1) * 512], KD, 512),
                        (KD * 512, wv(w_ple[l])[:, :, q * 512:(q + 1) * 512], 2, 512)]
            raise ValueError(kind)

        layer_units = [("in_ag", 0), ("in_ag", 1), ("in_u", 0), ("out", 0), ("out", 1)]
        for qq in range(NQ):
            layer_units += [("ff1", 2 * qq), ("ff1", 2 * qq + 1), ("ff2", 2 * qq), ("ff2", 2 * qq + 1)]
        layer_units += [("gp", 0), ("gp", 1)]
        NU = len(layer_units)
        unit_index = {ku: i for i, ku in enumerate(layer_units)}
        n_units_total = n_pass * depth * NU

        def issue_loads(S, upto):
            upto = min(upto, n_units_total - 1)
            while S.issued <= upto:
                i = S.issued
                l = (i // NU) % depth
                kind, q = layer_units[i % NU]
                s = i % NSLOT
                for (off, src, nk, nn) in unit_srcs(l, kind, q):
                    dst = S.ring[s][:, off:off + nk * nn].rearrange("p (k n) -> p k n", k=nk)
                    half = max(1, nk // 2) if nk * nn >= 2048 else nk
                    for k0 in range(0, nk, half):
                        dma(POOL, S.ringD[s], dst[:, k0:k0 + half, :], src[:, k0:k0 + half, :], writes=[S.ringB[s]])
                S.issued += 1

        def get_unit(S, p, l, kind, q):
            i = (p * depth + l) * NU + unit_index[(kind, q)]
            issue_loads(S, i + NSLOT - 1)
            s = i % NSLOT
            return S.ring[s], S.ringB[s]

        def wview(slot, nk, nn, off=0):
            return slot[:, off:off + nk * nn].rearrange("p (k n) -> p k n", k=nk)

        def mmcost(n, T):
            return n * 0.25 if T == TT else ("s", n * 0.11)

        def mm_group(bk, bkB, T, pairs, reads, per=None):
            n = len(pairs)
            PE.wait(_deps(reads, [bkB]))
            inst = None
            for i, (lhsT, rhs) in enumerate(pairs):
                if per is not None:
                    PE.wait(_deps([per[i]], []))
                inst = PE.eng.matmul(bk[:, 0:T], lhsT=lhsT, rhs=rhs, start=(i == 0), stop=(i == n - 1))
            PE.n += 1
            inst.then_inc(PE.sem, 1)
            _commit((PE, PE.n), list(reads) + (list(per) if per is not None else []), [bkB])

        def rstd_from(src_ap, srcB, T, bias):
            s_, sB = new_stat()
            if bias:
                op(ACT, lambda e: e.activation(out=s_[:, 0:T], in_=src_ap, func=AF.Ln, bias=epsb[:, 0:1]),
                   reads=srcB, writes=[sB])
            else:
                op(ACT, lambda e: e.activation(out=s_[:, 0:T], in_=src_ap, func=AF.Ln), reads=srcB, writes=[sB])
            r, rB = new_stat()
            op(ACT, lambda e: e.activation(out=r[:, 0:T], in_=s_[:, 0:T], func=AF.Exp, scale=-0.5), reads=[sB], writes=[rB])
            return r, rB

        def rmsnorm(t, T, gbase, l, out_f32=False):
            for hh in range(2):
                op(ACT, lambda e, hh=hh: e.activation(out=t.h[:, 4 * hh:4 * hh + 4, 0:T], in_=t.x[:, 4 * hh:4 * hh + 4, 0:T], func=AF.Square),
                   reads=t.xB[4 * hh:4 * hh + 4], writes=t.hB[4 * hh:4 * hh + 4])
            bk, bkB = next_bank()
            mm_group(bk, bkB, T, [(ones_d[:], t.h[:, k, 0:T]) for k in range(KD)], reads=[], per=t.hB)
            r, rB = rstd_from(bk[:, 0:T], [bkB], T, True)
            fns = []
            for k in range(KD):
                gcol = tab[:, C_GFIN + k:C_GFIN + k + 1] if l is None else tcol(l, gbase, k)
                dst = t.x[:, k, 0:T] if out_f32 else t.h[:, k, 0:T]
                fns.append(lambda e, k=k, gcol=gcol, dst=dst: e.scalar_tensor_tensor(
                    out=dst, in0=t.x[:, k, 0:T], scalar=gcol, in1=r[:, 0:T], op0=ALU.mult, op1=ALU.mult))
            grp(DVE, fns, reads=[rB] + t.xB, writes=(t.xB if out_f32 else t.hB))
            return 12.0 if T == TT else ("s", 5.0)

        def load_x(t, T, src, tok0):
            nch = max(1, T // 128)
            rows = min(128, T)
            for tc in range(nch):
                b = tc % 2
                dma(SP, t.xstD[b], t.xst[b][0:rows, :], src[tok0 + tc * 128:tok0 + tc * 128 + rows, :], writes=t.xstB[b])
                for half in range(2):
                    bk, bkB = next_bank()
                    fns = []
                    for kk in range(4):
                        k = half * 4 + kk
                        fns.append(lambda e, k=k, kk=kk, b=b, bk=bk: e.transpose(
                            bk[:, kk * 128:kk * 128 + rows], t.xst[b][0:rows, k * 128:(k + 1) * 128], ident[0:rows, 0:rows]))
                    grp(PE, fns, reads=t.xstB[b], writes=[bkB])
                    src_v = bk[:, :].rearrange("p (a b) -> p a b", a=4)[:, :, 0:rows]
                    op(ACT, lambda e, half=half, tc=tc, src_v=src_v: e.activation(
                        out=t.x[:, half * 4:half * 4 + 4, tc * 128:tc * 128 + rows], in_=src_v, func=AF.Copy),
                       reads=[bkB], writes=t.xB[half * 4:half * 4 + 4])
                yield 2.5 if T == TT else ("s", 2.5)

        def store_y(t, T, dst, tok0):
            nch = max(1, T // 128)
            rows = min(128, T)
            for tc in range(nch):
                b = tc % 2
                for half in range(2):
                    bk, bkB = next_bank()
                    fns = []
                    for kk in range(4):
                        k = half * 4 + kk
                        fns.append(lambda e, k=k, kk=kk, bk=bk, tc=tc: e.transpose(
                            bk[0:rows, kk * 128:(kk + 1) * 128], t.x[:, k, tc * 128:tc * 128 + rows], ident[:, :]))
                    grp(PE, fns, reads=t.xB[half * 4:half * 4 + 4], writes=[bkB])
                    op(ACT, lambda e, half=half, b=b, bk=bk: e.activation(
                        out=t.xst[b][0:rows, half * 512:(half + 1) * 512], in_=bk[0:rows, :], func=AF.Copy),
                       reads=[bkB], writes=t.xstB[b])
                dma(SP, t.xstD[b], dst[tok0 + tc * 128:tok0 + tc * 128 + rows, :], t.xst[b][0:rows, :], reads=t.xstB[b])
                yield 2.5 if T == TT else ("s", 2.5)

        def load_p(t, T, src_l, tok0, pT, pTB):
            nch = max(1, T // 128)
            rows = min(128, T)
            if nch > 1:
                for a in range(2):
                    dma(SP, t.pstD[a], t.pst[a], src_l[tok0 + a * 256:tok0 + (a + 1) * 256, :].rearrange("(c p) f -> p c f", p=128),
                        writes=[t.pstB[a]])
                pch = [(t.pst[tc // 2][:, tc % 2, :], t.pstB[tc // 2]) for tc in range(nch)]
            else:
                dma(SP, t.pstD[0], t.pst[0][0:rows, 0, :], src_l[tok0:tok0 + rows, :], writes=[t.pstB[0]])
                pch = [(t.pst[0][:, 0, :], t.pstB[0])]
            for fc in range(2):
                bk, bkB = next_bank()
                fns = []
                for tc in range(nch):
                    fns.append(lambda e, tc=tc, fc=fc, bk=bk: e.transpose(
                        bk[:, tc * 128:tc * 128 + rows], pch[tc][0][0:rows, fc * 128:(fc + 1) * 128], ident[0:rows, 0:rows]))
                grp(PE, fns, reads=[b for (_, b) in pch], writes=[bkB])
                op(ACT, lambda e, fc=fc, bk=bk: e.activation(out=pT[:, fc, 0:T], in_=bk[:, 0:T], func=AF.Copy),
                   reads=[bkB], writes=pTB)

        def out_hist(src_ap3, srcB, n, dst):
            if CFG.get("dbg_no_out_hist"):
                return
            bk, bkB = next_bank()
            fns = []
            for m in range(4):
                fns.append(lambda e, m=m, bk=bk: e.transpose(bk[0:n, m * 128:(m + 1) * 128], src_ap3[:, m, :], ident[:, :]))
            grp(PE, fns, reads=srcB, writes=[bkB])
            hs, hsB, hsD = new_stat3()
            op(ACT, lambda e, bk=bk: e.activation(out=hs[0:n, :], in_=bk[0:n, :], func=AF.Copy), reads=[bkB], writes=[hsB])
            dma(SP, hsD, dst, hs[0:n, :], reads=[hsB])

        def load_hist(src, n, dst_ap, dstB):
            hs, hsB, hsD = new_stat3()
            dma(SP, hsD, hs[0:n, :], src, writes=[hsB])
            bk, bkB = next_bank()
            fns = []
            for m in range(4):
                fns.append(lambda e, m=m, bk=bk: e.transpose(bk[:, m * 32:m * 32 + n], hs[0:n, m * 128:(m + 1) * 128], ident[0:n, 0:n]))
            grp(PE, fns, reads=[hsB], writes=[bkB])
            src_v = bk[:, 0:128].rearrange("p (a b) -> p a b", a=4)[:, :, 0:n]
            op(ACT, lambda e: e.activation(out=dst_ap, in_=src_v, func=AF.Copy), reads=[bkB], writes=[dstB])

        hist_flag = set()

        def mixer_half(S, p, l, tiles):
            for (t, T, info) in tiles:
                yield rmsnorm(t, T, C_GMIX, l)
            hs = [None] * len(tiles)
            for q in range(2):
                w, wB = get_unit(S, p, l, "in_ag", q)
                Wa = wview(w, KD, 256); Wg = wview(w, KD, 256, off=KD * 256)
                for (t, T, info) in tiles:
                    hc, hcB = info["hc"][:, l, :, :], info["hcB"][l]
                    for mm in range(2):
                        m = 2 * q + mm
                        bka, bkaB = next_bank()
                        mm_group(bka, bkaB, T, [(Wa[:, k, mm * 128:(mm + 1) * 128], t.h[:, k, 0:T]) for k in range(KD)],
                                 reads=[wB], per=t.hB)
                        bkg, bkgB = next_bank()
                        mm_group(bkg, bkgB, T, [(Wg[:, k, mm * 128:(mm + 1) * 128], t.h[:, k, 0:T]) for k in range(KD)],
                                 reads=[wB], per=t.hB)
                        sg, sgB = new_sig()
                        op(ACT, lambda e, sg=sg, bkg=bkg, T=T: e.activation(out=sg[:, 0:T], in_=bkg[:, 0:T], func=AF.Sigmoid),
                           reads=[bkgB], writes=[sgB])
                        fns = [lambda e, t=t, m=m, bka=bka, sg=sg, T=T: e.tensor_tensor(
                            out=t.cbf[:, m, HC:HC + T], in0=bka[:, 0:T], in1=sg[:, 0:T], op=ALU.mult)]
                        wr = [t.cB[m]]
                        if info["last"]:
                            fns.append(lambda e, m=m, bka=bka, sg=sg, T=T: e.tensor_tensor(
                                out=tail32[:, m, :], in0=bka[:, T - HC:T], in1=sg[:, T - HC:T], op=ALU.mult))
                            wr = wr + [tail32B]
                        grp(DVE, fns, reads=[bkaB, sgB], writes=wr)
                        yield mmcost(16, T)
                    if q == 1:
                        op(ACT, lambda e, t=t, hc=hc: e.activation(out=t.cbf[:, :, 0:HC], in_=hc, func=AF.Copy),
                           reads=[hcB], writes=t.cB)
                        if info["last"]:
                            out_hist(tail32, [tail32B], HC, info["nc_out"][l])
                        else:
                            op(ACT, lambda e, t=t, T=T, hc=hc: e.activation(out=hc, in_=t.cbf[:, :, T:T + HC], func=AF.Copy),
                               reads=t.cB, writes=[hcB])
            w, wB = get_unit(S, p, l, "in_u", 0)
            Wu = wview(w, KD, DC); Wp = wview(w, 4, 128, off=KD * DC)
            for (t, T, info) in tiles:
                hu, huB = info["hu"][:, l, :, :], info["huB"][l]
                op(ACT, lambda e, t=t, hu=hu: e.activation(out=t.u[:, :, 0:HP], in_=hu, func=AF.Copy),
                   reads=[huB], writes=t.uB)
                for m in range(4):
                    bk, bkB = next_bank()
                    mm_group(bk, bkB, T, [(Wu[:, k, m * 128:(m + 1) * 128], t.h[:, k, 0:T]) for k in range(KD)],
                             reads=[wB], per=t.hB)
                    op(ACT, lambda e, t=t, m=m, bk=bk, T=T: e.activation(out=t.u[:, m, HP:HP + T], in_=bk[:, 0:T], func=AF.Copy),
                       reads=[bkB], writes=[t.uB[m]])
                    yield mmcost(8, T)
                op(ACT, lambda e, t=t, T=T, hu=hu: e.activation(out=hu, in_=t.u[:, :, T:T + HP], func=AF.Copy),
                   reads=t.uB, writes=[huB])
                if info["last"]:
                    out_hist(hu, [huB], HP, info["np_out"][l])
            WpB = wB
            infos = {id(t): info for (t, T, info) in tiles}

            def pool_chunk(t, T, g):
                info = infos[id(t)]
                L = HP + T
                ug = t.u[:, g, :]
                steps = [(psA, psAB, 1)]
                if g >= 1:
                    steps.append((psB_, psBB, 2))
                if g >= 2:
                    steps.append((psA, psAB, 4))
                if g >= 3:
                    steps.append((psB_, psBB, 8))
                src, srcB, lo = ug, t.uB[g], 0
                for (dst, dstB, sh) in steps:
                    nlo = lo + sh
                    op(DVE, lambda e, dst=dst, src=src, nlo=nlo, sh=sh: e.tensor_tensor(
                        out=dst[:, nlo:L], in0=src[:, nlo:L], in1=src[:, nlo - sh:L - sh], op=ALU.add),
                       reads=[srcB], writes=[dstB])
                    src, srcB, lo = dst, dstB, nlo
                w_ = POOL_W[g]
                op(DVE, lambda e: e.scalar_tensor_tensor(
                    out=t.h[:, g, 0:T], in0=src[:, HP:HP + T], scalar=1.0 / w_, in1=ug[:, HP:HP + T],
                    op0=ALU.mult, op1=ALU.subtract), reads=[srcB, t.uB[g]], writes=[t.hB[g]])
                if info["first"]:
                    nfix = w_ - 1
                    fx, fxB = new_stat()
                    op(DVE, lambda e: e.tensor_tensor(
                        out=fx[:, 0:nfix], in0=src[:, HP:HP + nfix], in1=invcnt[:, g, 0:nfix], op=ALU.mult),
                       reads=[srcB], writes=[fxB])
                    op(DVE, lambda e: e.tensor_tensor(
                        out=t.h[:, g, 0:nfix], in0=fx[:, 0:nfix], in1=ug[:, HP:HP + nfix], op=ALU.subtract),
                       reads=[fxB, t.uB[g]], writes=[t.hB[g]])
                bk, bkB = next_bank()
                mm_group(bk, bkB, T, [(Wp[:, g, :], t.h[:, g, 0:T])], reads=[WpB, t.hB[g]])
                op(ACT, lambda e: e.activation(out=t.h[:, 4 + g, 0:T], in_=bk[:, 0:T], func=AF.Copy,
                                               scale=tcol(l, C_PS, g)),
                   reads=[bkB], writes=[t.hB[4 + g]])

            def conv_mm(t, T, m):
                bk, bkB = next_bank()
                mm_group(bk, bkB, T, [(diagL[:, m, j, :], t.cbf[:, m, j:j + T]) for j in range(CW)],
                         reads=[diagLB[m], t.cB[m]])
                op(ACT, lambda e: e.activation(out=t.cv[:, m, 0:T], in_=bk[:, 0:T], func=AF.Identity, bias=tcol(l, C_CB, m)),
                   reads=[bkB], writes=[t.hidB[2 * m], t.hidB[2 * m + 1]])

            def ln_pre(t, T):
                op(ACT, lambda e: e.activation(out=lnsc[:, :, 0:T], in_=t.cv[:, :, 0:T], func=AF.Copy),
                   reads=t.hidB, writes=[lnscB])
                op(ACT, lambda e: e.activation(out=t.h[:, 0:4, 0:T], in_=t.cv[:, :, 0:T], func=AF.Square),
                   reads=t.hidB, writes=t.hB[0:4])

            def ln_post(t, T):
                bkm, bkmB = next_bank()
                mm_group(bkm, bkmB, T, [(ones_c[:], lnsc[:, m, 0:T]) for m in range(4)], reads=[lnscB])
                bkq, bkqB = next_bank()
                mm_group(bkq, bkqB, T, [(ones_c[:], t.h[:, m, 0:T]) for m in range(4)], reads=t.hB[0:4])
                mean, meanB = new_stat()
                var, varB = new_stat()
                op(ACT, lambda e: e.activation(out=mean[:, 0:T], in_=bkm[:, 0:T], func=AF.Copy), reads=[bkmB], writes=[meanB])
                op(DVE, lambda e: e.tensor_tensor(out=var[:, 0:T], in0=mean[:, 0:T], in1=mean[:, 0:T], op=ALU.mult),
                   reads=[meanB], writes=[varB])
                op(DVE, lambda e: e.tensor_tensor(out=var[:, 0:T], in0=bkq[:, 0:T], in1=var[:, 0:T], op=ALU.subtract),
                   reads=[bkqB, varB], writes=[varB])
                op(DVE, lambda e: e.tensor_scalar(out=var[:, 0:T], in0=var[:, 0:T], scalar1=0.0, scalar2=EPS, op0=ALU.max, op1=ALU.add),
                   reads=[varB], writes=[varB])
                r, rB = rstd_from(var[:, 0:T], [varB], T, False)
                op(DVE, lambda e: e.scalar_tensor_tensor(out=mean[:, 0:T], in0=mean[:, 0:T], scalar=-1.0, in1=r[:, 0:T],
                                                        op0=ALU.mult, op1=ALU.mult), reads=[meanB, rB], writes=[meanB])
                for m in range(4):
                    hb = [t.hidB[2 * m], t.hidB[2 * m + 1]]
                    op(DVE, lambda e, m=m: e.tensor_tensor(out=t.cv[:, m, 0:T], in0=t.cv[:, m, 0:T], in1=r[:, 0:T], op=ALU.mult),
                       reads=hb + [rB], writes=hb)
                    op(DVE, lambda e, m=m: e.tensor_tensor(out=t.cv[:, m, 0:T], in0=t.cv[:, m, 0:T], in1=mean[:, 0:T], op=ALU.add),
                       reads=hb + [meanB], writes=hb)
                    op(ACT, lambda e, m=m: e.activation(out=t.h[:, m, 0:T], in_=t.cv[:, m, 0:T], func=AF.Silu,
                                                        bias=tcol(l, C_LB, m), scale=tcol(l, C_LG, m)), reads=hb, writes=[t.hB[m]])

            work = [(t, T, m) for (t, T, info) in tiles for m in range(4)]
            prev = None
            for i, (t, T, m) in enumerate(work):
                conv_mm(t, T, m)
                pool_chunk(t, T, m)
                if m == 0 and prev is not None:
                    ln_post(*prev)
                if m == 3:
                    ln_pre(t, T)
                    prev = (t, T)
                yield 8.0
            ln_post(*prev)
            yield 12.0
            for q in range(2):
                w, wB = get_unit(S, p, l, "out", q)
                W = wview(w, KD, 512)
                for (t, T, info) in tiles:
                    for mm in range(4):
                        m = 4 * q + mm
                        bk, bkB = next_bank()
                        mm_group(bk, bkB, T, [(W[:, k, mm * 128:(mm + 1) * 128], t.h[:, k, 0:T]) for k in range(KD)],
                                 reads=[wB], per=t.hB)
                        op(DVE, lambda e, t=t, m=m, bk=bk, T=T: e.tensor_tensor(out=t.x[:, m, 0:T], in0=bk[:, 0:T], in1=t.x[:, m, 0:T], op=ALU.add),
                           reads=[bkB, t.xB[m]], writes=[t.xB[m]])
                        yield mmcost(8, T)

        def ffn_half(S, p, l, tiles):
            for (t, T, info) in tiles:
                yield rmsnorm(t, T, C_GFFN, l)
            for qq in range(NQ):
                if not (p == n_pass - 1 and l == depth - 1):
                    build_diag_chunk((l + 1) % depth, qq)
                for hf in range(2):
                    w, wB = get_unit(S, p, l, "ff1", 2 * qq + hf)
                    W1 = wview(w, KD, 512)
                    for (t, T, info) in tiles:
                        for mm in range(4):
                            m = 4 * hf + mm
                            bk, bkB = next_bank()
                            mm_group(bk, bkB, T, [(W1[:, k, mm * 128:(mm + 1) * 128], t.h[:, k, 0:T]) for k in range(KD)],
                                     reads=[wB], per=t.hB)
                            rl, rlB = new_relu()
                            op(ACT, lambda e, rl=rl, bk=bk, T=T: e.activation(out=rl[:, 0:T], in_=bk[:, 0:T], func=AF.Relu),
                               reads=[bkB], writes=[rlB])
                            op(ACT, lambda e, t=t, m=m, rl=rl, T=T: e.activation(out=t.hid[:, m, 0:T], in_=rl[:, 0:T], func=AF.Square),
                               reads=[rlB], writes=[t.hidB[m]])
                            yield mmcost(8, T)
                for hf in range(2):
                    w, wB = get_unit(S, p, l, "ff2", 2 * qq + hf)
                    W2 = wview(w, KD, 512)
                    for (t, T, info) in tiles:
                        for mm in range(4):
                            m = 4 * hf + mm
                            bk, bkB = next_bank()
                            mm_group(bk, bkB, T, [(W2[:, k, mm * 128:(mm + 1) * 128], t.hid[:, k, 0:T]) for k in range(KD)],
                                     reads=[wB], per=t.hidB)
                            op(DVE, lambda e, t=t, m=m, bk=bk, T=T: e.tensor_tensor(out=t.x[:, m, 0:T], in0=bk[:, 0:T], in1=t.x[:, m, 0:T], op=ALU.add),
                               reads=[bkB, t.xB[m]], writes=[t.xB[m]])
                            yield mmcost(8, T)
            for (t, T, info) in tiles:
                c = rmsnorm(t, T, C_GPLE, l)
                load_p(t, T, info["psrc"][l], info["tok0"], t.cbf, t.cB[0:2])
                yield (c + 2.0) if T == TT else ("s", 7.0)
            for q in range(2):
                w, wB = get_unit(S, p, l, "gp", q)
                Wg = wview(w, KD, 512); Wpl = wview(w, 2, 512, off=KD * 512)
                for (t, T, info) in tiles:
                    for mm in range(4):
                        m = 4 * q + mm
                        bkg, bkgB = next_bank()
                        mm_group(bkg, bkgB, T, [(Wg[:, k, mm * 128:(mm + 1) * 128], t.h[:, k, 0:T]) for k in range(KD)],
                                 reads=[wB], per=t.hB)
                        bkp, bkpB = next_bank()
                        mm_group(bkp, bkpB, T, [(Wpl[:, kc, mm * 128:(mm + 1) * 128], t.cbf[:, kc, 0:T]) for kc in range(2)],
                                 reads=[wB] + t.cB[0:2])
                        sg, sgB = new_sig()
                        op(ACT, lambda e, sg=sg, bkg=bkg, T=T: e.activation(out=sg[:, 0:T], in_=bkg[:, 0:T], func=AF.Sigmoid),
                           reads=[bkgB], writes=[sgB])
                        gt, gtB = new_gtmp()
                        op(DVE, lambda e, gt=gt, sg=sg, bkp=bkp, T=T: e.tensor_tensor(out=gt[:, 0:T], in0=bkp[:, 0:T], in1=sg[:, 0:T], op=ALU.mult),
                           reads=[bkpB, sgB], writes=[gtB])
                        op(DVE, lambda e, t=t, m=m, gt=gt, T=T: e.tensor_tensor(out=t.x[:, m, 0:T], in0=gt[:, 0:T], in1=t.x[:, m, 0:T], op=ALU.add),
                           reads=[gtB, t.xB[m]], writes=[t.xB[m]])
                        yield mmcost(10, T)

        def stream_gen(S):
            for p in range(n_pass):
                tiles = []
                for ti, tile in enumerate((tA, tB)):
                    tok0 = p * 2 * TT + ti * TT
                    info = {"first": tok0 == 0, "last": tok0 + TT == SEQ, "nc_out": ncp, "np_out": npp,
                            "tok0": tok0, "xsrc": xp, "psrc": pp, "ydst": yp,
                            "hc": hist_c[:, 0], "hcB": hist_cB[0], "hu": hist_u[:, 0], "huB": hist_uB[0]}
                    tiles.append((tile, TT, info))
                if p == 0:
                    for l in range(depth):
                        op(DVE, lambda e, l=l: e.memset(hist_c[:, 0, l, :, :], 0.0), writes=[hist_cB[0][l]])
                        op(DVE, lambda e, l=l: e.memset(hist_u[:, 0, l, :, :], 0.0), writes=[hist_uB[0][l]])
                    if tS is not None:
                        infoS = {"first": False, "last": True, "nc_out": ncs, "np_out": nps,
                                 "tok0": 0, "xsrc": xs, "psrc": psm, "ydst": ys,
                                 "hc": hist_c[:, 1], "hcB": hist_cB[1], "hu": hist_u[:, 1], "huB": hist_uB[1]}
                        tiles.append((tS, DEC, infoS))
                        for l in range(depth):
                            load_hist(cc[l], HC, hist_c[:, 1, l, :, :], hist_cB[1][l])
                            load_hist(cpl[l], HP, hist_u[:, 1, l, :, :], hist_uB[1][l])
                for (t, T, inf) in tiles:
                    yield from load_x(t, T, inf["xsrc"], inf["tok0"])
                for l in range(depth):
                    yield from mixer_half(S, p, l, tiles)
                    yield from ffn_half(S, p, l, tiles)
                for (t, T, inf) in tiles:
                    rmsnorm(t, T, None, None, out_f32=True)
                for (t, T, inf) in tiles:
                    yield from store_y(t, T, inf["ydst"], inf["tok0"])

        for m_ in range(4):
            build_diag_chunk(0, m_)
        for _ in stream_gen(streams[0]):
            pass

        fin = [(d_, d_.n) for d_ in new_stat3.sems]
        for t_ in (tA, tB, tS):
            if t_ is not None:
                fin += [(d_, d_.n) for d_ in t_.xstD + t_.pstD]
        SP.wait(fin)
    return nc


_PROGRAM = [None]


def kernel(x_prompt, x_sample, p_prompt, p_sample, cache_conv, cache_pool, w_in, conv_w, conv_b,
           ln_g, ln_b, pool_w, pool_scale, w_out, g_mix, g_ffn, g_ple, w_ff1, w_ff2, w_ple, w_gate, g_final):
    f = lambda a: np.ascontiguousarray(np.asarray(a, dtype=np.float32))
    shared = {"w_in": f(w_in), "conv_w": f(conv_w), "conv_b": f(conv_b), "ln_g": f(ln_g), "ln_b": f(ln_b),
              "pool_w": f(pool_w), "pool_scale": f(pool_scale), "w_out": f(w_out), "g_mix": f(g_mix),
              "g_ffn": f(g_ffn), "g_ple": f(g_ple), "w_ff1": f(w_ff1), "w_ff2": f(w_ff2), "w_ple": f(w_ple),
              "w_gate": f(w_gate), "g_final": f(g_final)}
    in_maps = []
    for i in range(NCORES):
        m = dict(shared)
        m["xp"] = f(x_prompt[i]); m["xs"] = f(x_sample[i])
        m["pp"] = f(p_prompt[:, i]); m["psm"] = f(p_sample[:, i])
        m["cc"] = f(cache_conv[:, i]); m["cpl"] = f(cache_pool[:, i])
        in_maps.append(m)
    nc = build_program()
    res = run_bass_kernel_spmd(nc, in_maps, core_ids=list(range(NCORES)))
    r = res.results
    y_prompt = np.stack([r[i]["yp"] for i in range(NCORES)], 0)
    y_sample = np.stack([r[i]["ys"] for i in range(NCORES)], 0)
    ncp = np.stack([r[i]["ncp"] for i in range(NCORES)], 1)
    npp = np.stack([r[i]["npp"] for i in range(NCORES)], 1)
    ncs = np.stack([r[i]["ncs"] for i in range(NCORES)], 1)
    nps = np.stack([r[i]["nps"] for i in range(NCORES)], 1)
    return (y_prompt.astype(np.float32), y_sample.astype(np.float32), ncp.astype(np.float32),
            npp.astype(np.float32), ncs.astype(np.float32), nps.astype(np.float32))
```
